# Optimizing a Trainium2 kernel written in Bass

```python
import math
import jax
import jax.numpy as jnp
from jax import lax
import numpy as np

D_MODEL = 1024
BATCH = 8
SEQ = 2048
DEPTH = 2
DEC_BATCH = 16
DEC_SEQ = 64
PAST_LEN = 2048

CHUNK = 64
Q_BLOCK = 128
EPS = 1e-6
N_BRANCH = 3
D_CONV = 512
CONV_W = 3
FOX_HEADS = 8
FOX_HD = 64
FOX_W = FOX_HEADS * FOX_HD
MLA_HEADS = 8
QK_NOPE = 64
QK_ROPE = 32
V_HD = 64
Q_LORA = 384
KV_LORA = 256
MLA_W = MLA_HEADS * V_HD
ROPE_THETA = 10000.0
N_MEM = 256
CA_HEADS = 4
CA_HD = 128
CA_W = CA_HEADS * CA_HD
D_FF = 2816
FFN_CONV_W = 3

IN_SPLITS = (D_CONV, D_CONV, D_CONV, FOX_W, FOX_W, FOX_W, FOX_HEADS, Q_LORA, KV_LORA, QK_ROPE, D_MODEL, D_MODEL, D_MODEL)
D_IN = 3 * D_CONV + 3 * FOX_W + FOX_HEADS + Q_LORA + KV_LORA + QK_ROPE + N_BRANCH * D_MODEL

kernel_name = 'hybrid_streaming_encoder_step'


def rmsnorm(x, g):
    xf = x.astype(jnp.float32)
    y = xf * lax.rsqrt(jnp.mean(xf * xf, axis=-1, keepdims=True) + EPS)
    return (y * g.astype(jnp.float32)).astype(x.dtype)


def split_in(z):
    offs = np.cumsum(np.array(IN_SPLITS))[:-1].tolist()
    return jnp.split(z, offs, axis=-1)


def causal_dwconv(u, w, prev):
    t = u.shape[1]
    width = w.shape[0]
    full = jnp.concatenate([prev.astype(u.dtype), u], axis=1)
    y = full[:, 0:t] * w[0]
    for j in range(1, width):
        y = y + full[:, j:j + t] * w[j]
    return y, full[:, -(width - 1):]


def rope(x, pos):
    half = x.shape[-1] // 2
    inv = ROPE_THETA ** (-jnp.arange(half, dtype=jnp.float32) / half)
    ang = pos.astype(jnp.float32)[:, None] * inv[None, :]
    shape = (1, x.shape[1]) + (1,) * (x.ndim - 3) + (half,)
    cos = jnp.cos(ang).reshape(shape)
    sin = jnp.sin(ang).reshape(shape)
    xf = x.astype(jnp.float32)
    x1, x2 = xf[..., :half], xf[..., half:]
    return jnp.concatenate([x1 * cos - x2 * sin, x2 * cos + x1 * sin], axis=-1).astype(x.dtype)


def sweep_queries(fn, q_args, qpos):
    t = qpos.shape[0]
    if t <= Q_BLOCK:
        return fn(q_args, qpos)
    nb = t // Q_BLOCK

    def blocks(a):
        return jnp.moveaxis(a.reshape((a.shape[0], nb, Q_BLOCK) + a.shape[2:]), 1, 0)

    out = lax.map(lambda args: fn(args[0], args[1]),
                  (tuple(blocks(a) for a in q_args), qpos.reshape(nb, Q_BLOCK)))
    out = jnp.moveaxis(out, 0, 1)
    return out.reshape((out.shape[0], t) + out.shape[3:])


def fox_attention(q, k, v, logf, k_c, v_c, logf_c, qpos):
    k_all = jnp.concatenate([k_c.astype(k.dtype), k], axis=1)
    v_all = jnp.concatenate([v_c.astype(v.dtype), v], axis=1)
    logf_all = jnp.concatenate([logf_c.astype(jnp.float32), logf], axis=1)
    cum = jnp.cumsum(logf_all, axis=1)
    tq = q.shape[1]
    cum_q = cum[:, -tq:]
    cum_k = jnp.swapaxes(cum, 1, 2)[:, :, None, :]
    kpos = jnp.arange(k_all.shape[1])
    scale = FOX_HD ** -0.5

    def fn(qa, qp):
        qb, cq = qa
        s = jnp.einsum('bqhd,bkhd->bhqk', qb, k_all, preferred_element_type=jnp.float32) * scale
        s = s + jnp.swapaxes(cq, 1, 2)[:, :, :, None] - cum_k
        s = jnp.where(kpos[None, :] <= qp[:, None], s, -jnp.inf)
        p = jax.nn.softmax(s, axis=-1)
        return jnp.einsum('bhqk,bkhd->bqhd', p.astype(v_all.dtype), v_all)

    return sweep_queries(fn, (q, cum_q), qpos)


def mla_attention(q_lat, q_rope, ckv_all, kr_all, qpos):
    kchunk = jnp.arange(ckv_all.shape[1]) // CHUNK
    scale = (QK_NOPE + QK_ROPE) ** -0.5

    def fn(qa, qp):
        ql, qr = qa
        s = (jnp.einsum('bqhc,bkc->bhqk', ql, ckv_all, preferred_element_type=jnp.float32)
             + jnp.einsum('bqhr,bkr->bhqk', qr, kr_all, preferred_element_type=jnp.float32)) * scale
        s = jnp.where(kchunk[None, :] <= (qp // CHUNK)[:, None], s, -jnp.inf)
        p = jax.nn.softmax(s, axis=-1)
        return jnp.einsum('bhqk,bkc->bqhc', p.astype(ckv_all.dtype), ckv_all)

    return sweep_queries(fn, (q_lat, q_rope), qpos)


def token_mixers(h, pos, lp, hist):
    b, t, _ = h.shape
    (c_b, c_c, c_x, f_q, f_k, f_v, f_f, c_q, c_kv, k_r, g_a, g_b, g_c) = split_in(h @ lp['w_in'])
    conv_out, conv_state = causal_dwconv(c_c * c_x, lp['conv_w'], hist['conv'])
    y_a = (c_b * conv_out) @ lp['w_conv_out']
    q = f_q.reshape(b, t, FOX_HEADS, FOX_HD)
    k = f_k.reshape(b, t, FOX_HEADS, FOX_HD)
    v = f_v.reshape(b, t, FOX_HEADS, FOX_HD)
    logf = jax.nn.log_sigmoid(f_f.astype(jnp.float32) + lp['b_forget'].astype(jnp.float32))
    o_b = fox_attention(q, k, v, logf, hist['fox_k'], hist['fox_v'], hist['fox_logf'], pos)
    y_b = o_b.reshape(b, t, FOX_W) @ lp['w_fox_out']
    c_q = rmsnorm(c_q, lp['g_q_lora'])
    q_m = jnp.einsum('btr,rhe->bthe', c_q, lp['w_uq'])
    q_rope = rope(q_m[..., QK_NOPE:], pos)
    q_lat = jnp.einsum('bthe,che->bthc', q_m[..., :QK_NOPE], lp['w_uk'])
    c_kv = rmsnorm(c_kv, lp['g_kv_lora'])
    k_r = rope(k_r, pos)
    ckv_all = jnp.concatenate([hist['mla_ckv'].astype(c_kv.dtype), c_kv], axis=1)
    kr_all = jnp.concatenate([hist['mla_kr'].astype(k_r.dtype), k_r], axis=1)
    o_lat = mla_attention(q_lat, q_rope, ckv_all, kr_all, pos)
    o_c = jnp.einsum('bthc,chv->bthv', o_lat, lp['w_uv']).reshape(b, t, MLA_W)
    y_c = o_c @ lp['w_mla_out']
    merged = jax.nn.sigmoid(g_a) * y_a + jax.nn.sigmoid(g_b) * y_b + jax.nn.sigmoid(g_c) * y_c
    new = {'conv': conv_state, 'fox_k': k, 'fox_v': v, 'fox_logf': logf, 'mla_ckv': c_kv, 'mla_kr': k_r}
    return merged @ lp['w_mix_out'], new


def memory_kv(mem, g, w_k, w_v):
    m = rmsnorm(mem, g)
    return jnp.einsum('bmd,dhe->bmhe', m, w_k), jnp.einsum('bmd,dhe->bmhe', m, w_v)


def cross_attention(h, mem_k, mem_v, w_q, w_o):
    q = jnp.einsum('btd,dhe->bthe', h, w_q)
    s = jnp.einsum('bthe,bmhe->bhtm', q, mem_k, preferred_element_type=jnp.float32) * CA_HD ** -0.5
    p = jax.nn.softmax(s, axis=-1)
    o = jnp.einsum('bhtm,bmhe->bthe', p.astype(mem_v.dtype), mem_v)
    return jnp.einsum('bthe,hed->btd', o, w_o)


def conv_ffn(h, w_up, conv_w, w_down, prev):
    u, state = causal_dwconv(h @ w_up, conv_w, prev)
    a, g = u[..., :D_FF], u[..., D_FF:]
    return (jax.nn.gelu(a, approximate=True) * g) @ w_down, state


def layer(x, pos, lp, hist, mem_k, mem_v):
    g = lp['g_norms']
    m, new = token_mixers(rmsnorm(x, g[0]), pos, lp, hist)
    x = x + rmsnorm(m, g[1])
    c = cross_attention(rmsnorm(x, g[2]), mem_k, mem_v, lp['w_ca_q'], lp['w_ca_o'])
    x = x + rmsnorm(c, g[3])
    f, ffn_state = conv_ffn(rmsnorm(x, g[4]), lp['w_up'], lp['ffn_conv_w'], lp['w_down'], hist['ffn'])
    x = x + rmsnorm(f, g[5])
    new['ffn'] = ffn_state
    return x, new


def empty_history(b, dtype):
    return {
        'conv': jnp.zeros((b, CONV_W - 1, D_CONV), dtype),
        'ffn': jnp.zeros((b, FFN_CONV_W - 1, 2 * D_FF), dtype),
        'fox_k': jnp.zeros((b, 0, FOX_HEADS, FOX_HD), dtype),
        'fox_v': jnp.zeros((b, 0, FOX_HEADS, FOX_HD), dtype),
        'fox_logf': jnp.zeros((b, 0, FOX_HEADS), jnp.float32),
        'mla_ckv': jnp.zeros((b, 0, KV_LORA), dtype),
        'mla_kr': jnp.zeros((b, 0, QK_ROPE), dtype),
    }


def setup_inputs(seed: int = 0) -> dict:
    key = jax.random.key(seed)
    ks = iter(jax.random.split(key, 40))

    def nrm(shape, scale=1.0):
        return jax.random.normal(next(ks), shape, jnp.float32) * scale

    L = DEPTH
    return {
        'x_prompt': nrm((BATCH, SEQ, D_MODEL)),
        'x_sample': nrm((DEC_BATCH, DEC_SEQ, D_MODEL)),
        'cache_fox_k': nrm((L, DEC_BATCH, PAST_LEN, FOX_HEADS, FOX_HD)),
        'cache_fox_v': nrm((L, DEC_BATCH, PAST_LEN, FOX_HEADS, FOX_HD)),
        'cache_fox_logf': jax.nn.log_sigmoid(2.0 + nrm((L, DEC_BATCH, PAST_LEN, FOX_HEADS), 0.5)),
        'cache_mla_ckv': nrm((L, DEC_BATCH, PAST_LEN, KV_LORA)),
        'cache_mla_kr': nrm((L, DEC_BATCH, PAST_LEN, QK_ROPE)),
        'state_conv': nrm((L, DEC_BATCH, CONV_W - 1, D_CONV)),
        'state_ffn_conv': nrm((L, DEC_BATCH, FFN_CONV_W - 1, 2 * D_FF)),
        'cache_mem_k': nrm((L, DEC_BATCH, N_MEM, CA_HEADS, CA_HD)),
        'cache_mem_v': nrm((L, DEC_BATCH, N_MEM, CA_HEADS, CA_HD)),
        'mem_prompt': nrm((BATCH, N_MEM, D_MODEL)),
        'w_in': nrm((L, D_MODEL, D_IN), D_MODEL ** -0.5),
        'b_forget': 2.0 + nrm((L, FOX_HEADS), 0.5),
        'conv_w': nrm((L, CONV_W, D_CONV), CONV_W ** -0.5),
        'g_q_lora': 1.0 + nrm((L, Q_LORA), 0.1),
        'g_kv_lora': 1.0 + nrm((L, KV_LORA), 0.1),
        'w_uq': nrm((L, Q_LORA, MLA_HEADS, QK_NOPE + QK_ROPE), Q_LORA ** -0.5),
        'w_uk': nrm((L, KV_LORA, MLA_HEADS, QK_NOPE), KV_LORA ** -0.5),
        'w_uv': nrm((L, KV_LORA, MLA_HEADS, V_HD), KV_LORA ** -0.5),
        'w_conv_out': nrm((L, D_CONV, D_MODEL), D_CONV ** -0.5),
        'w_fox_out': nrm((L, FOX_W, D_MODEL), FOX_W ** -0.5),
        'w_mla_out': nrm((L, MLA_W, D_MODEL), MLA_W ** -0.5),
        'w_mix_out': nrm((L, D_MODEL, D_MODEL), D_MODEL ** -0.5),
        'g_mem': 1.0 + nrm((L, D_MODEL), 0.1),
        'w_ca_q': nrm((L, D_MODEL, CA_HEADS, CA_HD), D_MODEL ** -0.5),
        'w_ca_k': nrm((L, D_MODEL, CA_HEADS, CA_HD), D_MODEL ** -0.5),
        'w_ca_v': nrm((L, D_MODEL, CA_HEADS, CA_HD), D_MODEL ** -0.5),
        'w_ca_o': nrm((L, CA_HEADS, CA_HD, D_MODEL), CA_W ** -0.5),
        'w_up': nrm((L, D_MODEL, 2 * D_FF), D_MODEL ** -0.5),
        'ffn_conv_w': nrm((L, FFN_CONV_W, 2 * D_FF), FFN_CONV_W ** -0.5),
        'w_down': nrm((L, D_FF, D_MODEL), D_FF ** -0.5),
        'g_norms': 1.0 + nrm((L, 6, D_MODEL), 0.1),
    }


def reference(x_prompt, x_sample, cache_fox_k, cache_fox_v, cache_fox_logf, cache_mla_ckv, cache_mla_kr,
              state_conv, state_ffn_conv, cache_mem_k, cache_mem_v, mem_prompt,
              w_in, b_forget, conv_w, g_q_lora, g_kv_lora, w_uq, w_uk, w_uv,
              w_conv_out, w_fox_out, w_mla_out, w_mix_out, g_mem, w_ca_q, w_ca_k, w_ca_v, w_ca_o,
              w_up, ffn_conv_w, w_down, g_norms):
    b_p, s_p, _ = x_prompt.shape
    t_s = x_sample.shape[1]
    past = cache_fox_k.shape[2]
    pos_p = jnp.arange(s_p)
    pos_s = past + jnp.arange(t_s)
    params = [dict(w_in=w_in[l], b_forget=b_forget[l], conv_w=conv_w[l], g_q_lora=g_q_lora[l],
                   g_kv_lora=g_kv_lora[l], w_uq=w_uq[l], w_uk=w_uk[l], w_uv=w_uv[l],
                   w_conv_out=w_conv_out[l], w_fox_out=w_fox_out[l], w_mla_out=w_mla_out[l],
                   w_mix_out=w_mix_out[l], w_ca_q=w_ca_q[l], w_ca_o=w_ca_o[l], w_up=w_up[l],
                   ffn_conv_w=ffn_conv_w[l], w_down=w_down[l], g_norms=g_norms[l])
              for l in range(DEPTH)]

    x = x_prompt
    p_new = []
    for l in range(DEPTH):
        mk, mv = memory_kv(mem_prompt, g_mem[l], w_ca_k[l], w_ca_v[l])
        x, new = layer(x, pos_p, params[l], empty_history(b_p, x.dtype), mk, mv)
        new['mem_k'] = mk
        new['mem_v'] = mv
        p_new.append(new)
    y_prompt = x

    x = x_sample
    s_new = []
    for l in range(DEPTH):
        hist = {'conv': state_conv[l], 'ffn': state_ffn_conv[l], 'fox_k': cache_fox_k[l],
                'fox_v': cache_fox_v[l], 'fox_logf': cache_fox_logf[l],
                'mla_ckv': cache_mla_ckv[l], 'mla_kr': cache_mla_kr[l]}
        x, new = layer(x, pos_s, params[l], hist, cache_mem_k[l], cache_mem_v[l])
        s_new.append(new)
    y_sample = x

    def st(lst, name):
        return jnp.stack([n[name] for n in lst], axis=0)

    return (y_prompt, y_sample,
            st(p_new, 'fox_k'), st(p_new, 'fox_v'), st(p_new, 'fox_logf'),
            st(p_new, 'mla_ckv'), st(p_new, 'mla_kr'), st(p_new, 'conv'), st(p_new, 'ffn'),
            st(p_new, 'mem_k'), st(p_new, 'mem_v'),
            st(s_new, 'fox_k'), st(s_new, 'fox_v'), st(s_new, 'fox_logf'),
            st(s_new, 'mla_ckv'), st(s_new, 'mla_kr'), st(s_new, 'conv'), st(s_new, 'ffn'))
```

```python
from contextlib import ExitStack
import numpy as np
import concourse.bass as bass
import concourse.mybir as mybir
from concourse.bass_utils import run_bass_kernel_spmd

F32 = mybir.dt.float32
BF16 = mybir.dt.bfloat16
AF = mybir.ActivationFunctionType
ALU = mybir.AluOpType

N_DMA_SEM = 5
D = 1024
NCORE = 8
SEQ = 2048
PAST = 2048
TS = 64
DIN = 6824
DFF = 2816
TP = 256
KTOT = PAST + TS
EPS = 1e-6
NEG = -30000.0
O_CB, O_CC, O_CX, O_FQ, O_FK, O_FV, O_FF, O_CQ, O_CKV, O_KR, O_GA, O_GB, O_GC = (
    0, 512, 1024, 1536, 2048, 2560, 3072, 3080, 3464, 3720, 3752, 4776, 5800)


class Op:
    __slots__ = ("eng", "fn", "waits", "need_inc", "sem", "val", "dma", "idx")

    def __init__(self, eng, fn, dma):
        self.eng = eng
        self.fn = fn
        self.waits = []
        self.need_inc = dma
        self.sem = None
        self.val = None
        self.dma = dma
        self.idx = 0


class Prog:
    ENGS = ("pe", "act", "dve", "pool", "sp")

    def __init__(self, nc, stack):
        self.nc = nc
        self.ops = {e: [] for e in self.ENGS}
        self.last_w = {}
        self.readers = {}
        self.sems = {e: stack.enter_context(nc.semaphore("s_" + e)) for e in self.ENGS}
        self.dsems = {e: [stack.enter_context(nc.semaphore("d_%s%d" % (e, i))) for i in range(N_DMA_SEM)]
                      for e in ("sp", "pool", "act")}
        self.n_dma = {e: 0 for e in ("sp", "pool", "act")}

    def op(self, eng, fn, reads=(), writes=(), dma=False):
        o = Op(eng, fn, dma)
        deps = []
        for k in reads:
            w = self.last_w.get(k)
            if w is not None:
                deps.append(w)
        for k in writes:
            w = self.last_w.get(k)
            if w is not None:
                deps.append(w)
            deps.extend(self.readers.get(k, ()))
        seen = set()
        for d in deps:
            if id(d) in seen:
                continue
            seen.add(id(d))
            if d.eng == eng and eng == "pe" and not d.dma:
                continue
            d.need_inc = True
            o.waits.append(d)
        for k in writes:
            self.last_w[k] = o
            self.readers[k] = []
        for k in reads:
            self.readers.setdefault(k, []).append(o)
        self.ops[eng].append(o)
        return o

    def dma(self, eng, out, in_, reads=(), writes=(), slow=False):
        if slow:
            return self.op(eng, lambda e: e.dma_start(out=out, in_=in_, allow_slow_non_contiguous=True),
                           reads, writes, dma=True)
        return self.op(eng, lambda e: e.dma_start(out=out, in_=in_), reads, writes, dma=True)

    def finalize(self):
        for e in self.ENGS:
            cnt = 0
            for o in self.ops[e]:
                if o.dma:
                    i = self.n_dma[e]
                    self.n_dma[e] += 1
                    o.sem = self.dsems[e][i % N_DMA_SEM]
                    o.val = 16 * (i // N_DMA_SEM + 1)
                    o.idx = i
                elif o.need_inc:
                    cnt += 1
                    o.sem = self.sems[e]
                    o.val = cnt

    def emit(self, ename, eng):
        waited = {}
        hist = []
        for o in self.ops[ename]:
            if o.dma:
                if o.idx >= N_DMA_SEM:
                    prev = hist[o.idx - N_DMA_SEM]
                    if waited.get(id(prev.sem), 0) < prev.val:
                        eng.wait_ge(prev.sem, prev.val)
                        waited[id(prev.sem)] = prev.val
                hist.append(o)
            for d in o.waits:
                if waited.get(id(d.sem), 0) < d.val:
                    eng.wait_ge(d.sem, d.val)
                    waited[id(d.sem)] = d.val
            ins = o.fn(eng)
            if o.dma:
                ins.then_inc(o.sem, 16)
            elif o.need_inc:
                ins.then_inc(o.sem, 1)

    def run(self, final_ops):
        self.finalize()
        nc = self.nc
        with nc.Block() as block:
            @block.tensor
            def _(e):
                self.emit("pe", e)

            @block.scalar
            def _(e):
                self.emit("act", e)

            @block.vector
            def _(e):
                self.emit("dve", e)

            @block.gpsimd
            def _(e):
                self.emit("pool", e)

            @block.sync
            def _(e):
                self.emit("sp", e)
                done = {}
                for o in final_ops:
                    if done.get(id(o.sem), (None, 0))[1] < o.val:
                        done[id(o.sem)] = (o.sem, o.val)
                for sem, val in done.values():
                    e.wait_ge(sem, val)


def build_program(stop_after=None):
    nc = bass.Bass("TRN2", target_bir_lowering=False)

    def din(name, shape):
        return nc.dram_tensor(name, list(shape), F32, kind="ExternalInput").ap()

    def dout(name, shape):
        return nc.dram_tensor(name, list(shape), F32, kind="ExternalOutput").ap()

    L = 2
    I = dict(
        x_prompt=din("x_prompt", (SEQ, D)), x_sample=din("x_sample", (2, TS, D)),
        cache_fox_k=din("cache_fox_k", (L, 2, PAST, 512)), cache_fox_v=din("cache_fox_v", (L, 2, PAST, 512)),
        cache_fox_logf=din("cache_fox_logf", (L, 2, PAST, 8)), cache_mla_ckv=din("cache_mla_ckv", (L, 2, PAST, 256)),
        cache_mla_kr=din("cache_mla_kr", (L, 2, PAST, 32)), state_conv=din("state_conv", (L, 2, 2, 512)),
        state_ffn_conv=din("state_ffn_conv", (L, 2, 2, 2 * DFF)), cache_mem_k=din("cache_mem_k", (L, 2, 256, 512)),
        cache_mem_v=din("cache_mem_v", (L, 2, 256, 512)), mem_prompt=din("mem_prompt", (256, D)),
        w_in=din("w_in", (L, D, DIN)), b_forget=din("b_forget", (L, 8)), conv_w=din("conv_w", (L, 3, 512)),
        g_q_lora=din("g_q_lora", (L, 384)), g_kv_lora=din("g_kv_lora", (L, 256)),
        w_uq=din("w_uq", (L, 384, 768)), w_uk=din("w_uk", (L, 256, 512)), w_uv=din("w_uv", (L, 256, 512)),
        w_conv_out=din("w_conv_out", (L, 512, D)), w_fox_out=din("w_fox_out", (L, 512, D)),
        w_mla_out=din("w_mla_out", (L, 512, D)), w_mix_out=din("w_mix_out", (L, D, D)), g_mem=din("g_mem", (L, D)),
        w_ca_q=din("w_ca_q", (L, D, 512)), w_ca_k=din("w_ca_k", (L, D, 512)), w_ca_v=din("w_ca_v", (L, D, 512)),
        w_ca_o=din("w_ca_o", (L, 512, D)), w_up=din("w_up", (L, D, 2 * DFF)), ffn_conv_w=din("ffn_conv_w", (L, 3, 2 * DFF)),
        w_down=din("w_down", (L, DFF, D)), g_norms=din("g_norms", (L, 6, D)),
        c_ident=din("c_ident", (128, 128)), c_mtri=din("c_mtri", (128, 128)), c_mchk=din("c_mchk", (128, 128)),
        c_ropeT=din("c_ropeT", (2, 32, KTOT)), c_ropeK=din("c_ropeK", (2, KTOT, 16)),
    )
    O = dict(
        y_p=dout("y_p", (SEQ, D)), y_s=dout("y_s", (2, TS, D)),
        fox_k_p=dout("fox_k_p", (L, SEQ, 512)), fox_v_p=dout("fox_v_p", (L, SEQ, 512)), logf_p=dout("logf_p", (L, SEQ, 8)),
        ckv_p=dout("ckv_p", (L, SEQ, 256)), kr_p=dout("kr_p", (L, SEQ, 32)), conv_p=dout("conv_p", (L, 2, 512)),
        ffn_p=dout("ffn_p", (L, 2, 2 * DFF)), mem_k_p=dout("mem_k_p", (L, 256, 512)), mem_v_p=dout("mem_v_p", (L, 256, 512)),
        fox_k_s=dout("fox_k_s", (L, 2, TS, 512)), fox_v_s=dout("fox_v_s", (L, 2, TS, 512)), logf_s=dout("logf_s", (L, 2, TS, 8)),
        ckv_s=dout("ckv_s", (L, 2, TS, 256)), kr_s=dout("kr_s", (L, 2, TS, 32)), conv_s=dout("conv_s", (L, 2, 2, 512)),
        ffn_s=dout("ffn_s", (L, 2, 2, 2 * DFF)),
    )
    xres = nc.dram_tensor("xres", [128, 8, SEQ + 2 * TS], F32, kind="Internal").ap()

    with ExitStack() as st:
        P = Prog(nc, st)
        fin = []

        def sb(name, shape, dt=F32):
            return st.enter_context(nc.sbuf_tensor(name, list(shape), dt))

        fK = sb("fK", (128, 8, KTOT), BF16)
        mK = sb("mK", (128, 8, KTOT), BF16)
        fV = sb("fV", (128, 17, 512), BF16)
        mV = sb("mV", (128, 17, 512), BF16)
        memKT = sb("memKT", (128, 4, 256), BF16)
        memV = sb("memV", (128, 2, 512), BF16)
        xT = sb("xT", (128, 8, TP))
        hT = sb("hT", (128, 8, TP), BF16)
        tb = sb("tb", (128, 8, TP), BF16)
        mT = sb("mT", (128, 8, TP))
        fQ = sb("fQ", (128, 8, TP), BF16)
        oT = sb("oT", (128, 4, TP), BF16)
        aT = sb("aT", (128, 4, TP), BF16)
        actT = sb("actT", (128, 11, TP), BF16)
        uT = sb("uT", (128, 4, TP + 2))
        uF = [sb("uF%d" % i, (128, TP + 2)) for i in range(2)]
        fhalo = sb("fhalo", (128, 44, 2))
        rstd = sb("rstd", (128, TP))
        tmpA = [sb("tmpA%d" % i, (128, TP)) for i in range(3)]
        pT = [sb("pT%d" % i, (128, TP), BF16) for i in range(3)]
        pT2 = [sb("pT2_%d" % i, (128, 512), BF16) for i in range(3)]
        stg = [sb("stg%d" % i, (128, 512)) for i in range(2)]
        cqT = sb("cqT", (128, 3, TP))
        cqn = sb("cqn", (128, 3, TP), BF16)
        ckT = sb("ckT", (128, 2, TP))
        ckn = sb("ckn", (128, 2, TP), BF16)
        krT = sb("krT", (128, TP), BF16)
        ropeT = sb("ropeT", (128, 2, TP))
        ropeK = sb("ropeK", (128, 2, 2, 16))
        lf = sb("lf", (8, TP))
        cum = sb("cum", (8, TP))
        carry = sb("carry", (8, 1))
        c8 = sb("c8", (8, TP))
        chi = sb("chi", (8, TP), BF16)
        clo = sb("clo", (8, TP), BF16)
        ident = sb("ident", (128, 128))
        identb = sb("identb", (128, 128), BF16)
        mtri = sb("mtri", (128, 128), BF16)
        mchk = sb("mchk", (128, 128), BF16)
        ones = sb("ones", (128, 128), BF16)
        gT = sb("gT", (128, 2, 6, 8))
        gqT = sb("gqT", (128, 2, 3))
        gkvT = sb("gkvT", (128, 2, 2))
        cwT = sb("cwT", (128, 2, 3, 4))
        fcwT = sb("fcwT", (128, 2, 3, 44))
        nbf = sb("nbf", (8, 2))
        gkvB = sb("gkvB", (128, 2, 256))
        gmemT = sb("gmemT", (128, 2, 8))
        bfB = sb("bfB", (128, 2, 8))
        small = sb("small", (128, 16))
        wslot = [sb("wslot%d" % i, (128, 2048), BF16) for i in range(4)]
        memx = sb("memx", (128, 1, D))
        NPS = 8
        psum = [st.enter_context(nc.psum_tensor("ps%d" % i, [128, 512], F32)) for i in range(NPS)]

        ctr = dict(ps=0, psf=0, w=0, tA=0, tB=0, pT=0, pT2=0, stg=0, uF=0)

        def ps_next(full=False):
            i = ctr["ps"] % 4
            ctr["ps"] += 1
            return psum[i][:, :], ("ps", i)

        def ps_acc(h):
            j = h % 2
            return (psum[4 + j][:, :], ("pacc", 4 + j)), (psum[6 + j][:, :], ("pacc", 6 + j))

        def rr(name, n):
            i = ctr[name] % n
            ctr[name] += 1
            return i

        wscratch = {}
        COLLECT = [None]

        def wcached(bkey, kc, cw, p, build_fn):
            if COLLECT[0] is not None:
                COLLECT[0].append((bkey, kc, cw, p, build_fn))
                return wslot[0][0:p, 0:kc * cw].rearrange("p (k c) -> p k c", k=kc), ("w", 0)
            i = rr("w", 4)
            flat = wslot[i][0:p, 0:kc * cw]
            view = flat.rearrange("p (k c) -> p k c", k=kc)
            key = ("w", i)
            if bkey in wscratch:
                P.dma("sp", flat, wscratch[bkey], reads=[("wsc", bkey)], writes=[key])
            else:
                build_fn(view, key)
                wscratch[bkey] = nc.dram_tensor("wsc%d" % len(wscratch), [p, kc * cw], BF16, kind="Internal").ap()
                P.dma("sp", wscratch[bkey], flat, reads=[key], writes=[("wsc", bkey)])
            return view, key

        def wload(src, kc, cw, p=128):
            bkey = (src.tensor.name, src.offset, str(src.ap), p, kc, cw)
            return wcached(bkey, kc, cw, p, lambda view, key: P.dma("pool", view, src, writes=[key]))

        def wrows(w2d, c0, cw, p=128):
            return w2d[:, c0:c0 + cw].rearrange("(k p) c -> p k c", p=p)

        def mm(out, pairs, reads, writes, first=True, last=True):
            def fn(e):
                ins = None
                n = len(pairs)
                for j, (a, b) in enumerate(pairs):
                    ins = e.matmul(out, a, b, start=(first and j == 0), stop=(last and j == n - 1))
                return ins
            return P.op("pe", fn, reads, writes)

        def act(out, in_, func, reads, writes, **kw):
            return P.op("act", lambda e: e.activation(out, in_, func, **kw), reads, writes)

        def acopy(out, in_, reads, writes):
            return P.op("act", lambda e: e.copy(out, in_), reads, writes)

        def vcopy(out, in_, reads, writes):
            return P.op("dve", lambda e: e.tensor_copy(out, in_), reads, writes)

        def tt(out, a, b, op, reads, writes, eng="dve"):
            return P.op(eng, lambda e: e.tensor_tensor(out, a, b, op), reads, writes)

        def ts(out, a, s1, s2, op0, op1, reads, writes, eng="dve"):
            if s2 is None:
                return P.op(eng, lambda e: e.tensor_scalar(out, a, s1, None, op0), reads, writes)
            return P.op(eng, lambda e: e.tensor_scalar(out, a, s1, s2, op0, op1), reads, writes)

        def stt(out, a, s, b, op0, op1, reads, writes, eng="dve"):
            return P.op(eng, lambda e: e.scalar_tensor_tensor(out, a, s, b, op0, op1), reads, writes)

        P.dma("sp", ident[:], I["c_ident"], writes=["ident"])
        vcopy(identb[:], ident[:], ["ident"], ["identb"])
        P.dma("pool", mtri[:], I["c_mtri"], writes=["mtri"])
        P.dma("pool", mchk[:], I["c_mchk"], writes=["mchk"])
        P.op("dve", lambda e: e.memset(ones[:], 1.0), writes=["ones"])
        P.op("dve", lambda e: e.memset(fK[64:128, :, :], 0.0), writes=["fKa"])
        P.op("dve", lambda e: e.memset(fK[64:68, :, :], 1.0), writes=["fKa"])
        P.op("dve", lambda e: e.memset(mK[64:128, :, :], 0.0), writes=["mKr"])
        P.op("dve", lambda e: e.memset(fQ[64:128, :, :], 0.0), writes=["fQa"])
        P.op("dve", lambda e: e.memset(fQ[64:68, :, :], 1.0), writes=["fQa"])
        for l in range(2):
            P.dma("sp", gT[:, l, :, :], I["g_norms"][l].rearrange("n (c p) -> p n c", p=128), writes=["gT"], slow=True)
            P.dma("sp", gqT[:, l, :], I["g_q_lora"][l:l + 1, :].rearrange("o (c p) -> p (o c)", p=128), writes=["gqT"], slow=True)
            P.dma("sp", gkvT[:, l, :], I["g_kv_lora"][l:l + 1, :].rearrange("o (c p) -> p (o c)", p=128), writes=["gkvT"], slow=True)
            P.dma("sp", cwT[:, l, :, :], I["conv_w"][l].rearrange("j (c p) -> p j c", p=128), writes=["cwT"], slow=True)
            P.dma("sp", fcwT[:, l, :, :], I["ffn_conv_w"][l].rearrange("j (c p) -> p j c", p=128), writes=["fcwT"], slow=True)
            P.dma("sp", nbf[:, l:l + 1], I["b_forget"][l:l + 1, :].rearrange("o h -> h o"), writes=["nbf"], slow=True)
            P.dma("sp", gkvB[:, l, :], I["g_kv_lora"][l:l + 1, :].broadcast_to([128, 256]), writes=["gkvB"])
            P.dma("sp", bfB[:, l, :], I["b_forget"][l:l + 1, :].broadcast_to([128, 8]), writes=["bfB"])
        ts(nbf[:], nbf[:], -1.0, None, ALU.mult, None, ["nbf"], ["nbf"])

        def norm_stats(src, nch, T, n, skeys):
            sq = tb[:, 0:nch, 0:T]
            act(sq, src, AF.Square, list(skeys), ["tb"])
            pt, pk = ps_next()
            mm(pt[:, 0:T], [(ones[:], tb[:, c, 0:T]) for c in range(nch)], ["ones", "tb"], [pk])
            act(rstd[:, 0:T], pt[:, 0:T], AF.Ln, [pk], ["rstd"], scale=1.0 / n, bias=small[:, 0:1])
            act(rstd[:, 0:T], rstd[:, 0:T], AF.Exp, ["rstd"], ["rstd"], scale=-0.5)

        P.op("dve", lambda e: e.memset(small[:, 0:1], EPS), writes=["small"])
        P.op("dve", lambda e: e.memset(small[:, 1:2], 1.0), writes=["small"])

        def prenorm(l, n_idx, T):
            norm_stats(xT[:, :, 0:T], 8, T, D, XT)
            for c in range(8):
                if c in POOLC:
                    ts(mT[:, c, 0:T], xT[:, c, 0:T], gT[:, l, n_idx, c:c + 1], None, ALU.mult, None,
                       ["xT", ("xTc", c), "gT", "mT", ("mTc", c)], [("mTc", c)], eng="pool")
                    tt(hT[:, c, 0:T], mT[:, c, 0:T], rstd[:, 0:T], ALU.mult, [("mTc", c), "mT", "rstd"], [("hT", c)], eng="pool")
                else:
                    stt(hT[:, c, 0:T], xT[:, c, 0:T], gT[:, l, n_idx, c:c + 1], rstd[:, 0:T], ALU.mult, ALU.mult,
                        ["xT", ("xTc", c), "rstd", "gT"], [("hT", c)])
        POOLC = ()
        FFN_ENG = "dve"
        HT = [("hT", c) for c in range(8)]
        XT = ["xT"] + [("xTc", c) for c in range(8)]

        def postnorm_residual(l, n_idx, T):
            norm_stats(mT[:, :, 0:T], 8, T, D, ["mT"])
            for c in range(8):
                en = "pool" if c in POOLC else "dve"
                tt(mT[:, c, 0:T], mT[:, c, 0:T], rstd[:, 0:T], ALU.mult, [("mTc", c), "mT", "rstd"], [("mTc", c)], eng=en)
                if en == "pool":
                    ts(mT[:, c, 0:T], mT[:, c, 0:T], gT[:, l, n_idx, c:c + 1], None, ALU.mult, None, [("mTc", c), "mT", "gT"], [("mTc", c)], eng=en)
                    tt(xT[:, c, 0:T], xT[:, c, 0:T], mT[:, c, 0:T], ALU.add, [("mTc", c), "mT", ("xTc", c), "xT"], [("xTc", c)], eng=en)
                else:
                    stt(xT[:, c, 0:T], mT[:, c, 0:T], gT[:, l, n_idx, c:c + 1], xT[:, c, 0:T], ALU.mult, ALU.add,
                        [("mTc", c), "mT", "gT", ("xTc", c), "xT"], [("xTc", c)], eng=en)

        def proj_fm(wsrc_fn, ncol_chunks, kc, rhs_fn, rkeys, T, consume, M=128, cw=256, p=128):
            per = cw // M
            j = 0
            while j < ncol_chunks:
                nb = min(per, ncol_chunks - j)
                wv, wk = wload(wsrc_fn(j * M, nb * M), kc, nb * M, p=p)
                for jj in range(nb):
                    pt, pk = ps_next()
                    mm(pt[0:M, 0:T], [(wv[0:p, k, jj * M:(jj + 1) * M], rhs_fn(k)) for k in range(kc)],
                       [wk] + rkeys, [pk])
                    consume(j + jj, pt, pk)
                j += nb

        def attention(Kst, Qt, Vst, vkey, krows, scale, mask, T, kblocks, hkeys, out_key):
            nblk = len(kblocks)
            LA = 3

            def s_block(h, bi):
                k0, ks, q0, diag = kblocks[bi]
                N = T - q0
                pt, pk = ps_next()
                mm(pt[0:ks, 0:N], [(Kst[0:128, h, k0:k0 + ks], Qt[0:128, h, q0:T])], hkeys, [pk], last=not diag)
                if diag:
                    dn = min(128, N)
                    mm(pt[0:ks, 0:dn], [(identb[0:ks, 0:ks], mask[0:ks, 0:dn])], ["identb", "mtri", "mchk"], [pk], first=False)
                return pt, pk
            units = [(h, bi) for h in range(8) for bi in range(nblk)]

            def finish_unit(h, bi, src_fn):
                k0, ks, q0, diag = kblocks[bi]
                N = T - q0
                (po, pok), (pz, pzk) = ps_acc(h)
                src, skey = src_fn(ks, N)
                kt, kp = k0 // 128, k0 % 128
                mm(po[0:128, q0:T], [(Vst[kp:kp + ks, kt, (h // 2) * 128:(h // 2 + 1) * 128], src)],
                   [skey, vkey], [pok], first=(bi == 0), last=(bi == nblk - 1))
                mm(pz[0:128, q0:T], [(ones[0:ks, 0:128], src)], [skey, "ones"], [pzk], first=(bi == 0), last=(bi == nblk - 1))
                if bi == nblk - 1:
                    i = rr("tA", 3)
                    r0 = (h % 2) * 64
                    P.op("dve", lambda e, i=i, pz=pz, r0=r0: e.reciprocal(tmpA[i][r0:r0 + 64, 0:T], pz[r0:r0 + 64, 0:T]), [pzk], [("tA", i)])
                    tt(oT[r0:r0 + 64, h // 2, 0:T], po[r0:r0 + 64, 0:T], tmpA[i][r0:r0 + 64, 0:T], ALU.mult, [pok, ("tA", i)], [out_key])

            if T == TP:
                def s_pair(p):
                    pt, pk = ps_next()
                    for j in (0, 1):
                        h, bi = units[2 * p + j]
                        k0, ks, q0, diag = kblocks[bi]
                        N = T - q0
                        reg = pt[:, j * 256:(j + 1) * 256]
                        mm(reg[0:ks, 0:N], [(Kst[0:128, h, k0:k0 + ks], Qt[0:128, h, q0:T])], hkeys, [pk], last=not diag)
                        if diag:
                            dn = min(128, N)
                            mm(reg[0:ks, 0:dn], [(identb[0:ks, 0:ks], mask[0:ks, 0:dn])], ["identb", "mtri", "mchk"], [pk], first=False)
                    return pt, pk
                npairs = len(units) // 2
                ppend = [s_pair(p) for p in range(min(2, npairs))]
                for p in range(npairs):
                    pt, pk = ppend.pop(0)
                    if p + 2 < npairs:
                        ppend.append(s_pair(p + 2))
                    i = rr("pT2", 3)
                    full = all(kblocks[units[2 * p + j][1]][1] == 128 and kblocks[units[2 * p + j][1]][2] == 0 for j in (0, 1))
                    if full:
                        act(pT2[i][:, :], pt[:, 0:512], AF.Exp, [pk], [("pT2", i)], scale=scale)
                    else:
                        for j in (0, 1):
                            _k0, ks_, q0_, _d = kblocks[units[2 * p + j][1]]
                            n_ = T - q0_
                            act(pT2[i][0:ks_, j * 256:j * 256 + n_], pt[0:ks_, j * 256:j * 256 + n_], AF.Exp, [pk], [("pT2", i)], scale=scale)
                    for j in (0, 1):
                        h, bi = units[2 * p + j]
                        finish_unit(h, bi, lambda ks, N, i=i, j=j: (pT2[i][0:ks, j * 256:j * 256 + N], ("pT2", i)))
                return
            pend = [s_block(*units[u]) for u in range(min(LA, len(units)))]
            for u, (h, bi) in enumerate(units):
                k0, ks, q0, diag = kblocks[bi]
                N = T - q0
                (po, pok), (pz, pzk) = ps_acc(h)
                pt, pk = pend.pop(0)
                if u + LA < len(units):
                    pend.append(s_block(*units[u + LA]))
                i = rr("pT", 3)
                act(pT[i][0:ks, 0:N], pt[0:ks, 0:N], AF.Exp, [pk], [("pT", i)], scale=scale)
                kt, kp = k0 // 128, k0 % 128
                mm(po[0:128, q0:T], [(Vst[kp:kp + ks, kt, (h // 2) * 128:(h // 2 + 1) * 128], pT[i][0:ks, 0:N])],
                   [("pT", i), vkey], [pok], first=(bi == 0), last=(bi == nblk - 1))
                mm(pz[0:128, q0:T], [(ones[0:ks, 0:128], pT[i][0:ks, 0:N])],
                   [("pT", i), "ones"], [pzk], first=(bi == 0), last=(bi == nblk - 1))
                if bi == nblk - 1:
                    i = rr("tA", 3)
                    r0 = (h % 2) * 64
                    P.op("dve", lambda e, i=i, pz=pz, r0=r0: e.reciprocal(tmpA[i][r0:r0 + 64, 0:T], pz[r0:r0 + 64, 0:T]), [pzk], [("tA", i)])
                    tt(oT[r0:r0 + 64, h // 2, 0:T], po[r0:r0 + 64, 0:T], tmpA[i][r0:r0 + 64, 0:T], ALU.mult, [pok, ("tA", i)], [out_key])

        def kblocks_for(kpos0, T):
            blks = []
            for kt in range(kpos0 // 128):
                blks.append((kt * 128, 128, 0, False))
            if T >= 128:
                for j in range(T // 128):
                    blks.append((kpos0 + j * 128, 128, j * 128, True))
            else:
                blks.append((kpos0, T, 0, True))
            return blks

        def out_dma(dst, src, reads):
            fin.append(P.dma("act", dst, src, reads=reads))

        def run_tile(l, T, pos0, xr0, first, last, grp, b, tile_i):
            subs = [(s * 128, min(128, T - s * 128)) for s in range((T + 127) // 128)]
            W = I["w_in"][l]
            if l == 0:
                src = I["x_prompt"][pos0:pos0 + T, :] if grp == "p" else I["x_sample"][b]
                for s, (r0, nr) in enumerate(subs):
                    P.dma("sp", memx[0:nr, 0, :], src[r0:r0 + nr, :], writes=[("memx", 0), ("memx", 1)])
                    for c in range(8):
                        pt, pk = ps_next()
                        P.op("pe", lambda e, pt=pt, c=c, nr=nr: e.transpose(pt[:, 0:nr], memx[0:nr, 0, c * 128:(c + 1) * 128], ident[0:nr, 0:nr]),
                             [("memx", 0), ("memx", 1)] + ["ident"], [pk])
                        acopy(xT[:, c, r0:r0 + nr], pt[:, 0:nr], [pk], ["xT"])
            else:
                P.dma("sp", xT[:, :, 0:T], xres[:, :, xr0:xr0 + T], reads=["xres"], writes=["xT"])
            P.dma("sp", ropeT[64:96, :, 0:T], I["c_ropeT"][:, :, pos0:pos0 + T].rearrange("a r t -> r a t"), writes=["ropeT"])
            for s, (r0, nr) in enumerate(subs):
                P.dma("sp", ropeK[0:nr, s, :, :], I["c_ropeK"][:, pos0 + r0:pos0 + r0 + nr, :].rearrange("a t r -> t a r"), writes=["ropeK"])

            prenorm(l, 0, T)
            rhs_h = lambda k: hT[:, k, 0:T]

            if first:
                if grp == "p":
                    P.op("dve", lambda e: e.memset(uT[:, :, 0:2], 0.0), writes=["uT"])
                else:
                    for jj in range(2):
                        P.dma("sp", uT[:, :, jj], I["state_conv"][l, b, jj:jj + 1, :].rearrange("o (c p) -> p (o c)", p=128), writes=["uT"], slow=True)
            ccs = {}

            def cons_cc(j, pt, pk):
                i = rr("tA", 3)
                acopy(tmpA[i][:, 0:T], pt[:, 0:T], [pk], [("tA", i)])
                ccs[j] = i
            for ch in range(4):
                proj_fm(lambda c0, cw, ch=ch: wrows(W, O_CC + ch * 128, 128), 1, 8, rhs_h, HT, T, lambda j, pt, pk, ch=ch: cons_cc(ch, pt, pk), cw=128)

                def cons_cx(j, pt, pk, ch=ch):
                    i = ccs[ch]
                    tt(uT[:, ch, 2:2 + T], pt[:, 0:T], tmpA[i][:, 0:T], ALU.mult, [pk, ("tA", i)], ["uT"])
                proj_fm(lambda c0, cw, ch=ch: wrows(W, O_CX + ch * 128, 128), 1, 8, rhs_h, HT, T, cons_cx, cw=128)
            if last:
                dst = O["conv_p"][l] if grp == "p" else O["conv_s"][l, b]
                for jj in range(2):
                    fin.append(P.dma("act", dst[jj:jj + 1, :].rearrange("o (c p) -> p (o c)", p=128), uT[:, :, T + jj], reads=["uT"], slow=True))
            convs = {}
            for ch in range(4):
                i = rr("tA", 3)
                ts(tmpA[i][:, 0:T], uT[:, ch, 0:T], cwT[:, l, 0, ch:ch + 1], None, ALU.mult, None, ["uT", "cwT"], [("tA", i)])
                stt(tmpA[i][:, 0:T], uT[:, ch, 1:1 + T], cwT[:, l, 1, ch:ch + 1], tmpA[i][:, 0:T], ALU.mult, ALU.add, ["uT", "cwT", ("tA", i)], [("tA", i)])
                stt(tmpA[i][:, 0:T], uT[:, ch, 2:2 + T], cwT[:, l, 2, ch:ch + 1], tmpA[i][:, 0:T], ALU.mult, ALU.add, ["uT", "cwT", ("tA", i)], [("tA", i)])

                def cons_cb(j, pt, pk, ch=ch, i=i):
                    tt(aT[:, ch, 0:T], pt[:, 0:T], tmpA[i][:, 0:T], ALU.mult, [pk, ("tA", i)], [("aT", ch)])
                proj_fm(lambda c0, cw, ch=ch: wrows(W, O_CB + ch * 128, 128), 1, 8, rhs_h, HT, T, cons_cb, cw=128)
            if not last:
                acopy(uT[:, :, 0:2], uT[:, :, T:T + 2], ["uT"], ["uT"])

            P.op("dve", lambda e: e.memset(fQ[64:128, :, :], 0.0), writes=["fQa"])
            P.op("dve", lambda e: e.memset(fQ[64:68, :, :], 1.0), writes=["fQa"])

            def cons_q(j, pt, pk):
                acopy(fQ[0:64, 2 * j, 0:T], pt[0:64, 0:T], [pk], ["fQ"])
                acopy(fQ[0:64, 2 * j + 1, 0:T], pt[64:128, 0:T], [pk], ["fQ"])
            proj_fm(lambda c0, cw: wrows(W, O_FQ + c0, cw), 4, 8, rhs_h, HT, T, cons_q)

            def cons_k(j, pt, pk):
                acopy(fK[0:64, 2 * j, pos0:pos0 + T], pt[0:64, 0:T], [pk], ["fK"])
                acopy(fK[0:64, 2 * j + 1, pos0:pos0 + T], pt[64:128, 0:T], [pk], ["fK"])
            proj_fm(lambda c0, cw: wrows(W, O_FK + c0, cw), 4, 8, rhs_h, HT, T, cons_k)
            wv, wk = wload(wrows(W, O_FF, 8), 8, 8)
            pt, pk = ps_next()
            mm(pt[0:8, 0:T], [(wv[:, k, 0:8], hT[:, k, 0:T]) for k in range(8)], [wk] + HT, [pk])
            act(lf[:, 0:T], pt[0:8, 0:T], AF.Exp, [pk, "nbf"], ["lf"], scale=-1.0, bias=nbf[:, l:l + 1])
            act(lf[:, 0:T], lf[:, 0:T], AF.Ln, ["lf"], ["lf"], bias=small[0:8, 1:2])
            ts(lf[:, 0:T], lf[:, 0:T], -1.0, None, ALU.mult, None, ["lf"], ["lf"])
            if first and grp == "p":
                P.op("dve", lambda e: e.memset(carry[:], 0.0), writes=["carry"])
            P.op("dve", lambda e: e.tensor_tensor_scan(cum[:, 0:T], ones_f[0:8, 0:T], lf[:, 0:T], carry[:, 0:1], ALU.mult, ALU.add),
                 ["lf", "carry", "ones_f"], ["cum"])
            cum_to_rows(T, pos0, True)
            okey = "fox_k_" + grp
            vkey = "fox_v_" + grp
            for (col0, dname, isv) in ((O_FK, okey, False), (O_FV, vkey, True)):
                for hf in range(2):
                    wv_, wk_ = wload(wrows(W, col0 + hf * 256, 256), 8, 256)
                    for s_, (r0, nr) in enumerate(subs):
                        pt, pk = ps_next()
                        mm(pt[0:nr, 0:256], [(hT[:, k, r0:r0 + nr], wv_[:, k, :]) for k in range(8)], [wk_] + HT, [pk])
                        i = rr("stg", 2)
                        acopy(stg[i][0:nr, 0:256], pt[0:nr, 0:256], [pk], [("stg", i)])
                        if isv:
                            kt = (pos0 + r0) // 128
                            vcopy(fV[0:nr, kt, hf * 256:(hf + 1) * 256], pt[0:nr, 0:256], [pk], ["fV"])
                        dst = O[dname][l, pos0 + r0:pos0 + r0 + nr, hf * 256:(hf + 1) * 256] if grp == "p" else O[dname][l, b, r0:r0 + nr, hf * 256:(hf + 1) * 256]
                        out_dma(dst, stg[i][0:nr, 0:256], [("stg", i)])
            kb = kblocks_for(pos0, T)
            attention(fK, fQ, fV, "fV", 68, 0.125, mtri, T, kb, ["fK", "fKa", "fQ", "fQa"], "oTf")
            wv, wk = wload(wrows(W, O_FF, 8), 8, 8)
            for s, (r0, nr) in enumerate(subs):
                pt, pk = ps_next()
                mm(pt[0:nr, 0:8], [(hT[:, k, r0:r0 + nr], wv[:, k, 0:8]) for k in range(8)], [wk] + HT, [pk])
                i = rr("stg", 2)
                tt(stg[i][0:nr, 0:8], pt[0:nr, 0:8], bfB[0:nr, l, :], ALU.add, [pk, "bfB"], [("stg", i)])
                act(stg[i][0:nr, 0:8], stg[i][0:nr, 0:8], AF.Exp, [("stg", i)], [("stg", i)], scale=-1.0)
                act(stg[i][0:nr, 0:8], stg[i][0:nr, 0:8], AF.Ln, [("stg", i)], [("stg", i)], bias=small[0:nr, 1:2])
                ts(stg[i][0:nr, 0:8], stg[i][0:nr, 0:8], -1.0, None, ALU.mult, None, [("stg", i)], [("stg", i)])
                dst = O["logf_p"][l, pos0 + r0:pos0 + r0 + nr, :] if grp == "p" else O["logf_s"][l, b, r0:r0 + nr, :]
                out_dma(dst, stg[i][0:nr, 0:8], [("stg", i)])

            def branch_out(wname, gate_off, rhs_fn, rkeys, kc, p, firstb):
                wout = I[wname][l]
                for c in range(8):
                    gi = [None]

                    def cons_g(j, pt, pk):
                        gi[0] = rr("tA", 3)
                        act(tmpA[gi[0]][:, 0:T], pt[:, 0:T], AF.Sigmoid, [pk], [("tA", gi[0])])
                    proj_fm(lambda c0, cw, c=c: wrows(W, gate_off + c * 128, 128), 1, 8, rhs_h, HT, T, cons_g, cw=128)

                    def cons_y(j, pt, pk, c=c):
                        g = gi[0]
                        if firstb:
                            tt(mT[:, c, 0:T], pt[:, 0:T], tmpA[g][:, 0:T], ALU.mult, [pk, ("tA", g)], ["mT"])
                        else:
                            tt(tmpA[g][:, 0:T], pt[:, 0:T], tmpA[g][:, 0:T], ALU.mult, [pk, ("tA", g)], [("tA", g)])
                            tt(mT[:, c, 0:T], mT[:, c, 0:T], tmpA[g][:, 0:T], ALU.add, ["mT", ("tA", g)], ["mT"])
                    proj_fm(lambda c0, cw, c=c: wout[:, c * 128:(c + 1) * 128].rearrange("(k p) c -> p k c", p=p), 1, kc, rhs_fn, rkeys, T, cons_y, cw=128, p=p)

            branch_out("w_conv_out", O_GA, lambda k: aT[:, k, 0:T], [("aT", c) for c in range(4)], 4, 128, True)
            branch_out("w_fox_out", O_GB, lambda k: oT[:, k, 0:T], ["oTf"], 4, 128, False)

            def cons_cq(j, pt, pk):
                acopy(cqT[:, j, 0:T], pt[:, 0:T], [pk], ["cqT"])
            proj_fm(lambda c0, cw: wrows(W, O_CQ + c0, cw), 3, 8, rhs_h, HT, T, cons_cq)
            norm_stats(cqT[:, :, 0:T], 3, T, 384, ["cqT"])
            for c in range(3):
                stt(cqn[:, c, 0:T], cqT[:, c, 0:T], gqT[:, l, c:c + 1], rstd[:, 0:T], ALU.mult, ALU.mult, ["cqT", "rstd", "gqT"], ["cqn"])
            wq4 = I["w_uq"][l].rearrange("(k p) (h e) -> p k h e", p=128, e=96)
            for h0 in (0, 4):
                wa, wak = wload(wrows(I["w_uq"][l], h0 * 96, 384), 3, 384)
                def build_sw(view, key, h0=h0):
                    P.dma("pool", view, wrows(I["w_uq"][l], h0 * 96, 384), writes=[key])
                    v4 = view.rearrange("p k (h e) -> p k h e", e=96)
                    for k3 in range(3):
                        P.dma("pool", v4[:, k3, :, 64:80], wq4[:, k3, h0:h0 + 4, 80:96], writes=[key])
                        P.dma("pool", v4[:, k3, :, 80:96], wq4[:, k3, h0:h0 + 4, 64:80], writes=[key])
                wb_, wbk = wcached(("uq_sw", l, h0), 3, 384, 128, build_sw)
                for hh in range(4):
                    h = h0 + hh
                    pa, pak = ps_next()
                    mm(pa[0:96, 0:T], [(wa[:, k, hh * 96:(hh + 1) * 96], cqn[:, k, 0:T]) for k in range(3)], [wak, "cqn"], [pak])
                    pb, pbk = ps_next()
                    mm(pb[0:96, 0:T], [(wb_[:, k, hh * 96:(hh + 1) * 96], cqn[:, k, 0:T]) for k in range(3)], [wbk, "cqn"], [pbk])
                    acopy(fQ[0:64, h, 0:T], pa[0:64, 0:T], [pak], ["fQ"])
                    i1, i2 = rr("tA", 3), rr("tA", 3)
                    tt(tmpA[i1][64:96, 0:T], pa[64:96, 0:T], ropeT[64:96, 0, 0:T], ALU.mult, [pak, "ropeT"], [("tA", i1)])
                    tt(tmpA[i2][64:96, 0:T], pb[64:96, 0:T], ropeT[64:96, 1, 0:T], ALU.mult, [pbk, "ropeT"], [("tA", i2)])
                    tt(fQ[64:96, h, 0:T], tmpA[i1][64:96, 0:T], tmpA[i2][64:96, 0:T], ALU.add, [("tA", i1), ("tA", i2)], ["fQa"])
            def cons_ckv(j, pt, pk):
                acopy(ckT[:, j, 0:T], pt[:, 0:T], [pk], ["ckT"])
            proj_fm(lambda c0, cw: wrows(W, O_CKV + c0, cw), 2, 8, rhs_h, HT, T, cons_ckv)
            norm_stats(ckT[:, :, 0:T], 2, T, 256, ["ckT"])
            for c in range(2):
                stt(ckn[:, c, 0:T], ckT[:, c, 0:T], gkvT[:, l, c:c + 1], rstd[:, 0:T], ALU.mult, ALU.mult, ["ckT", "rstd", "gkvT"], ["ckn"])
            wkr, wkrk = wload(wrows(W, O_KR - 64, 96), 8, 96)
            def build_ks(view, key):
                P.dma("pool", view, wrows(W, O_KR - 64, 96), writes=[key])
                P.dma("pool", view[:, :, 64:80], wrows(W, O_KR + 16, 16), writes=[key])
                P.dma("pool", view[:, :, 80:96], wrows(W, O_KR, 16), writes=[key])
            wks, wksk = wcached(("kr_sw", l), 8, 96, 128, build_ks)
            pa, pak = ps_next()
            mm(pa[0:96, 0:T], [(wkr[:, k, :], hT[:, k, 0:T]) for k in range(8)], [wkrk] + HT, [pak])
            pb, pbk = ps_next()
            mm(pb[0:96, 0:T], [(wks[:, k, :], hT[:, k, 0:T]) for k in range(8)], [wksk] + HT, [pbk])
            i1, i2 = rr("tA", 3), rr("tA", 3)
            tt(tmpA[i1][64:96, 0:T], pa[64:96, 0:T], ropeT[64:96, 0, 0:T], ALU.mult, [pak, "ropeT"], [("tA", i1)])
            tt(tmpA[i2][64:96, 0:T], pb[64:96, 0:T], ropeT[64:96, 1, 0:T], ALU.mult, [pbk, "ropeT"], [("tA", i2)])
            tt(krT[64:96, 0:T], tmpA[i1][64:96, 0:T], tmpA[i2][64:96, 0:T], ALU.add, [("tA", i1), ("tA", i2)], ["krT"])
            mla_kv(l, pos0, T)
            attention(mK, fQ, mV, "mV", 96, 96.0 ** -0.5, mchk, T, kb, ["mK", "mKr", "fQ", "fQa"], "oTf")
            wck, wckk = wload(wrows(W, O_CKV, 256), 8, 256)
            wkr, wkrk = wload(wrows(W, O_KR - 64, 96), 8, 96)
            for s, (r0, nr) in enumerate(subs):
                pt, pk = ps_next()
                mm(pt[0:nr, 0:256], [(hT[:, k, r0:r0 + nr], wck[:, k, :]) for k in range(8)], [wckk] + HT, [pk])
                i = rr("stg", 2)
                act(stg[i][0:nr, 256:512], pt[0:nr, 0:256], AF.Square, [pk], [("stg", i), "small2"], accum_out=small[0:nr, 2:3])
                act(small[0:nr, 3:4], small[0:nr, 2:3], AF.Sqrt, ["small2"], ["small3"], scale=1.0 / 256, bias=small[0:nr, 0:1])
                P.op("dve", lambda e, nr=nr: e.reciprocal(small[0:nr, 3:4], small[0:nr, 3:4]), ["small3"], ["small3"])
                stt(stg[i][0:nr, 0:256], pt[0:nr, 0:256], small[0:nr, 3:4], gkvB[0:nr, l, :], ALU.mult, ALU.mult, [pk, "small3", "gkvB", ("stg", i)], [("stg", i)])
                dst = O["ckv_p"][l, pos0 + r0:pos0 + r0 + nr, :] if grp == "p" else O["ckv_s"][l, b, r0:r0 + nr, :]
                out_dma(dst, stg[i][0:nr, 0:256], [("stg", i)])
                pt, pk = ps_next()
                mm(pt[0:nr, 0:32], [(hT[:, k, r0:r0 + nr], wkr[:, k, 64:96]) for k in range(8)], [wkrk] + HT, [pk])
                i = rr("stg", 2)
                cs, sn = ropeK[0:nr, s, 0, :], ropeK[0:nr, s, 1, :]
                tt(stg[i][0:nr, 0:16], pt[0:nr, 0:16], cs, ALU.mult, [pk, "ropeK"], [("stg", i)])
                tt(stg[i][0:nr, 32:48], pt[0:nr, 16:32], sn, ALU.mult, [pk, "ropeK"], [("stg", i)])
                tt(stg[i][0:nr, 0:16], stg[i][0:nr, 0:16], stg[i][0:nr, 32:48], ALU.subtract, [("stg", i)], [("stg", i)])
                tt(stg[i][0:nr, 16:32], pt[0:nr, 16:32], cs, ALU.mult, [pk, "ropeK"], [("stg", i)])
                tt(stg[i][0:nr, 32:48], pt[0:nr, 0:16], sn, ALU.mult, [pk, "ropeK"], [("stg", i)])
                tt(stg[i][0:nr, 16:32], stg[i][0:nr, 16:32], stg[i][0:nr, 32:48], ALU.add, [("stg", i)], [("stg", i)])
                dst = O["kr_p"][l, pos0 + r0:pos0 + r0 + nr, :] if grp == "p" else O["kr_s"][l, b, r0:r0 + nr, :]
                out_dma(dst, stg[i][0:nr, 0:32], [("stg", i)])
            branch_out("w_mla_out", O_GC, lambda k: oT[:, k, 0:T], ["oTf"], 4, 128, False)

            vcopy(tb[:, :, 0:T], mT[:, :, 0:T], ["mT"], ["tb"])

            def cons_m(j, pt, pk):
                acopy(mT[:, j, 0:T], pt[:, 0:T], [pk], ["mT"])
            proj_fm(lambda c0, cw: wrows(I["w_mix_out"][l], c0, cw), 8, 8, lambda k: tb[:, k, 0:T], ["tb"], T, cons_m)
            postnorm_residual(l, 1, T)

            prenorm(l, 2, T)

            def cons_caq(j, pt, pk):
                acopy(fQ[:, j, 0:T], pt[:, 0:T], [pk], ["fQ", "fQa"])
            proj_fm(lambda c0, cw: wrows(I["w_ca_q"][l], c0, cw), 4, 8, rhs_h, HT, T, cons_caq)
            def ca_s(h, mt):
                pt, pk = ps_next()
                mm(pt[:, 0:T], [(memKT[:, h, mt * 128:(mt + 1) * 128], fQ[:, h, 0:T])], ["memKT", "fQ", "fQa"], [pk])
                return pt, pk
            cunits = [(h, mt) for h in range(4) for mt in range(2)]
            cpend = [ca_s(*cunits[u]) for u in range(3)]
            for u, (h, mt) in enumerate(cunits):
                (po, pok), (pz, pzk) = ps_acc(h)
                pt, pk = cpend.pop(0)
                if u + 3 < len(cunits):
                    cpend.append(ca_s(*cunits[u + 3]))
                i = rr("pT", 3)
                act(pT[i][:, 0:T], pt[:, 0:T], AF.Exp, [pk], [("pT", i)], scale=128.0 ** -0.5)
                mm(po[:, 0:T], [(memV[:, mt, h * 128:(h + 1) * 128], pT[i][:, 0:T])], [("pT", i), "memV"], [pok], first=(mt == 0), last=(mt == 1))
                mm(pz[:, 0:T], [(ones[:, :], pT[i][:, 0:T])], [("pT", i), "ones"], [pzk], first=(mt == 0), last=(mt == 1))
                if mt == 1:
                    i = rr("tA", 3)
                    P.op("dve", lambda e, i=i, pz=pz: e.reciprocal(tmpA[i][:, 0:T], pz[:, 0:T]), [pzk], [("tA", i)])
                    tt(aT[:, h, 0:T], po[:, 0:T], tmpA[i][:, 0:T], ALU.mult, [pok, ("tA", i)], [("aT", h)])
            proj_fm(lambda c0, cw: wrows(I["w_ca_o"][l], c0, cw), 8, 4, lambda k: aT[:, k, 0:T], [("aT", c) for c in range(4)], T, cons_m)
            postnorm_residual(l, 3, T)

            prenorm(l, 4, T)
            if first:
                if grp == "p":
                    P.op("dve", lambda e: e.memset(fhalo[:], 0.0), writes=["fhalo"])
                else:
                    for jj in range(2):
                        P.dma("sp", fhalo[:, :, jj], I["state_ffn_conv"][l, b, jj:jj + 1, :].rearrange("o (c p) -> p (o c)", p=128), writes=["fhalo"], slow=True)
            Wup = I["w_up"][l]
            for half in range(2):
                for j in range(11):
                    ja = half * 11 + j
                    res = []

                    def build_up(view, key, ja=ja):
                        P.dma("pool", view[:, 0:8, :], wrows(Wup, ja * 128, 128), writes=[key])
                        P.dma("pool", view[:, 8:16, :], wrows(Wup, (22 + ja) * 128, 128), writes=[key])
                    wup, wupk = wcached(("up2", l, ja), 16, 128, 128, build_up)
                    for u_i, idx in enumerate((ja, 22 + ja)):
                        uf = uF[u_i]
                        ukey = ("uF", u_i)
                        acopy(uf[:, 0:2], fhalo[:, idx, :], ["fhalo", ("fhalo", u_i)], [ukey])

                        i = rr("tA", 3)

                        def cons_u(jj, pt, pk, uf=uf, ukey=ukey, i=i, idx=idx):
                            acopy(uf[:, 2:2 + T], pt[:, 0:T], [pk], [ukey])
                            act(tmpA[i][:, 0:T], pt[:, 0:T], AF.Copy, [pk, "fcwT"], [("tA", i)], scale=fcwT[:, l, 2, idx:idx + 1])
                        pt, pk = ps_next()
                        mm(pt[:, 0:T], [(wup[:, u_i * 8 + k, :], hT[:, k, 0:T]) for k in range(8)], [wupk] + HT, [pk])
                        cons_u(0, pt, pk)
                        acopy(fhalo[:, idx, :], uf[:, T:T + 2], [ukey], [("fhalo", u_i)])
                        stt(tmpA[i][:, 0:T], uf[:, 1:1 + T], fcwT[:, l, 1, idx:idx + 1], tmpA[i][:, 0:T], ALU.mult, ALU.add, [ukey, "fcwT", ("tA", i)], [("tA", i)])
                        stt(tmpA[i][:, 0:T], uf[:, 0:T], fcwT[:, l, 0, idx:idx + 1], tmpA[i][:, 0:T], ALU.mult, ALU.add, [ukey, "fcwT", ("tA", i)], [("tA", i)])
                        res.append(i)
                    act(tmpA[res[0]][:, 0:T], tmpA[res[0]][:, 0:T], AF.Gelu_apprx_tanh, [("tA", res[0])], [("tA", res[0])])
                    tt(actT[:, j, 0:T], tmpA[res[0]][:, 0:T], tmpA[res[1]][:, 0:T], ALU.mult, [("tA", res[0]), ("tA", res[1])], ["actT"], eng=FFN_ENG)
                Wd = I["w_down"][l]

                def cons_d(jj, pt, pk, half=half):
                    if half == 0:
                        acopy(mT[:, jj, 0:T], pt[:, 0:T], [pk], ["mT"])
                    else:
                        tt(mT[:, jj, 0:T], mT[:, jj, 0:T], pt[:, 0:T], ALU.add, [pk, "mT"], ["mT"])
                proj_fm(lambda c0, cw, half=half: Wd[half * 1408:(half + 1) * 1408, c0:c0 + cw].rearrange("(k p) c -> p k c", p=128),
                        8, 11, lambda k: actT[:, k, 0:T], ["actT"], T, cons_d, cw=128)
            if last:
                dst = O["ffn_p"][l] if grp == "p" else O["ffn_s"][l, b]
                for jj in range(2):
                    fin.append(P.dma("act", dst[jj:jj + 1, :].rearrange("o (c p) -> p (o c)", p=128), fhalo[:, :, jj], reads=["fhalo", ("fhalo", 0), ("fhalo", 1)], slow=True))
            postnorm_residual(l, 5, T)

            if l == 0:
                P.dma("act", xres[:, :, xr0:xr0 + T], xT[:, :, 0:T], reads=XT, writes=["xres"])
            else:
                for s, (r0, nr) in enumerate(subs):
                    for hf in range(2):
                        pt, pk = ps_next(full=True)
                        for cc in range(4):
                            c = hf * 4 + cc
                            P.op("pe", lambda e, pt=pt, c=c, cc=cc, r0=r0, nr=nr: e.transpose(pt[0:nr, cc * 128:(cc + 1) * 128], xT[:, c, r0:r0 + nr], ident[:, :]),
                                 XT + ["ident"], [pk])
                        i = rr("stg", 2)
                        acopy(stg[i][0:nr, 0:512], pt[0:nr, 0:512], [pk], [("stg", i)])
                        dst = O["y_p"][pos0 + r0:pos0 + r0 + nr, hf * 512:(hf + 1) * 512] if grp == "p" else O["y_s"][b, r0:r0 + nr, hf * 512:(hf + 1) * 512]
                        out_dma(dst, stg[i][0:nr, 0:512], [("stg", i)])

        def mla_kv(l, pos0, T):
            for h in range(8):
                vcopy(mK[64:96, h, pos0:pos0 + T], krT[64:96, 0:T], ["krT"], ["mKr"])
            wuk, wukk = wload(wrows(I["w_uk"][l], 0, 512), 2, 512)
            for hp in range(4):
                pt, pk = ps_next()
                mm(pt[0:128, 0:T], [(wuk[:, k, hp * 128:(hp + 1) * 128], ckn[:, k, 0:T]) for k in range(2)], [wukk, "ckn"], [pk])
                acopy(mK[0:64, 2 * hp, pos0:pos0 + T], pt[0:64, 0:T], [pk], ["mK"])
                acopy(mK[0:64, 2 * hp + 1, pos0:pos0 + T], pt[64:128, 0:T], [pk], ["mK"])
            wuv, wuvk = wload(wrows(I["w_uv"][l], 0, 512), 2, 512)
            for s in range((T + 127) // 128):
                r0 = s * 128
                nr = min(128, T - r0)
                pt, pk = ps_next(full=True)
                mm(pt[0:nr, 0:512], [(ckn[:, k, r0:r0 + nr], wuv[:, k, :]) for k in range(2)], [wuvk, "ckn"], [pk])
                vcopy(mV[0:nr, (pos0 + r0) // 128, :], pt[0:nr, 0:512], [pk], ["mV"])

        def prep_mem_prompt(l):
            P.dma("sp", gmemT[:, l, :], I["g_mem"][l:l + 1, :].rearrange("o (c p) -> p (o c)", p=128), writes=["gmemT"], slow=True)
            for mt in range(2):
                P.dma("sp", memx[:, 0, :], I["mem_prompt"][mt * 128:(mt + 1) * 128, :], writes=[("memx", 0), ("memx", 1)])
                i = rr("stg", 2)
                act(stg[i][:, 0:512], memx[:, 0, 0:512], AF.Square, [("memx", 0), ("memx", 1)], [("stg", i), "small4"], accum_out=small[:, 4:5])
                act(stg[i][:, 0:512], memx[:, 0, 512:1024], AF.Square, [("memx", 0), ("memx", 1)], [("stg", i), "small5"], accum_out=small[:, 5:6])
                tt(small[:, 4:5], small[:, 4:5], small[:, 5:6], ALU.add, ["small4", "small5"], ["small4"])
                act(small[:, 6:7], small[:, 4:5], AF.Sqrt, ["small4"], ["small6"], scale=1.0 / D, bias=small[:, 0:1])
                P.op("dve", lambda e: e.reciprocal(small[:, 6:7], small[:, 6:7]), ["small6"], ["small6"])
                ts(memx[:, 0, :], memx[:, 0, :], small[:, 6:7], None, ALU.mult, None, [("memx", 0), ("memx", 1)] + ["small6"], [("memx", 0), ("memx", 1)])
                for c in range(8):
                    pt, pk = ps_next()
                    P.op("pe", lambda e, pt=pt, c=c: e.transpose(pt[:, 0:128], memx[:, 0, c * 128:(c + 1) * 128], ident[:, :]), [("memx", 0), ("memx", 1)] + ["ident"], [pk])
                    ts(hT[:, c, mt * 128:(mt + 1) * 128], pt[:, 0:128], gmemT[:, l, c:c + 1], None, ALU.mult, None, [pk, "gmemT"], [("hT", c)])

            def cons_mk(j, pt, pk):
                acopy(memKT[:, j, :], pt[:, 0:256], [pk], ["memKT"])
            proj_fm(lambda c0, cw: wrows(I["w_ca_k"][l], c0, cw), 4, 8, lambda k: hT[:, k, 0:256], HT, 256, cons_mk)
            for (wn, on, isv) in (("w_ca_k", "mem_k_p", False), ("w_ca_v", "mem_v_p", True)):
                for hf in range(2):
                    wv_, wk_ = wload(wrows(I[wn][l], hf * 256, 256), 8, 256)
                    for mt in range(2):
                        pt, pk = ps_next()
                        mm(pt[:, 0:256], [(hT[:, k, mt * 128:(mt + 1) * 128], wv_[:, k, :]) for k in range(8)], [wk_] + HT, [pk])
                        i = rr("stg", 2)
                        acopy(stg[i][:, 0:256], pt[:, 0:256], [pk], [("stg", i)])
                        if isv:
                            vcopy(memV[:, mt, hf * 256:(hf + 1) * 256], pt[:, 0:256], [pk], ["memV"])
                        out_dma(O[on][l, mt * 128:(mt + 1) * 128, hf * 256:(hf + 1) * 256], stg[i][:, 0:256], [("stg", i)])

        def prep_mem_sample(l, b):
            for mt in range(2):
                mo = mt * 512
                P.dma("sp", memx[:, 0, mo:mo + 512], I["cache_mem_k"][l, b, mt * 128:(mt + 1) * 128, :], writes=[("memx", mt)])
                pt, pk = ps_next(full=True)
                for h in range(4):
                    P.op("pe", lambda e, pt=pt, h=h, mo=mo: e.transpose(pt[:, h * 128:(h + 1) * 128], memx[:, 0, mo + h * 128:mo + (h + 1) * 128], ident[:, :]), [("memx", mt), "ident"], [pk])
                acopy(memKT[:, :, mt * 128:(mt + 1) * 128], pt[:, 0:512].rearrange("p (h m) -> p h m", h=4), [pk], ["memKT"])
            P.dma("pool", memV[:, :, :], I["cache_mem_v"][l, b].rearrange("(t p) c -> p t c", p=128), writes=["memV"])

        def prep_cache(l, b):
            P.dma("pool", fV[:, 0:16, :], I["cache_fox_v"][l, b].rearrange("(t p) c -> p t c", p=128), writes=["fV"])
            for kt in range(16):
                mj = kt % 2
                mo = mj * 512
                P.dma("sp", memx[:, 0, mo:mo + 512], I["cache_fox_k"][l, b, kt * 128:(kt + 1) * 128, :], writes=[("memx", mj)])
                for hg in range(2):
                    pt, pk = ps_next(full=True)
                    for hh in range(4):
                        h = hg * 4 + hh
                        P.op("pe", lambda e, pt=pt, h=h, hh=hh, mo=mo: e.transpose(pt[0:64, hh * 128:(hh + 1) * 128], memx[:, 0, mo + h * 64:mo + (h + 1) * 64], ident[:, :]), [("memx", mj), "ident"], [pk])
                    acopy(fK[0:64, hg * 4:(hg + 1) * 4, kt * 128:(kt + 1) * 128], pt[0:64, 0:512].rearrange("p (h m) -> p h m", h=4), [pk], ["fK"])
            P.op("dve", lambda e: e.memset(carry[:], 0.0), writes=["carry"])
            for pc in range(PAST // TP):
                P.dma("sp", lf[:, 0:TP], I["cache_fox_logf"][l, b, pc * TP:(pc + 1) * TP, :].rearrange("t h -> h t"), writes=["lf"], slow=True)
                P.op("dve", lambda e: e.tensor_tensor_scan(cum[:, 0:TP], ones_f[0:8, 0:TP], lf[:, 0:TP], carry[:, 0:1], ALU.mult, ALU.add),
                     ["lf", "carry", "ones_f"], ["cum"])
                cum_to_rows(TP, pc * TP, False)
            for pc in range(PAST // TP):
                for s in range(TP // 128):
                    kt = pc * (TP // 128) + s
                    mj = kt % 2
                    mo = mj * 512
                    P.dma("sp", memx[:, 0, mo:mo + 256], I["cache_mla_ckv"][l, b, kt * 128:(kt + 1) * 128, :], writes=[("memx", mj)])
                    P.dma("sp", memx[:, 0, mo + 320:mo + 352], I["cache_mla_kr"][l, b, kt * 128:(kt + 1) * 128, :], writes=[("memx", mj)])
                    pt, pk = ps_next()
                    for c in range(2):
                        P.op("pe", lambda e, pt=pt, c=c, mo=mo: e.transpose(pt[:, c * 128:(c + 1) * 128], memx[:, 0, mo + c * 128:mo + (c + 1) * 128], ident[:, :]), [("memx", mj), "ident"], [pk])
                    acopy(ckn[:, :, s * 128:(s + 1) * 128], pt[:, 0:256].rearrange("p (c m) -> p c m", c=2), [pk], ["ckn"])
                    pt, pk = ps_next()
                    P.op("pe", lambda e, pt=pt, mo=mo: e.transpose(pt[0:96, 0:128], memx[:, 0, mo + 256:mo + 352], ident[:, :]), [("memx", mj), "ident"], [pk])
                    acopy(krT[64:96, s * 128:(s + 1) * 128], pt[64:96, 0:128], [pk], ["krT"])
                mla_kv(l, pc * TP, TP)

        def cum_to_rows(T, pos0, is_q):
            ts(c8[:, 0:T], cum[:, 0:T], 8.0, None, ALU.mult, None, ["cum"], ["c8"])
            vcopy(chi[:, 0:T], c8[:, 0:T], ["c8"], ["chi"])
            tt(clo[:, 0:T], c8[:, 0:T], chi[:, 0:T], ALU.subtract, ["c8", "chi"], ["clo"])
            if is_q:
                P.dma("act", fQ[64:65, :, 0:T], chi[:, 0:T], reads=["chi"], writes=["fQa"])
                P.dma("act", fQ[65:66, :, 0:T], clo[:, 0:T], reads=["clo"], writes=["fQa"])
            vcopy(carry[:, 0:1], cum[:, T - 1:T], ["cum"], ["carry"])
            ts(chi[:, 0:T], chi[:, 0:T], -1.0, None, ALU.mult, None, ["chi", "fQa"], ["chi"])
            ts(clo[:, 0:T], clo[:, 0:T], -1.0, None, ALU.mult, None, ["clo", "fQa"], ["clo"])
            P.dma("act", fK[66:67, :, pos0:pos0 + T], chi[:, 0:T], reads=["chi"], writes=["fKa"])
            P.dma("act", fK[67:68, :, pos0:pos0 + T], clo[:, 0:T], reads=["clo"], writes=["fKa"])

        ones_f = sb("ones_f", (8, TP))
        P.op("dve", lambda e: e.memset(ones_f[:], 1.0), writes=["ones_f"])

        NTP = SEQ // TP

        class _Dummy:
            def op(self, *a, **k):
                return None

            def dma(self, *a, **k):
                return None

        def convert_ahead(l):
            nonlocal P
            realP, saved_ctr, nfin = P, dict(ctr), len(fin)
            COLLECT[0] = []
            P = _Dummy()
            try:
                run_tile(l, TP, 0, 0, True, False, "p", 0, 0)
            finally:
                P = realP
                blocks, COLLECT[0] = COLLECT[0], None
                ctr.clear()
                ctr.update(saved_ctr)
                del fin[nfin:]
            for (bkey, kc, cw, p, build_fn) in blocks:
                if bkey in wscratch:
                    continue
                wscratch[bkey] = nc.dram_tensor("wsc%d" % len(wscratch), [p, kc * cw], BF16, kind="Internal").ap()
                build_fn(wscratch[bkey].rearrange("p (k c) -> p k c", k=kc), ("wsc", bkey))

        convert_ahead(0)
        for l in range(2):
            prep_mem_prompt(l)
            for i in range(NTP):
                run_tile(l, TP, i * TP, i * TP, i == 0, i == NTP - 1, "p", 0, i)
                if l == 0 and i == 0:
                    convert_ahead(1)
            for b in range(2):
                prep_mem_sample(l, b)
                prep_cache(l, b)
                run_tile(l, TS, PAST, SEQ + b * TS, True, True, "s", b, 0)
        P.run(fin)
    return nc


_CACHE = {}


def _consts():
    ident = np.eye(128, dtype=np.float32)
    k = np.arange(128)[:, None]
    q = np.arange(128)[None, :]
    mtri = np.where(k <= q, 0.0, NEG).astype(np.float32)
    mchk = np.where((k // 64) <= (q // 64), 0.0, NEG).astype(np.float32)
    pos = np.arange(KTOT, dtype=np.float32)
    inv = (10000.0 ** (-np.arange(16, dtype=np.float32) / 16)).astype(np.float32)
    ang = pos[:, None] * inv[None, :]
    cos, sin = np.cos(ang).astype(np.float32), np.sin(ang).astype(np.float32)
    ropeT = np.stack([np.concatenate([cos, cos], 1).T, np.concatenate([-sin, sin], 1).T], 0)
    ropeK = np.stack([cos, sin], 0)
    return dict(c_ident=ident, c_mtri=mtri, c_mchk=mchk, c_ropeT=np.ascontiguousarray(ropeT, dtype=np.float32),
                c_ropeK=np.ascontiguousarray(ropeK, dtype=np.float32))


def kernel(**inputs):
    inp = {k: np.asarray(v, dtype=np.float32) for k, v in inputs.items()}
    if "nc" not in _CACHE:
        _CACHE["nc"] = build_program()
    nc = _CACHE["nc"]
    consts = _consts()
    L = 2
    shared = {}
    for k in ("w_in", "b_forget", "conv_w", "g_q_lora", "g_kv_lora", "w_conv_out", "w_fox_out", "w_mla_out", "w_mix_out",
              "g_mem", "w_up", "ffn_conv_w", "w_down", "g_norms"):
        shared[k] = inp[k]
    for k in ("w_uq", "w_uk", "w_uv", "w_ca_q", "w_ca_k", "w_ca_v"):
        a = inp[k]
        shared[k] = a.reshape(a.shape[0], a.shape[1], -1)
    shared["w_ca_o"] = inp["w_ca_o"].reshape(L, 512, D)
    shared.update(consts)
    in_maps = []
    for c in range(NCORE):
        m = dict(shared)
        m["x_prompt"] = inp["x_prompt"][c]
        m["x_sample"] = inp["x_sample"][2 * c:2 * c + 2]
        m["mem_prompt"] = inp["mem_prompt"][c]
        for k in ("cache_fox_k", "cache_fox_v", "cache_mem_k", "cache_mem_v"):
            a = inp[k][:, 2 * c:2 * c + 2]
            m[k] = np.ascontiguousarray(a.reshape(a.shape[0], 2, a.shape[2], -1))
        for k in ("cache_fox_logf", "cache_mla_ckv", "cache_mla_kr", "state_conv", "state_ffn_conv"):
            m[k] = np.ascontiguousarray(inp[k][:, 2 * c:2 * c + 2])
        in_maps.append(m)
    res = run_bass_kernel_spmd(nc, in_maps, core_ids=list(range(NCORE)))
    R = res.results
    _CACHE["last"] = R

    def cat(name, axis, shape=None):
        a = np.stack([np.asarray(R[c][name]) for c in range(NCORE)], axis=axis)
        return a

    y_p = cat("y_p", 0)
    y_s = np.concatenate([R[c]["y_s"] for c in range(NCORE)], 0)
    outs = [y_p, y_s]
    outs.append(cat("fox_k_p", 1).reshape(L, 8, SEQ, 8, 64))
    outs.append(cat("fox_v_p", 1).reshape(L, 8, SEQ, 8, 64))
    outs.append(cat("logf_p", 1))
    outs.append(cat("ckv_p", 1))
    outs.append(cat("kr_p", 1))
    outs.append(cat("conv_p", 1))
    outs.append(cat("ffn_p", 1))
    outs.append(cat("mem_k_p", 1).reshape(L, 8, 256, 4, 128))
    outs.append(cat("mem_v_p", 1).reshape(L, 8, 256, 4, 128))

    def cats(name):
        return np.concatenate([R[c][name] for c in range(NCORE)], 1)
    outs.append(cats("fox_k_s").reshape(L, 16, TS, 8, 64))
    outs.append(cats("fox_v_s").reshape(L, 16, TS, 8, 64))
    outs.append(cats("logf_s"))
    outs.append(cats("ckv_s"))
    outs.append(cats("kr_s"))
    outs.append(cats("conv_s"))
    outs.append(cats("ffn_s"))
    return tuple(np.ascontiguousarray(o, dtype=np.float32) for o in outs)
```

```python
from contextlib import ExitStack
import numpy as np
import concourse.bass as bass
import concourse.mybir as mybir
from concourse.bass_utils import run_bass_kernel_spmd

F32 = mybir.dt.float32
BF16 = mybir.dt.bfloat16
AF = mybir.ActivationFunctionType
ALU = mybir.AluOpType

N_DMA_SEM = 5
D = 1024
NCORE = 8
SEQ = 2048
PAST = 2048
TS = 64
DIN = 6824
DFF = 2816
TP = 256
KTOT = PAST + TS
EPS = 1e-6
NEG = -30000.0
O_CB, O_CC, O_CX, O_FQ, O_FK, O_FV, O_FF, O_CQ, O_CKV, O_KR, O_GA, O_GB, O_GC = (
    0, 512, 1024, 1536, 2048, 2560, 3072, 3080, 3464, 3720, 3752, 4776, 5800)


class Op:
    __slots__ = ("eng", "fn", "waits", "need_inc", "sem", "val", "dma", "idx")

    def __init__(self, eng, fn, dma):
        self.eng = eng
        self.fn = fn
        self.waits = []
        self.need_inc = dma
        self.sem = None
        self.val = None
        self.dma = dma
        self.idx = 0


class Prog:
    ENGS = ("pe", "act", "dve", "pool", "sp")

    def __init__(self, nc, stack):
        self.nc = nc
        self.ops = {e: [] for e in self.ENGS}
        self.last_w = {}
        self.readers = {}
        self.sems = {e: stack.enter_context(nc.semaphore("s_" + e)) for e in self.ENGS}
        self.dsems = {e: [stack.enter_context(nc.semaphore("d_%s%d" % (e, i))) for i in range(N_DMA_SEM)]
                      for e in ("sp", "pool", "act")}
        self.n_dma = {e: 0 for e in ("sp", "pool", "act")}

    def op(self, eng, fn, reads=(), writes=(), dma=False):
        o = Op(eng, fn, dma)
        deps = []
        for k in reads:
            w = self.last_w.get(k)
            if w is not None:
                deps.append(w)
        for k in writes:
            w = self.last_w.get(k)
            if w is not None:
                deps.append(w)
            deps.extend(self.readers.get(k, ()))
        seen = set()
        for d in deps:
            if id(d) in seen:
                continue
            seen.add(id(d))
            if d.eng == eng and eng == "pe" and not d.dma:
                continue
            d.need_inc = True
            o.waits.append(d)
        for k in writes:
            self.last_w[k] = o
            self.readers[k] = []
        for k in reads:
            self.readers.setdefault(k, []).append(o)
        self.ops[eng].append(o)
        return o

    def dma(self, eng, out, in_, reads=(), writes=(), slow=False):
        if slow:
            return self.op(eng, lambda e: e.dma_start(out=out, in_=in_, allow_slow_non_contiguous=True),
                           reads, writes, dma=True)
        return self.op(eng, lambda e: e.dma_start(out=out, in_=in_), reads, writes, dma=True)

    def finalize(self):
        for e in self.ENGS:
            cnt = 0
            for o in self.ops[e]:
                if o.dma:
                    i = self.n_dma[e]
                    self.n_dma[e] += 1
                    o.sem = self.dsems[e][i % N_DMA_SEM]
                    o.val = 16 * (i // N_DMA_SEM + 1)
                    o.idx = i
                elif o.need_inc:
                    cnt += 1
                    o.sem = self.sems[e]
                    o.val = cnt

    def emit(self, ename, eng):
        waited = {}
        hist = []
        for o in self.ops[ename]:
            if o.dma:
                if o.idx >= N_DMA_SEM:
                    prev = hist[o.idx - N_DMA_SEM]
                    if waited.get(id(prev.sem), 0) < prev.val:
                        eng.wait_ge(prev.sem, prev.val)
                        waited[id(prev.sem)] = prev.val
                hist.append(o)
            for d in o.waits:
                if waited.get(id(d.sem), 0) < d.val:
                    eng.wait_ge(d.sem, d.val)
                    waited[id(d.sem)] = d.val
            ins = o.fn(eng)
            if o.dma:
                ins.then_inc(o.sem, 16)
            elif o.need_inc:
                ins.then_inc(o.sem, 1)

    def run(self, final_ops):
        self.finalize()
        nc = self.nc
        with nc.Block() as block:
            @block.tensor
            def _(e):
                self.emit("pe", e)

            @block.scalar
            def _(e):
                self.emit("act", e)

            @block.vector
            def _(e):
                self.emit("dve", e)

            @block.gpsimd
            def _(e):
                self.emit("pool", e)

            @block.sync
            def _(e):
                self.emit("sp", e)
                done = {}
                for o in final_ops:
                    if done.get(id(o.sem), (None, 0))[1] < o.val:
                        done[id(o.sem)] = (o.sem, o.val)
                for sem, val in done.values():
                    e.wait_ge(sem, val)


def build_program(stop_after=None):
    nc = bass.Bass("TRN2", target_bir_lowering=False)

    def din(name, shape):
        return nc.dram_tensor(name, list(shape), F32, kind="ExternalInput").ap()

    def dout(name, shape):
        return nc.dram_tensor(name, list(shape), F32, kind="ExternalOutput").ap()

    L = 2
    I = dict(
        x_prompt=din("x_prompt", (SEQ, D)), x_sample=din("x_sample", (2, TS, D)),
        cache_fox_k=din("cache_fox_k", (L, 2, PAST, 512)), cache_fox_v=din("cache_fox_v", (L, 2, PAST, 512)),
        cache_fox_logf=din("cache_fox_logf", (L, 2, PAST, 8)), cache_mla_ckv=din("cache_mla_ckv", (L, 2, PAST, 256)),
        cache_mla_kr=din("cache_mla_kr", (L, 2, PAST, 32)), state_conv=din("state_conv", (L, 2, 2, 512)),
        state_ffn_conv=din("state_ffn_conv", (L, 2, 2, 2 * DFF)), cache_mem_k=din("cache_mem_k", (L, 2, 256, 512)),
        cache_mem_v=din("cache_mem_v", (L, 2, 256, 512)), mem_prompt=din("mem_prompt", (256, D)),
        w_in=din("w_in", (L, D, DIN)), b_forget=din("b_forget", (L, 8)), conv_w=din("conv_w", (L, 3, 512)),
        g_q_lora=din("g_q_lora", (L, 384)), g_kv_lora=din("g_kv_lora", (L, 256)),
        w_uq=din("w_uq", (L, 384, 768)), w_uk=din("w_uk", (L, 256, 512)), w_uv=din("w_uv", (L, 256, 512)),
        w_conv_out=din("w_conv_out", (L, 512, D)), w_fox_out=din("w_fox_out", (L, 512, D)),
        w_mla_out=din("w_mla_out", (L, 512, D)), w_mix_out=din("w_mix_out", (L, D, D)), g_mem=din("g_mem", (L, D)),
        w_ca_q=din("w_ca_q", (L, D, 512)), w_ca_k=din("w_ca_k", (L, D, 512)), w_ca_v=din("w_ca_v", (L, D, 512)),
        w_ca_o=din("w_ca_o", (L, 512, D)), w_up=din("w_up", (L, D, 2 * DFF)), ffn_conv_w=din("ffn_conv_w", (L, 3, 2 * DFF)),
        w_down=din("w_down", (L, DFF, D)), g_norms=din("g_norms", (L, 6, D)),
        c_ident=din("c_ident", (128, 128)), c_mtri=din("c_mtri", (128, 128)), c_mchk=din("c_mchk", (128, 128)),
        c_ropeT=din("c_ropeT", (2, 32, KTOT)), c_ropeK=din("c_ropeK", (2, KTOT, 16)),
    )
    O = dict(
        y_p=dout("y_p", (SEQ, D)), y_s=dout("y_s", (2, TS, D)),
        fox_k_p=dout("fox_k_p", (L, SEQ, 512)), fox_v_p=dout("fox_v_p", (L, SEQ, 512)), logf_p=dout("logf_p", (L, SEQ, 8)),
        ckv_p=dout("ckv_p", (L, SEQ, 256)), kr_p=dout("kr_p", (L, SEQ, 32)), conv_p=dout("conv_p", (L, 2, 512)),
        ffn_p=dout("ffn_p", (L, 2, 2 * DFF)), mem_k_p=dout("mem_k_p", (L, 256, 512)), mem_v_p=dout("mem_v_p", (L, 256, 512)),
        fox_k_s=dout("fox_k_s", (L, 2, TS, 512)), fox_v_s=dout("fox_v_s", (L, 2, TS, 512)), logf_s=dout("logf_s", (L, 2, TS, 8)),
        ckv_s=dout("ckv_s", (L, 2, TS, 256)), kr_s=dout("kr_s", (L, 2, TS, 32)), conv_s=dout("conv_s", (L, 2, 2, 512)),
        ffn_s=dout("ffn_s", (L, 2, 2, 2 * DFF)),
    )
    xres = nc.dram_tensor("xres", [128, 8, SEQ + 2 * TS], F32, kind="Internal").ap()

    with ExitStack() as st:
        P = Prog(nc, st)
        fin = []

        def sb(name, shape, dt=F32):
            return st.enter_context(nc.sbuf_tensor(name, list(shape), dt))

        fK = sb("fK", (128, 8, KTOT), BF16)
        mK = sb("mK", (128, 8, KTOT), BF16)
        fV = sb("fV", (128, 17, 512), BF16)
        mV = sb("mV", (128, 17, 512), BF16)
        memKT = sb("memKT", (128, 4, 256), BF16)
        memV = sb("memV", (128, 2, 512), BF16)
        xT = sb("xT", (128, 8, TP))
        hT = sb("hT", (128, 8, TP), BF16)
        tb = sb("tb", (128, 8, TP), BF16)
        mT = sb("mT", (128, 8, TP))
        fQ = sb("fQ", (128, 8, TP), BF16)
        oT = sb("oT", (128, 4, TP), BF16)
        aT = sb("aT", (128, 4, TP), BF16)
        actT = sb("actT", (128, 11, TP), BF16)
        uT = sb("uT", (128, 4, TP + 2))
        uF = [sb("uF%d" % i, (128, TP + 2)) for i in range(2)]
        fhalo = sb("fhalo", (128, 44, 2))
        rstd = sb("rstd", (128, TP))
        tmpA = [sb("tmpA%d" % i, (128, TP)) for i in range(3)]
        pT = [sb("pT%d" % i, (128, TP), BF16) for i in range(3)]
        pT2 = [sb("pT2_%d" % i, (128, 512), BF16) for i in range(3)]
        stg = [sb("stg%d" % i, (128, 512)) for i in range(2)]
        cqT = sb("cqT", (128, 3, TP))
        cqn = sb("cqn", (128, 3, TP), BF16)
        ckT = sb("ckT", (128, 2, TP))
        ckn = sb("ckn", (128, 2, TP), BF16)
        krT = sb("krT", (128, TP), BF16)
        ropeT = sb("ropeT", (128, 2, TP))
        ropeK = sb("ropeK", (128, 2, 2, 16))
        lf = sb("lf", (8, TP))
        cum = sb("cum", (8, TP))
        carry = sb("carry", (8, 1))
        c8 = sb("c8", (8, TP))
        chi = sb("chi", (8, TP), BF16)
        clo = sb("clo", (8, TP), BF16)
        ident = sb("ident", (128, 128))
        identb = sb("identb", (128, 128), BF16)
        mtri = sb("mtri", (128, 128), BF16)
        mchk = sb("mchk", (128, 128), BF16)
        ones = sb("ones", (128, 128), BF16)
        gT = sb("gT", (128, 2, 6, 8))
        gqT = sb("gqT", (128, 2, 3))
        gkvT = sb("gkvT", (128, 2, 2))
        cwT = sb("cwT", (128, 2, 3, 4))
        fcwT = sb("fcwT", (128, 2, 3, 44))
        nbf = sb("nbf", (8, 2))
        gkvB = sb("gkvB", (128, 2, 256))
        gmemT = sb("gmemT", (128, 2, 8))
        bfB = sb("bfB", (128, 2, 8))
        small = sb("small", (128, 16))
        wslot = [sb("wslot%d" % i, (128, 2048), BF16) for i in range(4)]
        memx = sb("memx", (128, 1, D))
        NPS = 8
        psum = [st.enter_context(nc.psum_tensor("ps%d" % i, [128, 512], F32)) for i in range(NPS)]

        ctr = dict(ps=0, psf=0, w=0, tA=0, tB=0, pT=0, pT2=0, stg=0, uF=0)

        def ps_next(full=False):
            i = ctr["ps"] % 4
            ctr["ps"] += 1
            return psum[i][:, :], ("ps", i)

        def ps_acc(h):
            j = h % 2
            return (psum[4 + j][:, :], ("pacc", 4 + j)), (psum[6 + j][:, :], ("pacc", 6 + j))

        def rr(name, n):
            i = ctr[name] % n
            ctr[name] += 1
            return i

        wscratch = {}
        COLLECT = [None]

        def wcached(bkey, kc, cw, p, build_fn):
            if COLLECT[0] is not None:
                COLLECT[0].append((bkey, kc, cw, p, build_fn))
                return wslot[0][0:p, 0:kc * cw].rearrange("p (k c) -> p k c", k=kc), ("w", 0)
            i = rr("w", 4)
            flat = wslot[i][0:p, 0:kc * cw]
            view = flat.rearrange("p (k c) -> p k c", k=kc)
            key = ("w", i)
            if bkey in wscratch:
                P.dma("sp", flat, wscratch[bkey], reads=[("wsc", bkey)], writes=[key])
            else:
                build_fn(view, key)
                wscratch[bkey] = nc.dram_tensor("wsc%d" % len(wscratch), [p, kc * cw], BF16, kind="Internal").ap()
                P.dma("sp", wscratch[bkey], flat, reads=[key], writes=[("wsc", bkey)])
            return view, key

        def wload(src, kc, cw, p=128):
            bkey = (src.tensor.name, src.offset, str(src.ap), p, kc, cw)
            return wcached(bkey, kc, cw, p, lambda view, key: P.dma("pool", view, src, writes=[key]))

        def wrows(w2d, c0, cw, p=128):
            return w2d[:, c0:c0 + cw].rearrange("(k p) c -> p k c", p=p)

        def mm(out, pairs, reads, writes, first=True, last=True):
            def fn(e):
                ins = None
                n = len(pairs)
                for j, (a, b) in enumerate(pairs):
                    ins = e.matmul(out, a, b, start=(first and j == 0), stop=(last and j == n - 1))
                return ins
            return P.op("pe", fn, reads, writes)

        def act(out, in_, func, reads, writes, **kw):
            return P.op("act", lambda e: e.activation(out, in_, func, **kw), reads, writes)

        def acopy(out, in_, reads, writes):
            return P.op("act", lambda e: e.copy(out, in_), reads, writes)

        def vcopy(out, in_, reads, writes):
            return P.op("dve", lambda e: e.tensor_copy(out, in_), reads, writes)

        def tt(out, a, b, op, reads, writes, eng="dve"):
            return P.op(eng, lambda e: e.tensor_tensor(out, a, b, op), reads, writes)

        def ts(out, a, s1, s2, op0, op1, reads, writes, eng="dve"):
            if s2 is None:
                return P.op(eng, lambda e: e.tensor_scalar(out, a, s1, None, op0), reads, writes)
            return P.op(eng, lambda e: e.tensor_scalar(out, a, s1, s2, op0, op1), reads, writes)

        def stt(out, a, s, b, op0, op1, reads, writes, eng="dve"):
            return P.op(eng, lambda e: e.scalar_tensor_tensor(out, a, s, b, op0, op1), reads, writes)

        P.dma("sp", ident[:], I["c_ident"], writes=["ident"])
        vcopy(identb[:], ident[:], ["ident"], ["identb"])
        P.dma("pool", mtri[:], I["c_mtri"], writes=["mtri"])
        P.dma("pool", mchk[:], I["c_mchk"], writes=["mchk"])
        P.op("dve", lambda e: e.memset(ones[:], 1.0), writes=["ones"])
        P.op("dve", lambda e: e.memset(fK[64:128, :, :], 0.0), writes=["fKa"])
        P.op("dve", lambda e: e.memset(fK[64:68, :, :], 1.0), writes=["fKa"])
        P.op("dve", lambda e: e.memset(mK[64:128, :, :], 0.0), writes=["mKr"])
        P.op("dve", lambda e: e.memset(fQ[64:128, :, :], 0.0), writes=["fQa"])
        P.op("dve", lambda e: e.memset(fQ[64:68, :, :], 1.0), writes=["fQa"])
        for l in range(2):
            P.dma("sp", gT[:, l, :, :], I["g_norms"][l].rearrange("n (c p) -> p n c", p=128), writes=["gT"], slow=True)
            P.dma("sp", gqT[:, l, :], I["g_q_lora"][l:l + 1, :].rearrange("o (c p) -> p (o c)", p=128), writes=["gqT"], slow=True)
            P.dma("sp", gkvT[:, l, :], I["g_kv_lora"][l:l + 1, :].rearrange("o (c p) -> p (o c)", p=128), writes=["gkvT"], slow=True)
            P.dma("sp", cwT[:, l, :, :], I["conv_w"][l].rearrange("j (c p) -> p j c", p=128), writes=["cwT"], slow=True)
            P.dma("sp", fcwT[:, l, :, :], I["ffn_conv_w"][l].rearrange("j (c p) -> p j c", p=128), writes=["fcwT"], slow=True)
            P.dma("sp", nbf[:, l:l + 1], I["b_forget"][l:l + 1, :].rearrange("o h -> h o"), writes=["nbf"], slow=True)
            P.dma("sp", gkvB[:, l, :], I["g_kv_lora"][l:l + 1, :].broadcast_to([128, 256]), writes=["gkvB"])
            P.dma("sp", bfB[:, l, :], I["b_forget"][l:l + 1, :].broadcast_to([128, 8]), writes=["bfB"])
        ts(nbf[:], nbf[:], -1.0, None, ALU.mult, None, ["nbf"], ["nbf"])

        def norm_stats(src, nch, T, n, skeys):
            sq = tb[:, 0:nch, 0:T]
            act(sq, src, AF.Square, list(skeys), ["tb"])
            pt, pk = ps_next()
            mm(pt[:, 0:T], [(ones[:], tb[:, c, 0:T]) for c in range(nch)], ["ones", "tb"], [pk])
            act(rstd[:, 0:T], pt[:, 0:T], AF.Ln, [pk], ["rstd"], scale=1.0 / n, bias=small[:, 0:1])
            act(rstd[:, 0:T], rstd[:, 0:T], AF.Exp, ["rstd"], ["rstd"], scale=-0.5)

        P.op("dve", lambda e: e.memset(small[:, 0:1], EPS), writes=["small"])
        P.op("dve", lambda e: e.memset(small[:, 1:2], 1.0), writes=["small"])

        def prenorm(l, n_idx, T):
            norm_stats(xT[:, :, 0:T], 8, T, D, XT)
            for c in range(8):
                if c in POOLC:
                    ts(mT[:, c, 0:T], xT[:, c, 0:T], gT[:, l, n_idx, c:c + 1], None, ALU.mult, None,
                       ["xT", ("xTc", c), "gT", "mT", ("mTc", c)], [("mTc", c)], eng="pool")
                    tt(hT[:, c, 0:T], mT[:, c, 0:T], rstd[:, 0:T], ALU.mult, [("mTc", c), "mT", "rstd"], [("hT", c)], eng="pool")
                else:
                    stt(hT[:, c, 0:T], xT[:, c, 0:T], gT[:, l, n_idx, c:c + 1], rstd[:, 0:T], ALU.mult, ALU.mult,
                        ["xT", ("xTc", c), "rstd", "gT"], [("hT", c)])
        POOLC = ()
        FFN_ENG = "dve"
        HT = [("hT", c) for c in range(8)]
        XT = ["xT"] + [("xTc", c) for c in range(8)]

        def postnorm_residual(l, n_idx, T):
            norm_stats(mT[:, :, 0:T], 8, T, D, ["mT"])
            for c in range(8):
                en = "pool" if c in POOLC else "dve"
                tt(mT[:, c, 0:T], mT[:, c, 0:T], rstd[:, 0:T], ALU.mult, [("mTc", c), "mT", "rstd"], [("mTc", c)], eng=en)
                if en == "pool":
                    ts(mT[:, c, 0:T], mT[:, c, 0:T], gT[:, l, n_idx, c:c + 1], None, ALU.mult, None, [("mTc", c), "mT", "gT"], [("mTc", c)], eng=en)
                    tt(xT[:, c, 0:T], xT[:, c, 0:T], mT[:, c, 0:T], ALU.add, [("mTc", c), "mT", ("xTc", c), "xT"], [("xTc", c)], eng=en)
                else:
                    stt(xT[:, c, 0:T], mT[:, c, 0:T], gT[:, l, n_idx, c:c + 1], xT[:, c, 0:T], ALU.mult, ALU.add,
                        [("mTc", c), "mT", "gT", ("xTc", c), "xT"], [("xTc", c)], eng=en)

        def proj_fm(wsrc_fn, ncol_chunks, kc, rhs_fn, rkeys, T, consume, M=128, cw=256, p=128):
            per = cw // M
            j = 0
            while j < ncol_chunks:
                nb = min(per, ncol_chunks - j)
                wv, wk = wload(wsrc_fn(j * M, nb * M), kc, nb * M, p=p)
                for jj in range(nb):
                    pt, pk = ps_next()
                    mm(pt[0:M, 0:T], [(wv[0:p, k, jj * M:(jj + 1) * M], rhs_fn(k)) for k in range(kc)],
                       [wk] + rkeys, [pk])
                    consume(j + jj, pt, pk)
                j += nb

        def attention(Kst, Qt, Vst, vkey, krows, scale, mask, T, kblocks, hkeys, out_key):
            nblk = len(kblocks)
            LA = 3

            def s_block(h, bi):
                k0, ks, q0, diag = kblocks[bi]
                N = T - q0
                pt, pk = ps_next()
                mm(pt[0:ks, 0:N], [(Kst[0:128, h, k0:k0 + ks], Qt[0:128, h, q0:T])], hkeys, [pk], last=not diag)
                if diag:
                    dn = min(128, N)
                    mm(pt[0:ks, 0:dn], [(identb[0:ks, 0:ks], mask[0:ks, 0:dn])], ["identb", "mtri", "mchk"], [pk], first=False)
                return pt, pk
            units = [(h, bi) for h in range(8) for bi in range(nblk)]

            def finish_unit(h, bi, src_fn):
                k0, ks, q0, diag = kblocks[bi]
                N = T - q0
                (po, pok), (pz, pzk) = ps_acc(h)
                src, skey = src_fn(ks, N)
                kt, kp = k0 // 128, k0 % 128
                mm(po[0:128, q0:T], [(Vst[kp:kp + ks, kt, (h // 2) * 128:(h // 2 + 1) * 128], src)],
                   [skey, vkey], [pok], first=(bi == 0), last=(bi == nblk - 1))
                mm(pz[0:128, q0:T], [(ones[0:ks, 0:128], src)], [skey, "ones"], [pzk], first=(bi == 0), last=(bi == nblk - 1))
                if bi == nblk - 1:
                    i = rr("tA", 3)
                    r0 = (h % 2) * 64
                    P.op("dve", lambda e, i=i, pz=pz, r0=r0: e.reciprocal(tmpA[i][r0:r0 + 64, 0:T], pz[r0:r0 + 64, 0:T]), [pzk], [("tA", i)])
                    tt(oT[r0:r0 + 64, h // 2, 0:T], po[r0:r0 + 64, 0:T], tmpA[i][r0:r0 + 64, 0:T], ALU.mult, [pok, ("tA", i)], [out_key])

            if T == TP:
                def s_pair(p):
                    pt, pk = ps_next()
                    for j in (0, 1):
                        h, bi = units[2 * p + j]
                        k0, ks, q0, diag = kblocks[bi]
                        N = T - q0
                        reg = pt[:, j * 256:(j + 1) * 256]
                        mm(reg[0:ks, 0:N], [(Kst[0:128, h, k0:k0 + ks], Qt[0:128, h, q0:T])], hkeys, [pk], last=not diag)
                        if diag:
                            dn = min(128, N)
                            mm(reg[0:ks, 0:dn], [(identb[0:ks, 0:ks], mask[0:ks, 0:dn])], ["identb", "mtri", "mchk"], [pk], first=False)
                    return pt, pk
                npairs = len(units) // 2
                ppend = [s_pair(p) for p in range(min(2, npairs))]
                for p in range(npairs):
                    pt, pk = ppend.pop(0)
                    if p + 2 < npairs:
                        ppend.append(s_pair(p + 2))
                    i = rr("pT2", 3)
                    full = all(kblocks[units[2 * p + j][1]][1] == 128 and kblocks[units[2 * p + j][1]][2] == 0 for j in (0, 1))
                    if full:
                        act(pT2[i][:, :], pt[:, 0:512], AF.Exp, [pk], [("pT2", i)], scale=scale)
                    else:
                        for j in (0, 1):
                            _k0, ks_, q0_, _d = kblocks[units[2 * p + j][1]]
                            n_ = T - q0_
                            act(pT2[i][0:ks_, j * 256:j * 256 + n_], pt[0:ks_, j * 256:j * 256 + n_], AF.Exp, [pk], [("pT2", i)], scale=scale)
                    for j in (0, 1):
                        h, bi = units[2 * p + j]
                        finish_unit(h, bi, lambda ks, N, i=i, j=j: (pT2[i][0:ks, j * 256:j * 256 + N], ("pT2", i)))
                return
            pend = [s_block(*units[u]) for u in range(min(LA, len(units)))]
            for u, (h, bi) in enumerate(units):
                k0, ks, q0, diag = kblocks[bi]
                N = T - q0
                (po, pok), (pz, pzk) = ps_acc(h)
                pt, pk = pend.pop(0)
                if u + LA < len(units):
                    pend.append(s_block(*units[u + LA]))
                i = rr("pT", 3)
                act(pT[i][0:ks, 0:N], pt[0:ks, 0:N], AF.Exp, [pk], [("pT", i)], scale=scale)
                kt, kp = k0 // 128, k0 % 128
                mm(po[0:128, q0:T], [(Vst[kp:kp + ks, kt, (h // 2) * 128:(h // 2 + 1) * 128], pT[i][0:ks, 0:N])],
                   [("pT", i), vkey], [pok], first=(bi == 0), last=(bi == nblk - 1))
                mm(pz[0:128, q0:T], [(ones[0:ks, 0:128], pT[i][0:ks, 0:N])],
                   [("pT", i), "ones"], [pzk], first=(bi == 0), last=(bi == nblk - 1))
                if bi == nblk - 1:
                    i = rr("tA", 3)
                    r0 = (h % 2) * 64
                    P.op("dve", lambda e, i=i, pz=pz, r0=r0: e.reciprocal(tmpA[i][r0:r0 + 64, 0:T], pz[r0:r0 + 64, 0:T]), [pzk], [("tA", i)])
                    tt(oT[r0:r0 + 64, h // 2, 0:T], po[r0:r0 + 64, 0:T], tmpA[i][r0:r0 + 64, 0:T], ALU.mult, [pok, ("tA", i)], [out_key])

        def kblocks_for(kpos0, T):
            blks = []
            for kt in range(kpos0 // 128):
                blks.append((kt * 128, 128, 0, False))
            if T >= 128:
                for j in range(T // 128):
                    blks.append((kpos0 + j * 128, 128, j * 128, True))
            else:
                blks.append((kpos0, T, 0, True))
            return blks

        def out_dma(dst, src, reads):
            fin.append(P.dma("act", dst, src, reads=reads))

        def run_tile(l, T, pos0, xr0, first, last, grp, b, tile_i):
            subs = [(s * 128, min(128, T - s * 128)) for s in range((T + 127) // 128)]
            W = I["w_in"][l]
            if l == 0:
                src = I["x_prompt"][pos0:pos0 + T, :] if grp == "p" else I["x_sample"][b]
                for s, (r0, nr) in enumerate(subs):
                    P.dma("sp", memx[0:nr, 0, :], src[r0:r0 + nr, :], writes=[("memx", 0), ("memx", 1)])
                    for c in range(8):
                        pt, pk = ps_next()
                        P.op("pe", lambda e, pt=pt, c=c, nr=nr: e.transpose(pt[:, 0:nr], memx[0:nr, 0, c * 128:(c + 1) * 128], ident[0:nr, 0:nr]),
                             [("memx", 0), ("memx", 1)] + ["ident"], [pk])
                        acopy(xT[:, c, r0:r0 + nr], pt[:, 0:nr], [pk], ["xT"])
            else:
                P.dma("sp", xT[:, :, 0:T], xres[:, :, xr0:xr0 + T], reads=["xres"], writes=["xT"])
            P.dma("sp", ropeT[64:96, :, 0:T], I["c_ropeT"][:, :, pos0:pos0 + T].rearrange("a r t -> r a t"), writes=["ropeT"])
            for s, (r0, nr) in enumerate(subs):
                P.dma("sp", ropeK[0:nr, s, :, :], I["c_ropeK"][:, pos0 + r0:pos0 + r0 + nr, :].rearrange("a t r -> t a r"), writes=["ropeK"])

            prenorm(l, 0, T)
            rhs_h = lambda k: hT[:, k, 0:T]

            if first:
                if grp == "p":
                    P.op("dve", lambda e: e.memset(uT[:, :, 0:2], 0.0), writes=["uT"])
                else:
                    for jj in range(2):
                        P.dma("sp", uT[:, :, jj], I["state_conv"][l, b, jj:jj + 1, :].rearrange("o (c p) -> p (o c)", p=128), writes=["uT"], slow=True)
            ccs = {}

            def cons_cc(j, pt, pk):
                i = rr("tA", 3)
                acopy(tmpA[i][:, 0:T], pt[:, 0:T], [pk], [("tA", i)])
                ccs[j] = i
            for ch in range(4):
                proj_fm(lambda c0, cw, ch=ch: wrows(W, O_CC + ch * 128, 128), 1, 8, rhs_h, HT, T, lambda j, pt, pk, ch=ch: cons_cc(ch, pt, pk), cw=128)

                def cons_cx(j, pt, pk, ch=ch):
                    i = ccs[ch]
                    tt(uT[:, ch, 2:2 + T], pt[:, 0:T], tmpA[i][:, 0:T], ALU.mult, [pk, ("tA", i)], ["uT"])
                proj_fm(lambda c0, cw, ch=ch: wrows(W, O_CX + ch * 128, 128), 1, 8, rhs_h, HT, T, cons_cx, cw=128)
            if last:
                dst = O["conv_p"][l] if grp == "p" else O["conv_s"][l, b]
                for jj in range(2):
                    fin.append(P.dma("act", dst[jj:jj + 1, :].rearrange("o (c p) -> p (o c)", p=128), uT[:, :, T + jj], reads=["uT"], slow=True))
            convs = {}
            for ch in range(4):
                i = rr("tA", 3)
                ts(tmpA[i][:, 0:T], uT[:, ch, 0:T], cwT[:, l, 0, ch:ch + 1], None, ALU.mult, None, ["uT", "cwT"], [("tA", i)])
                stt(tmpA[i][:, 0:T], uT[:, ch, 1:1 + T], cwT[:, l, 1, ch:ch + 1], tmpA[i][:, 0:T], ALU.mult, ALU.add, ["uT", "cwT", ("tA", i)], [("tA", i)])
                stt(tmpA[i][:, 0:T], uT[:, ch, 2:2 + T], cwT[:, l, 2, ch:ch + 1], tmpA[i][:, 0:T], ALU.mult, ALU.add, ["uT", "cwT", ("tA", i)], [("tA", i)])

                def cons_cb(j, pt, pk, ch=ch, i=i):
                    tt(aT[:, ch, 0:T], pt[:, 0:T], tmpA[i][:, 0:T], ALU.mult, [pk, ("tA", i)], [("aT", ch)])
                proj_fm(lambda c0, cw, ch=ch: wrows(W, O_CB + ch * 128, 128), 1, 8, rhs_h, HT, T, cons_cb, cw=128)
            if not last:
                acopy(uT[:, :, 0:2], uT[:, :, T:T + 2], ["uT"], ["uT"])

            P.op("dve", lambda e: e.memset(fQ[64:128, :, :], 0.0), writes=["fQa"])
            P.op("dve", lambda e: e.memset(fQ[64:68, :, :], 1.0), writes=["fQa"])

            def cons_q(j, pt, pk):
                acopy(fQ[0:64, 2 * j, 0:T], pt[0:64, 0:T], [pk], ["fQ"])
                vcopy(fQ[0:64, 2 * j + 1, 0:T], pt[64:128, 0:T], [pk], ["fQ"])
            proj_fm(lambda c0, cw: wrows(W, O_FQ + c0, cw), 4, 8, rhs_h, HT, T, cons_q)

            def cons_k(j, pt, pk):
                acopy(fK[0:64, 2 * j, pos0:pos0 + T], pt[0:64, 0:T], [pk], ["fK"])
                vcopy(fK[0:64, 2 * j + 1, pos0:pos0 + T], pt[64:128, 0:T], [pk], ["fK"])
            proj_fm(lambda c0, cw: wrows(W, O_FK + c0, cw), 4, 8, rhs_h, HT, T, cons_k)
            wv, wk = wload(wrows(W, O_FF, 8), 8, 8)
            pt, pk = ps_next()
            mm(pt[0:8, 0:T], [(wv[:, k, 0:8], hT[:, k, 0:T]) for k in range(8)], [wk] + HT, [pk])
            act(lf[:, 0:T], pt[0:8, 0:T], AF.Exp, [pk, "nbf"], ["lf"], scale=-1.0, bias=nbf[:, l:l + 1])
            act(lf[:, 0:T], lf[:, 0:T], AF.Ln, ["lf"], ["lf"], bias=small[0:8, 1:2])
            ts(lf[:, 0:T], lf[:, 0:T], -1.0, None, ALU.mult, None, ["lf"], ["lf"])
            if first and grp == "p":
                P.op("dve", lambda e: e.memset(carry[:], 0.0), writes=["carry"])
            P.op("dve", lambda e: e.tensor_tensor_scan(cum[:, 0:T], ones_f[0:8, 0:T], lf[:, 0:T], carry[:, 0:1], ALU.mult, ALU.add),
                 ["lf", "carry", "ones_f"], ["cum"])
            cum_to_rows(T, pos0, True)
            okey = "fox_k_" + grp
            vkey = "fox_v_" + grp
            for (col0, dname, isv) in ((O_FK, okey, False), (O_FV, vkey, True)):
                for hf in range(2):
                    wv_, wk_ = wload(wrows(W, col0 + hf * 256, 256), 8, 256)
                    for s_, (r0, nr) in enumerate(subs):
                        pt, pk = ps_next()
                        mm(pt[0:nr, 0:256], [(hT[:, k, r0:r0 + nr], wv_[:, k, :]) for k in range(8)], [wk_] + HT, [pk])
                        i = rr("stg", 2)
                        acopy(stg[i][0:nr, 0:256], pt[0:nr, 0:256], [pk], [("stg", i)])
                        if isv:
                            kt = (pos0 + r0) // 128
                            vcopy(fV[0:nr, kt, hf * 256:(hf + 1) * 256], pt[0:nr, 0:256], [pk], ["fV"])
                        dst = O[dname][l, pos0 + r0:pos0 + r0 + nr, hf * 256:(hf + 1) * 256] if grp == "p" else O[dname][l, b, r0:r0 + nr, hf * 256:(hf + 1) * 256]
                        out_dma(dst, stg[i][0:nr, 0:256], [("stg", i)])
            kb = kblocks_for(pos0, T)
            attention(fK, fQ, fV, "fV", 68, 0.125, mtri, T, kb, ["fK", "fKa", "fQ", "fQa"], "oTf")
            wv, wk = wload(wrows(W, O_FF, 8), 8, 8)
            for s, (r0, nr) in enumerate(subs):
                pt, pk = ps_next()
                mm(pt[0:nr, 0:8], [(hT[:, k, r0:r0 + nr], wv[:, k, 0:8]) for k in range(8)], [wk] + HT, [pk])
                i = rr("stg", 2)
                tt(stg[i][0:nr, 0:8], pt[0:nr, 0:8], bfB[0:nr, l, :], ALU.add, [pk, "bfB"], [("stg", i)])
                act(stg[i][0:nr, 0:8], stg[i][0:nr, 0:8], AF.Exp, [("stg", i)], [("stg", i)], scale=-1.0)
                act(stg[i][0:nr, 0:8], stg[i][0:nr, 0:8], AF.Ln, [("stg", i)], [("stg", i)], bias=small[0:nr, 1:2])
                ts(stg[i][0:nr, 0:8], stg[i][0:nr, 0:8], -1.0, None, ALU.mult, None, [("stg", i)], [("stg", i)])
                dst = O["logf_p"][l, pos0 + r0:pos0 + r0 + nr, :] if grp == "p" else O["logf_s"][l, b, r0:r0 + nr, :]
                out_dma(dst, stg[i][0:nr, 0:8], [("stg", i)])

            def branch_out(wname, gate_off, rhs_fn, rkeys, kc, p, firstb):
                wout = I[wname][l]
                for c in range(8):
                    gi = [None]

                    def cons_g(j, pt, pk):
                        gi[0] = rr("tA", 3)
                        act(tmpA[gi[0]][:, 0:T], pt[:, 0:T], AF.Sigmoid, [pk], [("tA", gi[0])])
                    proj_fm(lambda c0, cw, c=c: wrows(W, gate_off + c * 128, 128), 1, 8, rhs_h, HT, T, cons_g, cw=128)

                    def cons_y(j, pt, pk, c=c):
                        g = gi[0]
                        if firstb:
                            tt(mT[:, c, 0:T], pt[:, 0:T], tmpA[g][:, 0:T], ALU.mult, [pk, ("tA", g)], ["mT"])
                        else:
                            tt(tmpA[g][:, 0:T], pt[:, 0:T], tmpA[g][:, 0:T], ALU.mult, [pk, ("tA", g)], [("tA", g)])
                            tt(mT[:, c, 0:T], mT[:, c, 0:T], tmpA[g][:, 0:T], ALU.add, ["mT", ("tA", g)], ["mT"])
                    proj_fm(lambda c0, cw, c=c: wout[:, c * 128:(c + 1) * 128].rearrange("(k p) c -> p k c", p=p), 1, kc, rhs_fn, rkeys, T, cons_y, cw=128, p=p)

            branch_out("w_conv_out", O_GA, lambda k: aT[:, k, 0:T], [("aT", c) for c in range(4)], 4, 128, True)
            branch_out("w_fox_out", O_GB, lambda k: oT[:, k, 0:T], ["oTf"], 4, 128, False)

            def cons_cq(j, pt, pk):
                acopy(cqT[:, j, 0:T], pt[:, 0:T], [pk], ["cqT"])
            proj_fm(lambda c0, cw: wrows(W, O_CQ + c0, cw), 3, 8, rhs_h, HT, T, cons_cq)
            norm_stats(cqT[:, :, 0:T], 3, T, 384, ["cqT"])
            for c in range(3):
                stt(cqn[:, c, 0:T], cqT[:, c, 0:T], gqT[:, l, c:c + 1], rstd[:, 0:T], ALU.mult, ALU.mult, ["cqT", "rstd", "gqT"], ["cqn"])
            wq4 = I["w_uq"][l].rearrange("(k p) (h e) -> p k h e", p=128, e=96)
            for h0 in (0, 4):
                wa, wak = wload(wrows(I["w_uq"][l], h0 * 96, 384), 3, 384)
                def build_sw(view, key, h0=h0):
                    P.dma("pool", view, wrows(I["w_uq"][l], h0 * 96, 384), writes=[key])
                    v4 = view.rearrange("p k (h e) -> p k h e", e=96)
                    for k3 in range(3):
                        P.dma("pool", v4[:, k3, :, 64:80], wq4[:, k3, h0:h0 + 4, 80:96], writes=[key])
                        P.dma("pool", v4[:, k3, :, 80:96], wq4[:, k3, h0:h0 + 4, 64:80], writes=[key])
                wb_, wbk = wcached(("uq_sw", l, h0), 3, 384, 128, build_sw)
                for hh in range(4):
                    h = h0 + hh
                    pa, pak = ps_next()
                    mm(pa[0:96, 0:T], [(wa[:, k, hh * 96:(hh + 1) * 96], cqn[:, k, 0:T]) for k in range(3)], [wak, "cqn"], [pak])
                    pb, pbk = ps_next()
                    mm(pb[0:96, 0:T], [(wb_[:, k, hh * 96:(hh + 1) * 96], cqn[:, k, 0:T]) for k in range(3)], [wbk, "cqn"], [pbk])
                    acopy(fQ[0:64, h, 0:T], pa[0:64, 0:T], [pak], ["fQ"])
                    i1, i2 = rr("tA", 3), rr("tA", 3)
                    tt(tmpA[i1][64:96, 0:T], pa[64:96, 0:T], ropeT[64:96, 0, 0:T], ALU.mult, [pak, "ropeT"], [("tA", i1)])
                    tt(tmpA[i2][64:96, 0:T], pb[64:96, 0:T], ropeT[64:96, 1, 0:T], ALU.mult, [pbk, "ropeT"], [("tA", i2)])
                    tt(fQ[64:96, h, 0:T], tmpA[i1][64:96, 0:T], tmpA[i2][64:96, 0:T], ALU.add, [("tA", i1), ("tA", i2)], ["fQa"])
            def cons_ckv(j, pt, pk):
                acopy(ckT[:, j, 0:T], pt[:, 0:T], [pk], ["ckT"])
            proj_fm(lambda c0, cw: wrows(W, O_CKV + c0, cw), 2, 8, rhs_h, HT, T, cons_ckv)
            norm_stats(ckT[:, :, 0:T], 2, T, 256, ["ckT"])
            for c in range(2):
                stt(ckn[:, c, 0:T], ckT[:, c, 0:T], gkvT[:, l, c:c + 1], rstd[:, 0:T], ALU.mult, ALU.mult, ["ckT", "rstd", "gkvT"], ["ckn"])
            wkr, wkrk = wload(wrows(W, O_KR - 64, 96), 8, 96)
            def build_ks(view, key):
                P.dma("pool", view, wrows(W, O_KR - 64, 96), writes=[key])
                P.dma("pool", view[:, :, 64:80], wrows(W, O_KR + 16, 16), writes=[key])
                P.dma("pool", view[:, :, 80:96], wrows(W, O_KR, 16), writes=[key])
            wks, wksk = wcached(("kr_sw", l), 8, 96, 128, build_ks)
            pa, pak = ps_next()
            mm(pa[0:96, 0:T], [(wkr[:, k, :], hT[:, k, 0:T]) for k in range(8)], [wkrk] + HT, [pak])
            pb, pbk = ps_next()
            mm(pb[0:96, 0:T], [(wks[:, k, :], hT[:, k, 0:T]) for k in range(8)], [wksk] + HT, [pbk])
            i1, i2 = rr("tA", 3), rr("tA", 3)
            tt(tmpA[i1][64:96, 0:T], pa[64:96, 0:T], ropeT[64:96, 0, 0:T], ALU.mult, [pak, "ropeT"], [("tA", i1)])
            tt(tmpA[i2][64:96, 0:T], pb[64:96, 0:T], ropeT[64:96, 1, 0:T], ALU.mult, [pbk, "ropeT"], [("tA", i2)])
            tt(krT[64:96, 0:T], tmpA[i1][64:96, 0:T], tmpA[i2][64:96, 0:T], ALU.add, [("tA", i1), ("tA", i2)], ["krT"])
            mla_kv(l, pos0, T)
            attention(mK, fQ, mV, "mV", 96, 96.0 ** -0.5, mchk, T, kb, ["mK", "mKr", "fQ", "fQa"], "oTf")
            wck, wckk = wload(wrows(W, O_CKV, 256), 8, 256)
            wkr, wkrk = wload(wrows(W, O_KR - 64, 96), 8, 96)
            for s, (r0, nr) in enumerate(subs):
                pt, pk = ps_next()
                mm(pt[0:nr, 0:256], [(hT[:, k, r0:r0 + nr], wck[:, k, :]) for k in range(8)], [wckk] + HT, [pk])
                i = rr("stg", 2)
                act(stg[i][0:nr, 256:512], pt[0:nr, 0:256], AF.Square, [pk], [("stg", i), "small2"], accum_out=small[0:nr, 2:3])
                act(small[0:nr, 3:4], small[0:nr, 2:3], AF.Sqrt, ["small2"], ["small3"], scale=1.0 / 256, bias=small[0:nr, 0:1])
                P.op("dve", lambda e, nr=nr: e.reciprocal(small[0:nr, 3:4], small[0:nr, 3:4]), ["small3"], ["small3"])
                stt(stg[i][0:nr, 0:256], pt[0:nr, 0:256], small[0:nr, 3:4], gkvB[0:nr, l, :], ALU.mult, ALU.mult, [pk, "small3", "gkvB", ("stg", i)], [("stg", i)])
                dst = O["ckv_p"][l, pos0 + r0:pos0 + r0 + nr, :] if grp == "p" else O["ckv_s"][l, b, r0:r0 + nr, :]
                out_dma(dst, stg[i][0:nr, 0:256], [("stg", i)])
                pt, pk = ps_next()
                mm(pt[0:nr, 0:32], [(hT[:, k, r0:r0 + nr], wkr[:, k, 64:96]) for k in range(8)], [wkrk] + HT, [pk])
                i = rr("stg", 2)
                cs, sn = ropeK[0:nr, s, 0, :], ropeK[0:nr, s, 1, :]
                tt(stg[i][0:nr, 0:16], pt[0:nr, 0:16], cs, ALU.mult, [pk, "ropeK"], [("stg", i)])
                tt(stg[i][0:nr, 32:48], pt[0:nr, 16:32], sn, ALU.mult, [pk, "ropeK"], [("stg", i)])
                tt(stg[i][0:nr, 0:16], stg[i][0:nr, 0:16], stg[i][0:nr, 32:48], ALU.subtract, [("stg", i)], [("stg", i)])
                tt(stg[i][0:nr, 16:32], pt[0:nr, 16:32], cs, ALU.mult, [pk, "ropeK"], [("stg", i)])
                tt(stg[i][0:nr, 32:48], pt[0:nr, 0:16], sn, ALU.mult, [pk, "ropeK"], [("stg", i)])
                tt(stg[i][0:nr, 16:32], stg[i][0:nr, 16:32], stg[i][0:nr, 32:48], ALU.add, [("stg", i)], [("stg", i)])
                dst = O["kr_p"][l, pos0 + r0:pos0 + r0 + nr, :] if grp == "p" else O["kr_s"][l, b, r0:r0 + nr, :]
                out_dma(dst, stg[i][0:nr, 0:32], [("stg", i)])
            branch_out("w_mla_out", O_GC, lambda k: oT[:, k, 0:T], ["oTf"], 4, 128, False)

            vcopy(tb[:, :, 0:T], mT[:, :, 0:T], ["mT"], ["tb"])

            def cons_m(j, pt, pk):
                (vcopy if j % 2 else acopy)(mT[:, j, 0:T], pt[:, 0:T], [pk], ["mT"])
            proj_fm(lambda c0, cw: wrows(I["w_mix_out"][l], c0, cw), 8, 8, lambda k: tb[:, k, 0:T], ["tb"], T, cons_m)
            postnorm_residual(l, 1, T)

            prenorm(l, 2, T)

            def cons_caq(j, pt, pk):
                (vcopy if j % 2 else acopy)(fQ[:, j, 0:T], pt[:, 0:T], [pk], ["fQ", "fQa"])
            proj_fm(lambda c0, cw: wrows(I["w_ca_q"][l], c0, cw), 4, 8, rhs_h, HT, T, cons_caq)
            def ca_s(h, mt):
                pt, pk = ps_next()
                mm(pt[:, 0:T], [(memKT[:, h, mt * 128:(mt + 1) * 128], fQ[:, h, 0:T])], ["memKT", "fQ", "fQa"], [pk])
                return pt, pk
            cunits = [(h, mt) for h in range(4) for mt in range(2)]
            cpend = [ca_s(*cunits[u]) for u in range(3)]
            for u, (h, mt) in enumerate(cunits):
                (po, pok), (pz, pzk) = ps_acc(h)
                pt, pk = cpend.pop(0)
                if u + 3 < len(cunits):
                    cpend.append(ca_s(*cunits[u + 3]))
                i = rr("pT", 3)
                act(pT[i][:, 0:T], pt[:, 0:T], AF.Exp, [pk], [("pT", i)], scale=128.0 ** -0.5)
                mm(po[:, 0:T], [(memV[:, mt, h * 128:(h + 1) * 128], pT[i][:, 0:T])], [("pT", i), "memV"], [pok], first=(mt == 0), last=(mt == 1))
                mm(pz[:, 0:T], [(ones[:, :], pT[i][:, 0:T])], [("pT", i), "ones"], [pzk], first=(mt == 0), last=(mt == 1))
                if mt == 1:
                    i = rr("tA", 3)
                    P.op("dve", lambda e, i=i, pz=pz: e.reciprocal(tmpA[i][:, 0:T], pz[:, 0:T]), [pzk], [("tA", i)])
                    tt(aT[:, h, 0:T], po[:, 0:T], tmpA[i][:, 0:T], ALU.mult, [pok, ("tA", i)], [("aT", h)])
            proj_fm(lambda c0, cw: wrows(I["w_ca_o"][l], c0, cw), 8, 4, lambda k: aT[:, k, 0:T], [("aT", c) for c in range(4)], T, cons_m)
            postnorm_residual(l, 3, T)

            prenorm(l, 4, T)
            if first:
                if grp == "p":
                    P.op("dve", lambda e: e.memset(fhalo[:], 0.0), writes=["fhalo"])
                else:
                    for jj in range(2):
                        P.dma("sp", fhalo[:, :, jj], I["state_ffn_conv"][l, b, jj:jj + 1, :].rearrange("o (c p) -> p (o c)", p=128), writes=["fhalo"], slow=True)
            Wup = I["w_up"][l]
            for half in range(2):
                for j in range(11):
                    ja = half * 11 + j
                    res = []
                    for u_i, idx in enumerate((ja, 22 + ja)):
                        uf = uF[u_i]
                        ukey = ("uF", u_i)
                        acopy(uf[:, 0:2], fhalo[:, idx, :], ["fhalo", ("fhalo", u_i)], [ukey])

                        i = rr("tA", 3)

                        def cons_u(jj, pt, pk, uf=uf, ukey=ukey, i=i, idx=idx):
                            acopy(uf[:, 2:2 + T], pt[:, 0:T], [pk], [ukey])
                            act(tmpA[i][:, 0:T], pt[:, 0:T], AF.Copy, [pk, "fcwT"], [("tA", i)], scale=fcwT[:, l, 2, idx:idx + 1])
                        proj_fm(lambda c0, cw, idx=idx: wrows(Wup, idx * 128, 128), 1, 8, rhs_h, HT, T, cons_u, cw=128)
                        acopy(fhalo[:, idx, :], uf[:, T:T + 2], [ukey], [("fhalo", u_i)])
                        stt(tmpA[i][:, 0:T], uf[:, 1:1 + T], fcwT[:, l, 1, idx:idx + 1], tmpA[i][:, 0:T], ALU.mult, ALU.add, [ukey, "fcwT", ("tA", i)], [("tA", i)])
                        stt(tmpA[i][:, 0:T], uf[:, 0:T], fcwT[:, l, 0, idx:idx + 1], tmpA[i][:, 0:T], ALU.mult, ALU.add, [ukey, "fcwT", ("tA", i)], [("tA", i)])
                        res.append(i)
                    act(tmpA[res[0]][:, 0:T], tmpA[res[0]][:, 0:T], AF.Gelu_apprx_tanh, [("tA", res[0])], [("tA", res[0])])
                    tt(actT[:, j, 0:T], tmpA[res[0]][:, 0:T], tmpA[res[1]][:, 0:T], ALU.mult, [("tA", res[0]), ("tA", res[1])], ["actT"], eng=FFN_ENG)
                Wd = I["w_down"][l]

                def cons_d(jj, pt, pk, half=half):
                    if half == 0:
                        acopy(mT[:, jj, 0:T], pt[:, 0:T], [pk], ["mT"])
                    else:
                        tt(mT[:, jj, 0:T], mT[:, jj, 0:T], pt[:, 0:T], ALU.add, [pk, "mT"], ["mT"])
                proj_fm(lambda c0, cw, half=half: Wd[half * 1408:(half + 1) * 1408, c0:c0 + cw].rearrange("(k p) c -> p k c", p=128),
                        8, 11, lambda k: actT[:, k, 0:T], ["actT"], T, cons_d, cw=128)
            if last:
                dst = O["ffn_p"][l] if grp == "p" else O["ffn_s"][l, b]
                for jj in range(2):
                    fin.append(P.dma("act", dst[jj:jj + 1, :].rearrange("o (c p) -> p (o c)", p=128), fhalo[:, :, jj], reads=["fhalo", ("fhalo", 0), ("fhalo", 1)], slow=True))
            postnorm_residual(l, 5, T)

            if l == 0:
                P.dma("act", xres[:, :, xr0:xr0 + T], xT[:, :, 0:T], reads=XT, writes=["xres"])
            else:
                for s, (r0, nr) in enumerate(subs):
                    for hf in range(2):
                        pt, pk = ps_next(full=True)
                        for cc in range(4):
                            c = hf * 4 + cc
                            P.op("pe", lambda e, pt=pt, c=c, cc=cc, r0=r0, nr=nr: e.transpose(pt[0:nr, cc * 128:(cc + 1) * 128], xT[:, c, r0:r0 + nr], ident[:, :]),
                                 XT + ["ident"], [pk])
                        i = rr("stg", 2)
                        acopy(stg[i][0:nr, 0:512], pt[0:nr, 0:512], [pk], [("stg", i)])
                        dst = O["y_p"][pos0 + r0:pos0 + r0 + nr, hf * 512:(hf + 1) * 512] if grp == "p" else O["y_s"][b, r0:r0 + nr, hf * 512:(hf + 1) * 512]
                        out_dma(dst, stg[i][0:nr, 0:512], [("stg", i)])

        def mla_kv(l, pos0, T):
            for h in range(8):
                vcopy(mK[64:96, h, pos0:pos0 + T], krT[64:96, 0:T], ["krT"], ["mKr"])
            wuk, wukk = wload(wrows(I["w_uk"][l], 0, 512), 2, 512)
            for hp in range(4):
                pt, pk = ps_next()
                mm(pt[0:128, 0:T], [(wuk[:, k, hp * 128:(hp + 1) * 128], ckn[:, k, 0:T]) for k in range(2)], [wukk, "ckn"], [pk])
                acopy(mK[0:64, 2 * hp, pos0:pos0 + T], pt[0:64, 0:T], [pk], ["mK"])
                vcopy(mK[0:64, 2 * hp + 1, pos0:pos0 + T], pt[64:128, 0:T], [pk], ["mK"])
            wuv, wuvk = wload(wrows(I["w_uv"][l], 0, 512), 2, 512)
            for s in range((T + 127) // 128):
                r0 = s * 128
                nr = min(128, T - r0)
                pt, pk = ps_next(full=True)
                mm(pt[0:nr, 0:512], [(ckn[:, k, r0:r0 + nr], wuv[:, k, :]) for k in range(2)], [wuvk, "ckn"], [pk])
                vcopy(mV[0:nr, (pos0 + r0) // 128, :], pt[0:nr, 0:512], [pk], ["mV"])

        def prep_mem_prompt(l):
            P.dma("sp", gmemT[:, l, :], I["g_mem"][l:l + 1, :].rearrange("o (c p) -> p (o c)", p=128), writes=["gmemT"], slow=True)
            for mt in range(2):
                P.dma("sp", memx[:, 0, :], I["mem_prompt"][mt * 128:(mt + 1) * 128, :], writes=[("memx", 0), ("memx", 1)])
                i = rr("stg", 2)
                act(stg[i][:, 0:512], memx[:, 0, 0:512], AF.Square, [("memx", 0), ("memx", 1)], [("stg", i), "small4"], accum_out=small[:, 4:5])
                act(stg[i][:, 0:512], memx[:, 0, 512:1024], AF.Square, [("memx", 0), ("memx", 1)], [("stg", i), "small5"], accum_out=small[:, 5:6])
                tt(small[:, 4:5], small[:, 4:5], small[:, 5:6], ALU.add, ["small4", "small5"], ["small4"])
                act(small[:, 6:7], small[:, 4:5], AF.Sqrt, ["small4"], ["small6"], scale=1.0 / D, bias=small[:, 0:1])
                P.op("dve", lambda e: e.reciprocal(small[:, 6:7], small[:, 6:7]), ["small6"], ["small6"])
                ts(memx[:, 0, :], memx[:, 0, :], small[:, 6:7], None, ALU.mult, None, [("memx", 0), ("memx", 1)] + ["small6"], [("memx", 0), ("memx", 1)])
                for c in range(8):
                    pt, pk = ps_next()
                    P.op("pe", lambda e, pt=pt, c=c: e.transpose(pt[:, 0:128], memx[:, 0, c * 128:(c + 1) * 128], ident[:, :]), [("memx", 0), ("memx", 1)] + ["ident"], [pk])
                    ts(hT[:, c, mt * 128:(mt + 1) * 128], pt[:, 0:128], gmemT[:, l, c:c + 1], None, ALU.mult, None, [pk, "gmemT"], [("hT", c)])

            def cons_mk(j, pt, pk):
                acopy(memKT[:, j, :], pt[:, 0:256], [pk], ["memKT"])
            proj_fm(lambda c0, cw: wrows(I["w_ca_k"][l], c0, cw), 4, 8, lambda k: hT[:, k, 0:256], HT, 256, cons_mk)
            for (wn, on, isv) in (("w_ca_k", "mem_k_p", False), ("w_ca_v", "mem_v_p", True)):
                for hf in range(2):
                    wv_, wk_ = wload(wrows(I[wn][l], hf * 256, 256), 8, 256)
                    for mt in range(2):
                        pt, pk = ps_next()
                        mm(pt[:, 0:256], [(hT[:, k, mt * 128:(mt + 1) * 128], wv_[:, k, :]) for k in range(8)], [wk_] + HT, [pk])
                        i = rr("stg", 2)
                        acopy(stg[i][:, 0:256], pt[:, 0:256], [pk], [("stg", i)])
                        if isv:
                            vcopy(memV[:, mt, hf * 256:(hf + 1) * 256], pt[:, 0:256], [pk], ["memV"])
                        out_dma(O[on][l, mt * 128:(mt + 1) * 128, hf * 256:(hf + 1) * 256], stg[i][:, 0:256], [("stg", i)])

        def prep_mem_sample(l, b):
            for mt in range(2):
                mo = mt * 512
                P.dma("sp", memx[:, 0, mo:mo + 512], I["cache_mem_k"][l, b, mt * 128:(mt + 1) * 128, :], writes=[("memx", mt)])
                pt, pk = ps_next(full=True)
                for h in range(4):
                    P.op("pe", lambda e, pt=pt, h=h, mo=mo: e.transpose(pt[:, h * 128:(h + 1) * 128], memx[:, 0, mo + h * 128:mo + (h + 1) * 128], ident[:, :]), [("memx", mt), "ident"], [pk])
                acopy(memKT[:, :, mt * 128:(mt + 1) * 128], pt[:, 0:512].rearrange("p (h m) -> p h m", h=4), [pk], ["memKT"])
            P.dma("pool", memV[:, :, :], I["cache_mem_v"][l, b].rearrange("(t p) c -> p t c", p=128), writes=["memV"])

        def prep_cache(l, b):
            P.dma("pool", fV[:, 0:16, :], I["cache_fox_v"][l, b].rearrange("(t p) c -> p t c", p=128), writes=["fV"])
            for kt in range(16):
                mj = kt % 2
                mo = mj * 512
                P.dma("sp", memx[:, 0, mo:mo + 512], I["cache_fox_k"][l, b, kt * 128:(kt + 1) * 128, :], writes=[("memx", mj)])
                for hg in range(2):
                    pt, pk = ps_next(full=True)
                    for hh in range(4):
                        h = hg * 4 + hh
                        P.op("pe", lambda e, pt=pt, h=h, hh=hh, mo=mo: e.transpose(pt[0:64, hh * 128:(hh + 1) * 128], memx[:, 0, mo + h * 64:mo + (h + 1) * 64], ident[:, :]), [("memx", mj), "ident"], [pk])
                    acopy(fK[0:64, hg * 4:(hg + 1) * 4, kt * 128:(kt + 1) * 128], pt[0:64, 0:512].rearrange("p (h m) -> p h m", h=4), [pk], ["fK"])
            P.op("dve", lambda e: e.memset(carry[:], 0.0), writes=["carry"])
            for pc in range(PAST // TP):
                P.dma("sp", lf[:, 0:TP], I["cache_fox_logf"][l, b, pc * TP:(pc + 1) * TP, :].rearrange("t h -> h t"), writes=["lf"], slow=True)
                P.op("dve", lambda e: e.tensor_tensor_scan(cum[:, 0:TP], ones_f[0:8, 0:TP], lf[:, 0:TP], carry[:, 0:1], ALU.mult, ALU.add),
                     ["lf", "carry", "ones_f"], ["cum"])
                cum_to_rows(TP, pc * TP, False)
            for pc in range(PAST // TP):
                for s in range(TP // 128):
                    kt = pc * (TP // 128) + s
                    mj = kt % 2
                    mo = mj * 512
                    P.dma("sp", memx[:, 0, mo:mo + 256], I["cache_mla_ckv"][l, b, kt * 128:(kt + 1) * 128, :], writes=[("memx", mj)])
                    P.dma("sp", memx[:, 0, mo + 320:mo + 352], I["cache_mla_kr"][l, b, kt * 128:(kt + 1) * 128, :], writes=[("memx", mj)])
                    pt, pk = ps_next()
                    for c in range(2):
                        P.op("pe", lambda e, pt=pt, c=c, mo=mo: e.transpose(pt[:, c * 128:(c + 1) * 128], memx[:, 0, mo + c * 128:mo + (c + 1) * 128], ident[:, :]), [("memx", mj), "ident"], [pk])
                    acopy(ckn[:, :, s * 128:(s + 1) * 128], pt[:, 0:256].rearrange("p (c m) -> p c m", c=2), [pk], ["ckn"])
                    pt, pk = ps_next()
                    P.op("pe", lambda e, pt=pt, mo=mo: e.transpose(pt[0:96, 0:128], memx[:, 0, mo + 256:mo + 352], ident[:, :]), [("memx", mj), "ident"], [pk])
                    acopy(krT[64:96, s * 128:(s + 1) * 128], pt[64:96, 0:128], [pk], ["krT"])
                mla_kv(l, pc * TP, TP)

        def cum_to_rows(T, pos0, is_q):
            ts(c8[:, 0:T], cum[:, 0:T], 8.0, None, ALU.mult, None, ["cum"], ["c8"])
            vcopy(chi[:, 0:T], c8[:, 0:T], ["c8"], ["chi"])
            tt(clo[:, 0:T], c8[:, 0:T], chi[:, 0:T], ALU.subtract, ["c8", "chi"], ["clo"])
            if is_q:
                P.dma("act", fQ[64:65, :, 0:T], chi[:, 0:T], reads=["chi"], writes=["fQa"])
                P.dma("act", fQ[65:66, :, 0:T], clo[:, 0:T], reads=["clo"], writes=["fQa"])
            vcopy(carry[:, 0:1], cum[:, T - 1:T], ["cum"], ["carry"])
            ts(chi[:, 0:T], chi[:, 0:T], -1.0, None, ALU.mult, None, ["chi", "fQa"], ["chi"])
            ts(clo[:, 0:T], clo[:, 0:T], -1.0, None, ALU.mult, None, ["clo", "fQa"], ["clo"])
            P.dma("act", fK[66:67, :, pos0:pos0 + T], chi[:, 0:T], reads=["chi"], writes=["fKa"])
            P.dma("act", fK[67:68, :, pos0:pos0 + T], clo[:, 0:T], reads=["clo"], writes=["fKa"])

        ones_f = sb("ones_f", (8, TP))
        P.op("dve", lambda e: e.memset(ones_f[:], 1.0), writes=["ones_f"])

        NTP = SEQ // TP

        class _Dummy:
            def op(self, *a, **k):
                return None

            def dma(self, *a, **k):
                return None

        def convert_ahead(l):
            nonlocal P
            realP, saved_ctr, nfin = P, dict(ctr), len(fin)
            COLLECT[0] = []
            P = _Dummy()
            try:
                run_tile(l, TP, 0, 0, True, False, "p", 0, 0)
            finally:
                P = realP
                blocks, COLLECT[0] = COLLECT[0], None
                ctr.clear()
                ctr.update(saved_ctr)
                del fin[nfin:]
            for (bkey, kc, cw, p, build_fn) in blocks:
                if bkey in wscratch:
                    continue
                wscratch[bkey] = nc.dram_tensor("wsc%d" % len(wscratch), [p, kc * cw], BF16, kind="Internal").ap()
                build_fn(wscratch[bkey].rearrange("p (k c) -> p k c", k=kc), ("wsc", bkey))

        convert_ahead(0)
        for l in range(2):
            prep_mem_prompt(l)
            for i in range(NTP):
                run_tile(l, TP, i * TP, i * TP, i == 0, i == NTP - 1, "p", 0, i)
                if l == 0 and i == 0:
                    convert_ahead(1)
            for b in range(2):
                prep_mem_sample(l, b)
                prep_cache(l, b)
                run_tile(l, TS, PAST, SEQ + b * TS, True, True, "s", b, 0)
        P.run(fin)
    return nc


_CACHE = {}


def _consts():
    ident = np.eye(128, dtype=np.float32)
    k = np.arange(128)[:, None]
    q = np.arange(128)[None, :]
    mtri = np.where(k <= q, 0.0, NEG).astype(np.float32)
    mchk = np.where((k // 64) <= (q // 64), 0.0, NEG).astype(np.float32)
    pos = np.arange(KTOT, dtype=np.float32)
    inv = (10000.0 ** (-np.arange(16, dtype=np.float32) / 16)).astype(np.float32)
    ang = pos[:, None] * inv[None, :]
    cos, sin = np.cos(ang).astype(np.float32), np.sin(ang).astype(np.float32)
    ropeT = np.stack([np.concatenate([cos, cos], 1).T, np.concatenate([-sin, sin], 1).T], 0)
    ropeK = np.stack([cos, sin], 0)
    return dict(c_ident=ident, c_mtri=mtri, c_mchk=mchk, c_ropeT=np.ascontiguousarray(ropeT, dtype=np.float32),
                c_ropeK=np.ascontiguousarray(ropeK, dtype=np.float32))


def kernel(**inputs):
    inp = {k: np.asarray(v, dtype=np.float32) for k, v in inputs.items()}
    if "nc" not in _CACHE:
        _CACHE["nc"] = build_program()
    nc = _CACHE["nc"]
    consts = _consts()
    L = 2
    shared = {}
    for k in ("w_in", "b_forget", "conv_w", "g_q_lora", "g_kv_lora", "w_conv_out", "w_fox_out", "w_mla_out", "w_mix_out",
              "g_mem", "w_up", "ffn_conv_w", "w_down", "g_norms"):
        shared[k] = inp[k]
    for k in ("w_uq", "w_uk", "w_uv", "w_ca_q", "w_ca_k", "w_ca_v"):
        a = inp[k]
        shared[k] = a.reshape(a.shape[0], a.shape[1], -1)
    shared["w_ca_o"] = inp["w_ca_o"].reshape(L, 512, D)
    shared.update(consts)
    in_maps = []
    for c in range(NCORE):
        m = dict(shared)
        m["x_prompt"] = inp["x_prompt"][c]
        m["x_sample"] = inp["x_sample"][2 * c:2 * c + 2]
        m["mem_prompt"] = inp["mem_prompt"][c]
        for k in ("cache_fox_k", "cache_fox_v", "cache_mem_k", "cache_mem_v"):
            a = inp[k][:, 2 * c:2 * c + 2]
            m[k] = np.ascontiguousarray(a.reshape(a.shape[0], 2, a.shape[2], -1))
        for k in ("cache_fox_logf", "cache_mla_ckv", "cache_mla_kr", "state_conv", "state_ffn_conv"):
            m[k] = np.ascontiguousarray(inp[k][:, 2 * c:2 * c + 2])
        in_maps.append(m)
    res = run_bass_kernel_spmd(nc, in_maps, core_ids=list(range(NCORE)))
    R = res.results
    _CACHE["last"] = R

    def cat(name, axis, shape=None):
        a = np.stack([np.asarray(R[c][name]) for c in range(NCORE)], axis=axis)
        return a

    y_p = cat("y_p", 0)
    y_s = np.concatenate([R[c]["y_s"] for c in range(NCORE)], 0)
    outs = [y_p, y_s]
    outs.append(cat("fox_k_p", 1).reshape(L, 8, SEQ, 8, 64))
    outs.append(cat("fox_v_p", 1).reshape(L, 8, SEQ, 8, 64))
    outs.append(cat("logf_p", 1))
    outs.append(cat("ckv_p", 1))
    outs.append(cat("kr_p", 1))
    outs.append(cat("conv_p", 1))
    outs.append(cat("ffn_p", 1))
    outs.append(cat("mem_k_p", 1).reshape(L, 8, 256, 4, 128))
    outs.append(cat("mem_v_p", 1).reshape(L, 8, 256, 4, 128))

    def cats(name):
        return np.concatenate([R[c][name] for c in range(NCORE)], 1)
    outs.append(cats("fox_k_s").reshape(L, 16, TS, 8, 64))
    outs.append(cats("fox_v_s").reshape(L, 16, TS, 8, 64))
    outs.append(cats("logf_s"))
    outs.append(cats("ckv_s"))
    outs.append(cats("kr_s"))
    outs.append(cats("conv_s"))
    outs.append(cats("ffn_s"))
    return tuple(np.ascontiguousarray(o, dtype=np.float32) for o in outs)
```

```python
from contextlib import ExitStack
import numpy as np
import concourse.bass as bass
import concourse.mybir as mybir
from concourse.bass_utils import run_bass_kernel_spmd

F32 = mybir.dt.float32
BF16 = mybir.dt.bfloat16
AF = mybir.ActivationFunctionType
ALU = mybir.AluOpType

N_DMA_SEM = 4
D = 1024
NCORE = 8
SEQ = 2048
PAST = 2048
TS = 64
DIN = 6824
DFF = 2816
TP = 256
KTOT = PAST + TS
EPS = 1e-6
NEG = -30000.0
O_CB, O_CC, O_CX, O_FQ, O_FK, O_FV, O_FF, O_CQ, O_CKV, O_KR, O_GA, O_GB, O_GC = (
    0, 512, 1024, 1536, 2048, 2560, 3072, 3080, 3464, 3720, 3752, 4776, 5800)


class Op:
    __slots__ = ("eng", "fn", "waits", "need_inc", "sem", "val", "dma", "idx")

    def __init__(self, eng, fn, dma):
        self.eng = eng
        self.fn = fn
        self.waits = []
        self.need_inc = dma
        self.sem = None
        self.val = None
        self.dma = dma
        self.idx = 0


class Prog:
    ENGS = ("pe", "act", "dve", "pool", "sp")

    def __init__(self, nc, stack):
        self.nc = nc
        self.ops = {e: [] for e in self.ENGS}
        self.last_w = {}
        self.readers = {}
        self.sems = {e: stack.enter_context(nc.semaphore("s_" + e)) for e in self.ENGS}
        self.dsems = {e: [stack.enter_context(nc.semaphore("d_%s%d" % (e, i))) for i in range(N_DMA_SEM)]
                      for e in ("sp", "pool", "act")}
        self.n_dma = {e: 0 for e in ("sp", "pool", "act")}

    def op(self, eng, fn, reads=(), writes=(), dma=False):
        o = Op(eng, fn, dma)
        deps = []
        for k in reads:
            w = self.last_w.get(k)
            if w is not None:
                deps.append(w)
        for k in writes:
            w = self.last_w.get(k)
            if w is not None:
                deps.append(w)
            deps.extend(self.readers.get(k, ()))
        seen = set()
        for d in deps:
            if id(d) in seen:
                continue
            seen.add(id(d))
            if d.eng == eng and eng == "pe" and not d.dma:
                continue
            d.need_inc = True
            o.waits.append(d)
        for k in writes:
            self.last_w[k] = o
            self.readers[k] = []
        for k in reads:
            self.readers.setdefault(k, []).append(o)
        self.ops[eng].append(o)
        return o

    def dma(self, eng, out, in_, reads=(), writes=(), slow=False):
        if slow:
            return self.op(eng, lambda e: e.dma_start(out=out, in_=in_, allow_slow_non_contiguous=True),
                           reads, writes, dma=True)
        return self.op(eng, lambda e: e.dma_start(out=out, in_=in_), reads, writes, dma=True)

    def finalize(self):
        for e in self.ENGS:
            cnt = 0
            for o in self.ops[e]:
                if o.dma:
                    i = self.n_dma[e]
                    self.n_dma[e] += 1
                    o.sem = self.dsems[e][i % N_DMA_SEM]
                    o.val = 16 * (i // N_DMA_SEM + 1)
                    o.idx = i
                elif o.need_inc:
                    cnt += 1
                    o.sem = self.sems[e]
                    o.val = cnt

    def emit(self, ename, eng):
        waited = {}
        hist = []
        for o in self.ops[ename]:
            if o.dma:
                if o.idx >= N_DMA_SEM:
                    prev = hist[o.idx - N_DMA_SEM]
                    if waited.get(id(prev.sem), 0) < prev.val:
                        eng.wait_ge(prev.sem, prev.val)
                        waited[id(prev.sem)] = prev.val
                hist.append(o)
            for d in o.waits:
                if waited.get(id(d.sem), 0) < d.val:
                    eng.wait_ge(d.sem, d.val)
                    waited[id(d.sem)] = d.val
            ins = o.fn(eng)
            if o.dma:
                ins.then_inc(o.sem, 16)
            elif o.need_inc:
                ins.then_inc(o.sem, 1)

    def run(self, final_ops):
        self.finalize()
        nc = self.nc
        with nc.Block() as block:
            @block.tensor
            def _(e):
                self.emit("pe", e)

            @block.scalar
            def _(e):
                self.emit("act", e)

            @block.vector
            def _(e):
                self.emit("dve", e)

            @block.gpsimd
            def _(e):
                self.emit("pool", e)

            @block.sync
            def _(e):
                self.emit("sp", e)
                done = {}
                for o in final_ops:
                    if done.get(id(o.sem), (None, 0))[1] < o.val:
                        done[id(o.sem)] = (o.sem, o.val)
                for sem, val in done.values():
                    e.wait_ge(sem, val)


def build_program(stop_after=None):
    nc = bass.Bass("TRN2", target_bir_lowering=False)

    def din(name, shape):
        return nc.dram_tensor(name, list(shape), F32, kind="ExternalInput").ap()

    def dout(name, shape):
        return nc.dram_tensor(name, list(shape), F32, kind="ExternalOutput").ap()

    L = 2
    I = dict(
        x_prompt=din("x_prompt", (SEQ, D)), x_sample=din("x_sample", (2, TS, D)),
        cache_fox_k=din("cache_fox_k", (L, 2, PAST, 512)), cache_fox_v=din("cache_fox_v", (L, 2, PAST, 512)),
        cache_fox_logf=din("cache_fox_logf", (L, 2, PAST, 8)), cache_mla_ckv=din("cache_mla_ckv", (L, 2, PAST, 256)),
        cache_mla_kr=din("cache_mla_kr", (L, 2, PAST, 32)), state_conv=din("state_conv", (L, 2, 2, 512)),
        state_ffn_conv=din("state_ffn_conv", (L, 2, 2, 2 * DFF)), cache_mem_k=din("cache_mem_k", (L, 2, 256, 512)),
        cache_mem_v=din("cache_mem_v", (L, 2, 256, 512)), mem_prompt=din("mem_prompt", (256, D)),
        w_in=din("w_in", (L, D, DIN)), b_forget=din("b_forget", (L, 8)), conv_w=din("conv_w", (L, 3, 512)),
        g_q_lora=din("g_q_lora", (L, 384)), g_kv_lora=din("g_kv_lora", (L, 256)),
        w_uq=din("w_uq", (L, 384, 768)), w_uk=din("w_uk", (L, 256, 512)), w_uv=din("w_uv", (L, 256, 512)),
        w_conv_out=din("w_conv_out", (L, 512, D)), w_fox_out=din("w_fox_out", (L, 512, D)),
        w_mla_out=din("w_mla_out", (L, 512, D)), w_mix_out=din("w_mix_out", (L, D, D)), g_mem=din("g_mem", (L, D)),
        w_ca_q=din("w_ca_q", (L, D, 512)), w_ca_k=din("w_ca_k", (L, D, 512)), w_ca_v=din("w_ca_v", (L, D, 512)),
        w_ca_o=din("w_ca_o", (L, 512, D)), w_up=din("w_up", (L, D, 2 * DFF)), ffn_conv_w=din("ffn_conv_w", (L, 3, 2 * DFF)),
        w_down=din("w_down", (L, DFF, D)), g_norms=din("g_norms", (L, 6, D)),
        c_ident=din("c_ident", (128, 128)), c_mtri=din("c_mtri", (128, 128)), c_mchk=din("c_mchk", (128, 128)),
        c_ropeT=din("c_ropeT", (2, 32, KTOT)), c_ropeK=din("c_ropeK", (2, KTOT, 16)),
    )
    O = dict(
        y_p=dout("y_p", (SEQ, D)), y_s=dout("y_s", (2, TS, D)),
        fox_k_p=dout("fox_k_p", (L, SEQ, 512)), fox_v_p=dout("fox_v_p", (L, SEQ, 512)), logf_p=dout("logf_p", (L, SEQ, 8)),
        ckv_p=dout("ckv_p", (L, SEQ, 256)), kr_p=dout("kr_p", (L, SEQ, 32)), conv_p=dout("conv_p", (L, 2, 512)),
        ffn_p=dout("ffn_p", (L, 2, 2 * DFF)), mem_k_p=dout("mem_k_p", (L, 256, 512)), mem_v_p=dout("mem_v_p", (L, 256, 512)),
        fox_k_s=dout("fox_k_s", (L, 2, TS, 512)), fox_v_s=dout("fox_v_s", (L, 2, TS, 512)), logf_s=dout("logf_s", (L, 2, TS, 8)),
        ckv_s=dout("ckv_s", (L, 2, TS, 256)), kr_s=dout("kr_s", (L, 2, TS, 32)), conv_s=dout("conv_s", (L, 2, 2, 512)),
        ffn_s=dout("ffn_s", (L, 2, 2, 2 * DFF)),
    )
    xres = nc.dram_tensor("xres", [128, 8, SEQ + 2 * TS], F32, kind="Internal").ap()

    with ExitStack() as st:
        P = Prog(nc, st)
        fin = []

        def sb(name, shape, dt=F32):
            return st.enter_context(nc.sbuf_tensor(name, list(shape), dt))

        fK = sb("fK", (128, 8, KTOT), BF16)
        mK = sb("mK", (128, 8, KTOT), BF16)
        fV = sb("fV", (128, 17, 512), BF16)
        mV = sb("mV", (128, 17, 512), BF16)
        memKT = sb("memKT", (128, 4, 256), BF16)
        memV = sb("memV", (128, 2, 512), BF16)
        xT = sb("xT", (128, 8, TP))
        hT = sb("hT", (128, 8, TP), BF16)
        tb = sb("tb", (128, 8, TP), BF16)
        mT = sb("mT", (128, 8, TP))
        fQ = sb("fQ", (128, 8, TP), BF16)
        oT = sb("oT", (128, 4, TP), BF16)
        aT = sb("aT", (128, 4, TP), BF16)
        actT = sb("actT", (128, 11, TP), BF16)
        uT = sb("uT", (128, 4, TP + 2))
        uF = [sb("uF%d" % i, (128, TP + 2)) for i in range(2)]
        fhalo = sb("fhalo", (128, 44, 2))
        rstd = sb("rstd", (128, TP))
        tmpA = [sb("tmpA%d" % i, (128, TP)) for i in range(3)]
        pT = [sb("pT%d" % i, (128, TP), BF16) for i in range(3)]
        pT2 = [sb("pT2_%d" % i, (128, 512), BF16) for i in range(3)]
        stg = [sb("stg%d" % i, (128, 512)) for i in range(2)]
        cqT = sb("cqT", (128, 3, TP))
        cqn = sb("cqn", (128, 3, TP), BF16)
        ckT = sb("ckT", (128, 2, TP))
        ckn = sb("ckn", (128, 2, TP), BF16)
        krT = sb("krT", (128, TP), BF16)
        ropeT = sb("ropeT", (128, 2, TP))
        ropeK = sb("ropeK", (128, 2, 2, 16))
        lf = sb("lf", (8, TP))
        cum = sb("cum", (8, TP))
        carry = sb("carry", (8, 1))
        c8 = sb("c8", (8, TP))
        chi = sb("chi", (8, TP), BF16)
        clo = sb("clo", (8, TP), BF16)
        ident = sb("ident", (128, 128))
        identb = sb("identb", (128, 128), BF16)
        mtri = sb("mtri", (128, 128), BF16)
        mchk = sb("mchk", (128, 128), BF16)
        ones = sb("ones", (128, 128), BF16)
        gT = sb("gT", (128, 2, 6, 8))
        gqT = sb("gqT", (128, 2, 3))
        gkvT = sb("gkvT", (128, 2, 2))
        cwT = sb("cwT", (128, 2, 3, 4))
        fcwT = sb("fcwT", (128, 2, 3, 44))
        nbf = sb("nbf", (8, 2))
        gkvB = sb("gkvB", (128, 2, 256))
        gmemT = sb("gmemT", (128, 2, 8))
        bfB = sb("bfB", (128, 2, 8))
        small = sb("small", (128, 16))
        wslot = [sb("wslot%d" % i, (128, 2048), BF16) for i in range(4)]
        memx = sb("memx", (128, 1, D))
        NPS = 8
        psum = [st.enter_context(nc.psum_tensor("ps%d" % i, [128, 512], F32)) for i in range(NPS)]

        ctr = dict(ps=0, psf=0, w=0, tA=0, tB=0, pT=0, pT2=0, stg=0, uF=0)

        def ps_next(full=False):
            i = ctr["ps"] % 4
            ctr["ps"] += 1
            return psum[i][:, :], ("ps", i)

        def ps_acc(h):
            j = h % 2
            return (psum[4 + j][:, :], ("pacc", 4 + j)), (psum[6 + j][:, :], ("pacc", 6 + j))

        def rr(name, n):
            i = ctr[name] % n
            ctr[name] += 1
            return i

        wscratch = {}
        COLLECT = [None]

        def wcached(bkey, kc, cw, p, build_fn):
            if COLLECT[0] is not None:
                COLLECT[0].append((bkey, kc, cw, p, build_fn))
                return wslot[0][0:p, 0:kc * cw].rearrange("p (k c) -> p k c", k=kc), ("w", 0)
            i = rr("w", 4)
            flat = wslot[i][0:p, 0:kc * cw]
            view = flat.rearrange("p (k c) -> p k c", k=kc)
            key = ("w", i)
            if bkey in wscratch:
                P.dma("sp", flat, wscratch[bkey], reads=[("wsc", bkey)], writes=[key])
            else:
                build_fn(view, key)
                wscratch[bkey] = nc.dram_tensor("wsc%d" % len(wscratch), [p, kc * cw], BF16, kind="Internal").ap()
                P.dma("sp", wscratch[bkey], flat, reads=[key], writes=[("wsc", bkey)])
            return view, key

        def wload(src, kc, cw, p=128):
            bkey = (src.tensor.name, src.offset, str(src.ap), p, kc, cw)
            return wcached(bkey, kc, cw, p, lambda view, key: P.dma("pool", view, src, writes=[key]))

        def wrows(w2d, c0, cw, p=128):
            return w2d[:, c0:c0 + cw].rearrange("(k p) c -> p k c", p=p)

        def mm(out, pairs, reads, writes, first=True, last=True):
            def fn(e):
                ins = None
                n = len(pairs)
                for j, (a, b) in enumerate(pairs):
                    ins = e.matmul(out, a, b, start=(first and j == 0), stop=(last and j == n - 1))
                return ins
            return P.op("pe", fn, reads, writes)

        def act(out, in_, func, reads, writes, **kw):
            return P.op("act", lambda e: e.activation(out, in_, func, **kw), reads, writes)

        def acopy(out, in_, reads, writes):
            return P.op("act", lambda e: e.copy(out, in_), reads, writes)

        def vcopy(out, in_, reads, writes):
            return P.op("dve", lambda e: e.tensor_copy(out, in_), reads, writes)

        def tt(out, a, b, op, reads, writes, eng="dve"):
            return P.op(eng, lambda e: e.tensor_tensor(out, a, b, op), reads, writes)

        def ts(out, a, s1, s2, op0, op1, reads, writes, eng="dve"):
            if s2 is None:
                return P.op(eng, lambda e: e.tensor_scalar(out, a, s1, None, op0), reads, writes)
            return P.op(eng, lambda e: e.tensor_scalar(out, a, s1, s2, op0, op1), reads, writes)

        def stt(out, a, s, b, op0, op1, reads, writes, eng="dve"):
            return P.op(eng, lambda e: e.scalar_tensor_tensor(out, a, s, b, op0, op1), reads, writes)

        P.dma("sp", ident[:], I["c_ident"], writes=["ident"])
        vcopy(identb[:], ident[:], ["ident"], ["identb"])
        P.dma("pool", mtri[:], I["c_mtri"], writes=["mtri"])
        P.dma("pool", mchk[:], I["c_mchk"], writes=["mchk"])
        P.op("dve", lambda e: e.memset(ones[:], 1.0), writes=["ones"])
        P.op("dve", lambda e: e.memset(fK[64:128, :, :], 0.0), writes=["fKa"])
        P.op("dve", lambda e: e.memset(fK[64:68, :, :], 1.0), writes=["fKa"])
        P.op("dve", lambda e: e.memset(mK[64:128, :, :], 0.0), writes=["mKr"])
        P.op("dve", lambda e: e.memset(fQ[64:128, :, :], 0.0), writes=["fQa"])
        P.op("dve", lambda e: e.memset(fQ[64:68, :, :], 1.0), writes=["fQa"])
        for l in range(2):
            P.dma("sp", gT[:, l, :, :], I["g_norms"][l].rearrange("n (c p) -> p n c", p=128), writes=["gT"], slow=True)
            P.dma("sp", gqT[:, l, :], I["g_q_lora"][l:l + 1, :].rearrange("o (c p) -> p (o c)", p=128), writes=["gqT"], slow=True)
            P.dma("sp", gkvT[:, l, :], I["g_kv_lora"][l:l + 1, :].rearrange("o (c p) -> p (o c)", p=128), writes=["gkvT"], slow=True)
            P.dma("sp", cwT[:, l, :, :], I["conv_w"][l].rearrange("j (c p) -> p j c", p=128), writes=["cwT"], slow=True)
            P.dma("sp", fcwT[:, l, :, :], I["ffn_conv_w"][l].rearrange("j (c p) -> p j c", p=128), writes=["fcwT"], slow=True)
            P.dma("sp", nbf[:, l:l + 1], I["b_forget"][l:l + 1, :].rearrange("o h -> h o"), writes=["nbf"], slow=True)
            P.dma("sp", gkvB[:, l, :], I["g_kv_lora"][l:l + 1, :].broadcast_to([128, 256]), writes=["gkvB"])
            P.dma("sp", bfB[:, l, :], I["b_forget"][l:l + 1, :].broadcast_to([128, 8]), writes=["bfB"])
        ts(nbf[:], nbf[:], -1.0, None, ALU.mult, None, ["nbf"], ["nbf"])

        def norm_stats(src, nch, T, n, skeys):
            sq = tb[:, 0:nch, 0:T]
            act(sq, src, AF.Square, list(skeys), ["tb"])
            pt, pk = ps_next()
            mm(pt[:, 0:T], [(ones[:], tb[:, c, 0:T]) for c in range(nch)], ["ones", "tb"], [pk])
            act(rstd[:, 0:T], pt[:, 0:T], AF.Ln, [pk], ["rstd"], scale=1.0 / n, bias=small[:, 0:1])
            act(rstd[:, 0:T], rstd[:, 0:T], AF.Exp, ["rstd"], ["rstd"], scale=-0.5)

        P.op("dve", lambda e: e.memset(small[:, 0:1], EPS), writes=["small"])
        P.op("dve", lambda e: e.memset(small[:, 1:2], 1.0), writes=["small"])

        def prenorm(l, n_idx, T):
            norm_stats(xT[:, :, 0:T], 8, T, D, XT)
            for c in range(8):
                if c in POOLC:
                    ts(mT[:, c, 0:T], xT[:, c, 0:T], gT[:, l, n_idx, c:c + 1], None, ALU.mult, None,
                       ["xT", ("xTc", c), "gT", "mT", ("mTc", c)], [("mTc", c)], eng="pool")
                    tt(hT[:, c, 0:T], mT[:, c, 0:T], rstd[:, 0:T], ALU.mult, [("mTc", c), "mT", "rstd"], [("hT", c)], eng="pool")
                else:
                    stt(hT[:, c, 0:T], xT[:, c, 0:T], gT[:, l, n_idx, c:c + 1], rstd[:, 0:T], ALU.mult, ALU.mult,
                        ["xT", ("xTc", c), "rstd", "gT"], [("hT", c)])
        POOLC = ()
        FFN_ENG = "dve"
        HT = [("hT", c) for c in range(8)]
        XT = ["xT"] + [("xTc", c) for c in range(8)]

        def postnorm_residual(l, n_idx, T):
            norm_stats(mT[:, :, 0:T], 8, T, D, ["mT"])
            for c in range(8):
                en = "pool" if c in POOLC else "dve"
                tt(mT[:, c, 0:T], mT[:, c, 0:T], rstd[:, 0:T], ALU.mult, [("mTc", c), "mT", "rstd"], [("mTc", c)], eng=en)
                if en == "pool":
                    ts(mT[:, c, 0:T], mT[:, c, 0:T], gT[:, l, n_idx, c:c + 1], None, ALU.mult, None, [("mTc", c), "mT", "gT"], [("mTc", c)], eng=en)
                    tt(xT[:, c, 0:T], xT[:, c, 0:T], mT[:, c, 0:T], ALU.add, [("mTc", c), "mT", ("xTc", c), "xT"], [("xTc", c)], eng=en)
                else:
                    stt(xT[:, c, 0:T], mT[:, c, 0:T], gT[:, l, n_idx, c:c + 1], xT[:, c, 0:T], ALU.mult, ALU.add,
                        [("mTc", c), "mT", "gT", ("xTc", c), "xT"], [("xTc", c)], eng=en)

        def proj_fm(wsrc_fn, ncol_chunks, kc, rhs_fn, rkeys, T, consume, M=128, cw=256, p=128):
            per = cw // M
            j = 0
            while j < ncol_chunks:
                nb = min(per, ncol_chunks - j)
                wv, wk = wload(wsrc_fn(j * M, nb * M), kc, nb * M, p=p)
                for jj in range(nb):
                    pt, pk = ps_next()
                    mm(pt[0:M, 0:T], [(wv[0:p, k, jj * M:(jj + 1) * M], rhs_fn(k)) for k in range(kc)],
                       [wk] + rkeys, [pk])
                    consume(j + jj, pt, pk)
                j += nb

        def attention(Kst, Qt, Vst, vkey, krows, scale, mask, T, kblocks, hkeys, out_key):
            nblk = len(kblocks)
            LA = 3

            def s_block(h, bi):
                k0, ks, q0, diag = kblocks[bi]
                N = T - q0
                pt, pk = ps_next()
                mm(pt[0:ks, 0:N], [(Kst[0:128, h, k0:k0 + ks], Qt[0:128, h, q0:T])], hkeys, [pk], last=not diag)
                if diag:
                    dn = min(128, N)
                    mm(pt[0:ks, 0:dn], [(identb[0:ks, 0:ks], mask[0:ks, 0:dn])], ["identb", "mtri", "mchk"], [pk], first=False)
                return pt, pk
            units = [(h, bi) for h in range(8) for bi in range(nblk)]

            def finish_unit(h, bi, src_fn):
                k0, ks, q0, diag = kblocks[bi]
                N = T - q0
                (po, pok), (pz, pzk) = ps_acc(h)
                src, skey = src_fn(ks, N)
                kt, kp = k0 // 128, k0 % 128
                mm(po[0:128, q0:T], [(Vst[kp:kp + ks, kt, (h // 2) * 128:(h // 2 + 1) * 128], src)],
                   [skey, vkey], [pok], first=(bi == 0), last=(bi == nblk - 1))
                mm(pz[0:128, q0:T], [(ones[0:ks, 0:128], src)], [skey, "ones"], [pzk], first=(bi == 0), last=(bi == nblk - 1))
                if bi == nblk - 1:
                    i = rr("tA", 3)
                    r0 = (h % 2) * 64
                    P.op("dve", lambda e, i=i, pz=pz, r0=r0: e.reciprocal(tmpA[i][r0:r0 + 64, 0:T], pz[r0:r0 + 64, 0:T]), [pzk], [("tA", i)])
                    tt(oT[r0:r0 + 64, h // 2, 0:T], po[r0:r0 + 64, 0:T], tmpA[i][r0:r0 + 64, 0:T], ALU.mult, [pok, ("tA", i)], [out_key])

            if T == TP:
                def s_pair(p):
                    pt, pk = ps_next()
                    for j in (0, 1):
                        h, bi = units[2 * p + j]
                        k0, ks, q0, diag = kblocks[bi]
                        N = T - q0
                        reg = pt[:, j * 256:(j + 1) * 256]
                        mm(reg[0:ks, 0:N], [(Kst[0:128, h, k0:k0 + ks], Qt[0:128, h, q0:T])], hkeys, [pk], last=not diag)
                        if diag:
                            dn = min(128, N)
                            mm(reg[0:ks, 0:dn], [(identb[0:ks, 0:ks], mask[0:ks, 0:dn])], ["identb", "mtri", "mchk"], [pk], first=False)
                    return pt, pk
                npairs = len(units) // 2
                ppend = [s_pair(p) for p in range(min(2, npairs))]
                for p in range(npairs):
                    pt, pk = ppend.pop(0)
                    if p + 2 < npairs:
                        ppend.append(s_pair(p + 2))
                    i = rr("pT2", 3)
                    full = all(kblocks[units[2 * p + j][1]][1] == 128 and kblocks[units[2 * p + j][1]][2] == 0 for j in (0, 1))
                    if full:
                        act(pT2[i][:, :], pt[:, 0:512], AF.Exp, [pk], [("pT2", i)], scale=scale)
                    else:
                        for j in (0, 1):
                            _k0, ks_, q0_, _d = kblocks[units[2 * p + j][1]]
                            n_ = T - q0_
                            act(pT2[i][0:ks_, j * 256:j * 256 + n_], pt[0:ks_, j * 256:j * 256 + n_], AF.Exp, [pk], [("pT2", i)], scale=scale)
                    for j in (0, 1):
                        h, bi = units[2 * p + j]
                        finish_unit(h, bi, lambda ks, N, i=i, j=j: (pT2[i][0:ks, j * 256:j * 256 + N], ("pT2", i)))
                return
            pend = [s_block(*units[u]) for u in range(min(LA, len(units)))]
            for u, (h, bi) in enumerate(units):
                k0, ks, q0, diag = kblocks[bi]
                N = T - q0
                (po, pok), (pz, pzk) = ps_acc(h)
                pt, pk = pend.pop(0)
                if u + LA < len(units):
                    pend.append(s_block(*units[u + LA]))
                i = rr("pT", 3)
                act(pT[i][0:ks, 0:N], pt[0:ks, 0:N], AF.Exp, [pk], [("pT", i)], scale=scale)
                kt, kp = k0 // 128, k0 % 128
                mm(po[0:128, q0:T], [(Vst[kp:kp + ks, kt, (h // 2) * 128:(h // 2 + 1) * 128], pT[i][0:ks, 0:N])],
                   [("pT", i), vkey], [pok], first=(bi == 0), last=(bi == nblk - 1))
                mm(pz[0:128, q0:T], [(ones[0:ks, 0:128], pT[i][0:ks, 0:N])],
                   [("pT", i), "ones"], [pzk], first=(bi == 0), last=(bi == nblk - 1))
                if bi == nblk - 1:
                    i = rr("tA", 3)
                    r0 = (h % 2) * 64
                    P.op("dve", lambda e, i=i, pz=pz, r0=r0: e.reciprocal(tmpA[i][r0:r0 + 64, 0:T], pz[r0:r0 + 64, 0:T]), [pzk], [("tA", i)])
                    tt(oT[r0:r0 + 64, h // 2, 0:T], po[r0:r0 + 64, 0:T], tmpA[i][r0:r0 + 64, 0:T], ALU.mult, [pok, ("tA", i)], [out_key])

        def kblocks_for(kpos0, T):
            blks = []
            for kt in range(kpos0 // 128):
                blks.append((kt * 128, 128, 0, False))
            if T >= 128:
                for j in range(T // 128):
                    blks.append((kpos0 + j * 128, 128, j * 128, True))
            else:
                blks.append((kpos0, T, 0, True))
            return blks

        def out_dma(dst, src, reads):
            fin.append(P.dma("act", dst, src, reads=reads))

        def run_tile(l, T, pos0, xr0, first, last, grp, b, tile_i):
            subs = [(s * 128, min(128, T - s * 128)) for s in range((T + 127) // 128)]
            W = I["w_in"][l]
            if l == 0:
                src = I["x_prompt"][pos0:pos0 + T, :] if grp == "p" else I["x_sample"][b]
                for s, (r0, nr) in enumerate(subs):
                    P.dma("sp", memx[0:nr, 0, :], src[r0:r0 + nr, :], writes=[("memx", 0), ("memx", 1)])
                    for c in range(8):
                        pt, pk = ps_next()
                        P.op("pe", lambda e, pt=pt, c=c, nr=nr: e.transpose(pt[:, 0:nr], memx[0:nr, 0, c * 128:(c + 1) * 128], ident[0:nr, 0:nr]),
                             [("memx", 0), ("memx", 1)] + ["ident"], [pk])
                        acopy(xT[:, c, r0:r0 + nr], pt[:, 0:nr], [pk], ["xT"])
            else:
                P.dma("sp", xT[:, :, 0:T], xres[:, :, xr0:xr0 + T], reads=["xres"], writes=["xT"])
            P.dma("sp", ropeT[64:96, :, 0:T], I["c_ropeT"][:, :, pos0:pos0 + T].rearrange("a r t -> r a t"), writes=["ropeT"])
            for s, (r0, nr) in enumerate(subs):
                P.dma("sp", ropeK[0:nr, s, :, :], I["c_ropeK"][:, pos0 + r0:pos0 + r0 + nr, :].rearrange("a t r -> t a r"), writes=["ropeK"])

            prenorm(l, 0, T)
            rhs_h = lambda k: hT[:, k, 0:T]

            if first:
                if grp == "p":
                    P.op("dve", lambda e: e.memset(uT[:, :, 0:2], 0.0), writes=["uT"])
                else:
                    for jj in range(2):
                        P.dma("sp", uT[:, :, jj], I["state_conv"][l, b, jj:jj + 1, :].rearrange("o (c p) -> p (o c)", p=128), writes=["uT"], slow=True)
            ccs = {}

            def cons_cc(j, pt, pk):
                i = rr("tA", 3)
                acopy(tmpA[i][:, 0:T], pt[:, 0:T], [pk], [("tA", i)])
                ccs[j] = i
            for ch in range(4):
                proj_fm(lambda c0, cw, ch=ch: wrows(W, O_CC + ch * 128, 128), 1, 8, rhs_h, HT, T, lambda j, pt, pk, ch=ch: cons_cc(ch, pt, pk), cw=128)

                def cons_cx(j, pt, pk, ch=ch):
                    i = ccs[ch]
                    tt(uT[:, ch, 2:2 + T], pt[:, 0:T], tmpA[i][:, 0:T], ALU.mult, [pk, ("tA", i)], ["uT"])
                proj_fm(lambda c0, cw, ch=ch: wrows(W, O_CX + ch * 128, 128), 1, 8, rhs_h, HT, T, cons_cx, cw=128)
            if last:
                dst = O["conv_p"][l] if grp == "p" else O["conv_s"][l, b]
                for jj in range(2):
                    fin.append(P.dma("act", dst[jj:jj + 1, :].rearrange("o (c p) -> p (o c)", p=128), uT[:, :, T + jj], reads=["uT"], slow=True))
            convs = {}
            for ch in range(4):
                i = rr("tA", 3)
                ts(tmpA[i][:, 0:T], uT[:, ch, 0:T], cwT[:, l, 0, ch:ch + 1], None, ALU.mult, None, ["uT", "cwT"], [("tA", i)])
                stt(tmpA[i][:, 0:T], uT[:, ch, 1:1 + T], cwT[:, l, 1, ch:ch + 1], tmpA[i][:, 0:T], ALU.mult, ALU.add, ["uT", "cwT", ("tA", i)], [("tA", i)])
                stt(tmpA[i][:, 0:T], uT[:, ch, 2:2 + T], cwT[:, l, 2, ch:ch + 1], tmpA[i][:, 0:T], ALU.mult, ALU.add, ["uT", "cwT", ("tA", i)], [("tA", i)])

                def cons_cb(j, pt, pk, ch=ch, i=i):
                    tt(aT[:, ch, 0:T], pt[:, 0:T], tmpA[i][:, 0:T], ALU.mult, [pk, ("tA", i)], [("aT", ch)])
                proj_fm(lambda c0, cw, ch=ch: wrows(W, O_CB + ch * 128, 128), 1, 8, rhs_h, HT, T, cons_cb, cw=128)
            if not last:
                acopy(uT[:, :, 0:2], uT[:, :, T:T + 2], ["uT"], ["uT"])

            P.op("dve", lambda e: e.memset(fQ[64:128, :, :], 0.0), writes=["fQa"])
            P.op("dve", lambda e: e.memset(fQ[64:68, :, :], 1.0), writes=["fQa"])

            def cons_q(j, pt, pk):
                acopy(fQ[0:64, 2 * j, 0:T], pt[0:64, 0:T], [pk], ["fQ"])
                acopy(fQ[0:64, 2 * j + 1, 0:T], pt[64:128, 0:T], [pk], ["fQ"])
            proj_fm(lambda c0, cw: wrows(W, O_FQ + c0, cw), 4, 8, rhs_h, HT, T, cons_q)

            def cons_k(j, pt, pk):
                acopy(fK[0:64, 2 * j, pos0:pos0 + T], pt[0:64, 0:T], [pk], ["fK"])
                acopy(fK[0:64, 2 * j + 1, pos0:pos0 + T], pt[64:128, 0:T], [pk], ["fK"])
            proj_fm(lambda c0, cw: wrows(W, O_FK + c0, cw), 4, 8, rhs_h, HT, T, cons_k)
            wv, wk = wload(wrows(W, O_FF, 8), 8, 8)
            pt, pk = ps_next()
            mm(pt[0:8, 0:T], [(wv[:, k, 0:8], hT[:, k, 0:T]) for k in range(8)], [wk] + HT, [pk])
            act(lf[:, 0:T], pt[0:8, 0:T], AF.Exp, [pk, "nbf"], ["lf"], scale=-1.0, bias=nbf[:, l:l + 1])
            act(lf[:, 0:T], lf[:, 0:T], AF.Ln, ["lf"], ["lf"], bias=small[0:8, 1:2])
            ts(lf[:, 0:T], lf[:, 0:T], -1.0, None, ALU.mult, None, ["lf"], ["lf"])
            if first and grp == "p":
                P.op("dve", lambda e: e.memset(carry[:], 0.0), writes=["carry"])
            P.op("dve", lambda e: e.tensor_tensor_scan(cum[:, 0:T], ones_f[0:8, 0:T], lf[:, 0:T], carry[:, 0:1], ALU.mult, ALU.add),
                 ["lf", "carry", "ones_f"], ["cum"])
            cum_to_rows(T, pos0, True)
            okey = "fox_k_" + grp
            vkey = "fox_v_" + grp
            for (col0, dname, isv) in ((O_FK, okey, False), (O_FV, vkey, True)):
                for hf in range(2):
                    wv_, wk_ = wload(wrows(W, col0 + hf * 256, 256), 8, 256)
                    for s_, (r0, nr) in enumerate(subs):
                        pt, pk = ps_next()
                        mm(pt[0:nr, 0:256], [(hT[:, k, r0:r0 + nr], wv_[:, k, :]) for k in range(8)], [wk_] + HT, [pk])
                        i = rr("stg", 2)
                        acopy(stg[i][0:nr, 0:256], pt[0:nr, 0:256], [pk], [("stg", i)])
                        if isv:
                            kt = (pos0 + r0) // 128
                            vcopy(fV[0:nr, kt, hf * 256:(hf + 1) * 256], pt[0:nr, 0:256], [pk], ["fV"])
                        dst = O[dname][l, pos0 + r0:pos0 + r0 + nr, hf * 256:(hf + 1) * 256] if grp == "p" else O[dname][l, b, r0:r0 + nr, hf * 256:(hf + 1) * 256]
                        out_dma(dst, stg[i][0:nr, 0:256], [("stg", i)])
            kb = kblocks_for(pos0, T)
            attention(fK, fQ, fV, "fV", 68, 0.125, mtri, T, kb, ["fK", "fKa", "fQ", "fQa"], "oTf")
            wv, wk = wload(wrows(W, O_FF, 8), 8, 8)
            for s, (r0, nr) in enumerate(subs):
                pt, pk = ps_next()
                mm(pt[0:nr, 0:8], [(hT[:, k, r0:r0 + nr], wv[:, k, 0:8]) for k in range(8)], [wk] + HT, [pk])
                i = rr("stg", 2)
                tt(stg[i][0:nr, 0:8], pt[0:nr, 0:8], bfB[0:nr, l, :], ALU.add, [pk, "bfB"], [("stg", i)])
                act(stg[i][0:nr, 0:8], stg[i][0:nr, 0:8], AF.Exp, [("stg", i)], [("stg", i)], scale=-1.0)
                act(stg[i][0:nr, 0:8], stg[i][0:nr, 0:8], AF.Ln, [("stg", i)], [("stg", i)], bias=small[0:nr, 1:2])
                ts(stg[i][0:nr, 0:8], stg[i][0:nr, 0:8], -1.0, None, ALU.mult, None, [("stg", i)], [("stg", i)])
                dst = O["logf_p"][l, pos0 + r0:pos0 + r0 + nr, :] if grp == "p" else O["logf_s"][l, b, r0:r0 + nr, :]
                out_dma(dst, stg[i][0:nr, 0:8], [("stg", i)])

            def branch_out(wname, gate_off, rhs_fn, rkeys, kc, p, firstb):
                wout = I[wname][l]
                for c in range(8):
                    gi = [None]

                    def cons_g(j, pt, pk):
                        gi[0] = rr("tA", 3)
                        act(tmpA[gi[0]][:, 0:T], pt[:, 0:T], AF.Sigmoid, [pk], [("tA", gi[0])])
                    proj_fm(lambda c0, cw, c=c: wrows(W, gate_off + c * 128, 128), 1, 8, rhs_h, HT, T, cons_g, cw=128)

                    def cons_y(j, pt, pk, c=c):
                        g = gi[0]
                        if firstb:
                            tt(mT[:, c, 0:T], pt[:, 0:T], tmpA[g][:, 0:T], ALU.mult, [pk, ("tA", g)], ["mT"])
                        else:
                            tt(tmpA[g][:, 0:T], pt[:, 0:T], tmpA[g][:, 0:T], ALU.mult, [pk, ("tA", g)], [("tA", g)])
                            tt(mT[:, c, 0:T], mT[:, c, 0:T], tmpA[g][:, 0:T], ALU.add, ["mT", ("tA", g)], ["mT"])
                    proj_fm(lambda c0, cw, c=c: wout[:, c * 128:(c + 1) * 128].rearrange("(k p) c -> p k c", p=p), 1, kc, rhs_fn, rkeys, T, cons_y, cw=128, p=p)

            branch_out("w_conv_out", O_GA, lambda k: aT[:, k, 0:T], [("aT", c) for c in range(4)], 4, 128, True)
            branch_out("w_fox_out", O_GB, lambda k: oT[:, k, 0:T], ["oTf"], 4, 128, False)

            def cons_cq(j, pt, pk):
                acopy(cqT[:, j, 0:T], pt[:, 0:T], [pk], ["cqT"])
            proj_fm(lambda c0, cw: wrows(W, O_CQ + c0, cw), 3, 8, rhs_h, HT, T, cons_cq)
            norm_stats(cqT[:, :, 0:T], 3, T, 384, ["cqT"])
            for c in range(3):
                stt(cqn[:, c, 0:T], cqT[:, c, 0:T], gqT[:, l, c:c + 1], rstd[:, 0:T], ALU.mult, ALU.mult, ["cqT", "rstd", "gqT"], ["cqn"])
            wq4 = I["w_uq"][l].rearrange("(k p) (h e) -> p k h e", p=128, e=96)
            for h0 in (0, 4):
                wa, wak = wload(wrows(I["w_uq"][l], h0 * 96, 384), 3, 384)
                def build_sw(view, key, h0=h0):
                    P.dma("pool", view, wrows(I["w_uq"][l], h0 * 96, 384), writes=[key])
                    v4 = view.rearrange("p k (h e) -> p k h e", e=96)
                    for k3 in range(3):
                        P.dma("pool", v4[:, k3, :, 64:80], wq4[:, k3, h0:h0 + 4, 80:96], writes=[key])
                        P.dma("pool", v4[:, k3, :, 80:96], wq4[:, k3, h0:h0 + 4, 64:80], writes=[key])
                wb_, wbk = wcached(("uq_sw", l, h0), 3, 384, 128, build_sw)
                for hh in range(4):
                    h = h0 + hh
                    pa, pak = ps_next()
                    mm(pa[0:96, 0:T], [(wa[:, k, hh * 96:(hh + 1) * 96], cqn[:, k, 0:T]) for k in range(3)], [wak, "cqn"], [pak])
                    pb, pbk = ps_next()
                    mm(pb[0:96, 0:T], [(wb_[:, k, hh * 96:(hh + 1) * 96], cqn[:, k, 0:T]) for k in range(3)], [wbk, "cqn"], [pbk])
                    acopy(fQ[0:64, h, 0:T], pa[0:64, 0:T], [pak], ["fQ"])
                    i1, i2 = rr("tA", 3), rr("tA", 3)
                    tt(tmpA[i1][64:96, 0:T], pa[64:96, 0:T], ropeT[64:96, 0, 0:T], ALU.mult, [pak, "ropeT"], [("tA", i1)])
                    tt(tmpA[i2][64:96, 0:T], pb[64:96, 0:T], ropeT[64:96, 1, 0:T], ALU.mult, [pbk, "ropeT"], [("tA", i2)])
                    tt(fQ[64:96, h, 0:T], tmpA[i1][64:96, 0:T], tmpA[i2][64:96, 0:T], ALU.add, [("tA", i1), ("tA", i2)], ["fQa"])
            def cons_ckv(j, pt, pk):
                acopy(ckT[:, j, 0:T], pt[:, 0:T], [pk], ["ckT"])
            proj_fm(lambda c0, cw: wrows(W, O_CKV + c0, cw), 2, 8, rhs_h, HT, T, cons_ckv)
            norm_stats(ckT[:, :, 0:T], 2, T, 256, ["ckT"])
            for c in range(2):
                stt(ckn[:, c, 0:T], ckT[:, c, 0:T], gkvT[:, l, c:c + 1], rstd[:, 0:T], ALU.mult, ALU.mult, ["ckT", "rstd", "gkvT"], ["ckn"])
            wkr, wkrk = wload(wrows(W, O_KR - 64, 96), 8, 96)
            def build_ks(view, key):
                P.dma("pool", view, wrows(W, O_KR - 64, 96), writes=[key])
                P.dma("pool", view[:, :, 64:80], wrows(W, O_KR + 16, 16), writes=[key])
                P.dma("pool", view[:, :, 80:96], wrows(W, O_KR, 16), writes=[key])
            wks, wksk = wcached(("kr_sw", l), 8, 96, 128, build_ks)
            pa, pak = ps_next()
            mm(pa[0:96, 0:T], [(wkr[:, k, :], hT[:, k, 0:T]) for k in range(8)], [wkrk] + HT, [pak])
            pb, pbk = ps_next()
            mm(pb[0:96, 0:T], [(wks[:, k, :], hT[:, k, 0:T]) for k in range(8)], [wksk] + HT, [pbk])
            i1, i2 = rr("tA", 3), rr("tA", 3)
            tt(tmpA[i1][64:96, 0:T], pa[64:96, 0:T], ropeT[64:96, 0, 0:T], ALU.mult, [pak, "ropeT"], [("tA", i1)])
            tt(tmpA[i2][64:96, 0:T], pb[64:96, 0:T], ropeT[64:96, 1, 0:T], ALU.mult, [pbk, "ropeT"], [("tA", i2)])
            tt(krT[64:96, 0:T], tmpA[i1][64:96, 0:T], tmpA[i2][64:96, 0:T], ALU.add, [("tA", i1), ("tA", i2)], ["krT"])
            mla_kv(l, pos0, T)
            attention(mK, fQ, mV, "mV", 96, 96.0 ** -0.5, mchk, T, kb, ["mK", "mKr", "fQ", "fQa"], "oTf")
            wck, wckk = wload(wrows(W, O_CKV, 256), 8, 256)
            wkr, wkrk = wload(wrows(W, O_KR - 64, 96), 8, 96)
            for s, (r0, nr) in enumerate(subs):
                pt, pk = ps_next()
                mm(pt[0:nr, 0:256], [(hT[:, k, r0:r0 + nr], wck[:, k, :]) for k in range(8)], [wckk] + HT, [pk])
                i = rr("stg", 2)
                act(stg[i][0:nr, 256:512], pt[0:nr, 0:256], AF.Square, [pk], [("stg", i), "small2"], accum_out=small[0:nr, 2:3])
                act(small[0:nr, 3:4], small[0:nr, 2:3], AF.Sqrt, ["small2"], ["small3"], scale=1.0 / 256, bias=small[0:nr, 0:1])
                P.op("dve", lambda e, nr=nr: e.reciprocal(small[0:nr, 3:4], small[0:nr, 3:4]), ["small3"], ["small3"])
                stt(stg[i][0:nr, 0:256], pt[0:nr, 0:256], small[0:nr, 3:4], gkvB[0:nr, l, :], ALU.mult, ALU.mult, [pk, "small3", "gkvB", ("stg", i)], [("stg", i)])
                dst = O["ckv_p"][l, pos0 + r0:pos0 + r0 + nr, :] if grp == "p" else O["ckv_s"][l, b, r0:r0 + nr, :]
                out_dma(dst, stg[i][0:nr, 0:256], [("stg", i)])
                pt, pk = ps_next()
                mm(pt[0:nr, 0:32], [(hT[:, k, r0:r0 + nr], wkr[:, k, 64:96]) for k in range(8)], [wkrk] + HT, [pk])
                i = rr("stg", 2)
                cs, sn = ropeK[0:nr, s, 0, :], ropeK[0:nr, s, 1, :]
                tt(stg[i][0:nr, 0:16], pt[0:nr, 0:16], cs, ALU.mult, [pk, "ropeK"], [("stg", i)])
                tt(stg[i][0:nr, 32:48], pt[0:nr, 16:32], sn, ALU.mult, [pk, "ropeK"], [("stg", i)])
                tt(stg[i][0:nr, 0:16], stg[i][0:nr, 0:16], stg[i][0:nr, 32:48], ALU.subtract, [("stg", i)], [("stg", i)])
                tt(stg[i][0:nr, 16:32], pt[0:nr, 16:32], cs, ALU.mult, [pk, "ropeK"], [("stg", i)])
                tt(stg[i][0:nr, 32:48], pt[0:nr, 0:16], sn, ALU.mult, [pk, "ropeK"], [("stg", i)])
                tt(stg[i][0:nr, 16:32], stg[i][0:nr, 16:32], stg[i][0:nr, 32:48], ALU.add, [("stg", i)], [("stg", i)])
                dst = O["kr_p"][l, pos0 + r0:pos0 + r0 + nr, :] if grp == "p" else O["kr_s"][l, b, r0:r0 + nr, :]
                out_dma(dst, stg[i][0:nr, 0:32], [("stg", i)])
            branch_out("w_mla_out", O_GC, lambda k: oT[:, k, 0:T], ["oTf"], 4, 128, False)

            vcopy(tb[:, :, 0:T], mT[:, :, 0:T], ["mT"], ["tb"])

            def cons_m(j, pt, pk):
                acopy(mT[:, j, 0:T], pt[:, 0:T], [pk], ["mT"])
            proj_fm(lambda c0, cw: wrows(I["w_mix_out"][l], c0, cw), 8, 8, lambda k: tb[:, k, 0:T], ["tb"], T, cons_m)
            postnorm_residual(l, 1, T)

            prenorm(l, 2, T)

            def cons_caq(j, pt, pk):
                acopy(fQ[:, j, 0:T], pt[:, 0:T], [pk], ["fQ", "fQa"])
            proj_fm(lambda c0, cw: wrows(I["w_ca_q"][l], c0, cw), 4, 8, rhs_h, HT, T, cons_caq)
            def ca_s(h, mt):
                pt, pk = ps_next()
                mm(pt[:, 0:T], [(memKT[:, h, mt * 128:(mt + 1) * 128], fQ[:, h, 0:T])], ["memKT", "fQ", "fQa"], [pk])
                return pt, pk
            cunits = [(h, mt) for h in range(4) for mt in range(2)]
            cpend = [ca_s(*cunits[u]) for u in range(3)]
            for u, (h, mt) in enumerate(cunits):
                (po, pok), (pz, pzk) = ps_acc(h)
                pt, pk = cpend.pop(0)
                if u + 3 < len(cunits):
                    cpend.append(ca_s(*cunits[u + 3]))
                i = rr("pT", 3)
                act(pT[i][:, 0:T], pt[:, 0:T], AF.Exp, [pk], [("pT", i)], scale=128.0 ** -0.5)
                mm(po[:, 0:T], [(memV[:, mt, h * 128:(h + 1) * 128], pT[i][:, 0:T])], [("pT", i), "memV"], [pok], first=(mt == 0), last=(mt == 1))
                mm(pz[:, 0:T], [(ones[:, :], pT[i][:, 0:T])], [("pT", i), "ones"], [pzk], first=(mt == 0), last=(mt == 1))
                if mt == 1:
                    i = rr("tA", 3)
                    P.op("dve", lambda e, i=i, pz=pz: e.reciprocal(tmpA[i][:, 0:T], pz[:, 0:T]), [pzk], [("tA", i)])
                    tt(aT[:, h, 0:T], po[:, 0:T], tmpA[i][:, 0:T], ALU.mult, [pok, ("tA", i)], [("aT", h)])
            proj_fm(lambda c0, cw: wrows(I["w_ca_o"][l], c0, cw), 8, 4, lambda k: aT[:, k, 0:T], [("aT", c) for c in range(4)], T, cons_m)
            postnorm_residual(l, 3, T)

            prenorm(l, 4, T)
            if first:
                if grp == "p":
                    P.op("dve", lambda e: e.memset(fhalo[:], 0.0), writes=["fhalo"])
                else:
                    for jj in range(2):
                        P.dma("sp", fhalo[:, :, jj], I["state_ffn_conv"][l, b, jj:jj + 1, :].rearrange("o (c p) -> p (o c)", p=128), writes=["fhalo"], slow=True)
            Wup = I["w_up"][l]
            for half in range(2):
                for j in range(11):
                    ja = half * 11 + j
                    res = []
                    for u_i, idx in enumerate((ja, 22 + ja)):
                        uf = uF[u_i]
                        ukey = ("uF", u_i)
                        acopy(uf[:, 0:2], fhalo[:, idx, :], ["fhalo", ("fhalo", u_i)], [ukey])

                        i = rr("tA", 3)

                        def cons_u(jj, pt, pk, uf=uf, ukey=ukey, i=i, idx=idx):
                            acopy(uf[:, 2:2 + T], pt[:, 0:T], [pk], [ukey])
                            act(tmpA[i][:, 0:T], pt[:, 0:T], AF.Copy, [pk, "fcwT"], [("tA", i)], scale=fcwT[:, l, 2, idx:idx + 1])
                        proj_fm(lambda c0, cw, idx=idx: wrows(Wup, idx * 128, 128), 1, 8, rhs_h, HT, T, cons_u, cw=128)
                        acopy(fhalo[:, idx, :], uf[:, T:T + 2], [ukey], [("fhalo", u_i)])
                        stt(tmpA[i][:, 0:T], uf[:, 1:1 + T], fcwT[:, l, 1, idx:idx + 1], tmpA[i][:, 0:T], ALU.mult, ALU.add, [ukey, "fcwT", ("tA", i)], [("tA", i)])
                        stt(tmpA[i][:, 0:T], uf[:, 0:T], fcwT[:, l, 0, idx:idx + 1], tmpA[i][:, 0:T], ALU.mult, ALU.add, [ukey, "fcwT", ("tA", i)], [("tA", i)])
                        res.append(i)
                    act(tmpA[res[0]][:, 0:T], tmpA[res[0]][:, 0:T], AF.Gelu_apprx_tanh, [("tA", res[0])], [("tA", res[0])])
                    tt(actT[:, j, 0:T], tmpA[res[0]][:, 0:T], tmpA[res[1]][:, 0:T], ALU.mult, [("tA", res[0]), ("tA", res[1])], ["actT"], eng=FFN_ENG)
                Wd = I["w_down"][l]

                def cons_d(jj, pt, pk, half=half):
                    if half == 0:
                        acopy(mT[:, jj, 0:T], pt[:, 0:T], [pk], ["mT"])
                    else:
                        tt(mT[:, jj, 0:T], mT[:, jj, 0:T], pt[:, 0:T], ALU.add, [pk, "mT"], ["mT"])
                proj_fm(lambda c0, cw, half=half: Wd[half * 1408:(half + 1) * 1408, c0:c0 + cw].rearrange("(k p) c -> p k c", p=128),
                        8, 11, lambda k: actT[:, k, 0:T], ["actT"], T, cons_d, cw=128)
            if last:
                dst = O["ffn_p"][l] if grp == "p" else O["ffn_s"][l, b]
                for jj in range(2):
                    fin.append(P.dma("act", dst[jj:jj + 1, :].rearrange("o (c p) -> p (o c)", p=128), fhalo[:, :, jj], reads=["fhalo", ("fhalo", 0), ("fhalo", 1)], slow=True))
            postnorm_residual(l, 5, T)

            if l == 0:
                P.dma("act", xres[:, :, xr0:xr0 + T], xT[:, :, 0:T], reads=XT, writes=["xres"])
            else:
                for s, (r0, nr) in enumerate(subs):
                    for hf in range(2):
                        pt, pk = ps_next(full=True)
                        for cc in range(4):
                            c = hf * 4 + cc
                            P.op("pe", lambda e, pt=pt, c=c, cc=cc, r0=r0, nr=nr: e.transpose(pt[0:nr, cc * 128:(cc + 1) * 128], xT[:, c, r0:r0 + nr], ident[:, :]),
                                 XT + ["ident"], [pk])
                        i = rr("stg", 2)
                        acopy(stg[i][0:nr, 0:512], pt[0:nr, 0:512], [pk], [("stg", i)])
                        dst = O["y_p"][pos0 + r0:pos0 + r0 + nr, hf * 512:(hf + 1) * 512] if grp == "p" else O["y_s"][b, r0:r0 + nr, hf * 512:(hf + 1) * 512]
                        out_dma(dst, stg[i][0:nr, 0:512], [("stg", i)])

        def mla_kv(l, pos0, T):
            for h in range(8):
                vcopy(mK[64:96, h, pos0:pos0 + T], krT[64:96, 0:T], ["krT"], ["mKr"])
            wuk, wukk = wload(wrows(I["w_uk"][l], 0, 512), 2, 512)
            for hp in range(4):
                pt, pk = ps_next()
                mm(pt[0:128, 0:T], [(wuk[:, k, hp * 128:(hp + 1) * 128], ckn[:, k, 0:T]) for k in range(2)], [wukk, "ckn"], [pk])
                acopy(mK[0:64, 2 * hp, pos0:pos0 + T], pt[0:64, 0:T], [pk], ["mK"])
                acopy(mK[0:64, 2 * hp + 1, pos0:pos0 + T], pt[64:128, 0:T], [pk], ["mK"])
            wuv, wuvk = wload(wrows(I["w_uv"][l], 0, 512), 2, 512)
            for s in range((T + 127) // 128):
                r0 = s * 128
                nr = min(128, T - r0)
                pt, pk = ps_next(full=True)
                mm(pt[0:nr, 0:512], [(ckn[:, k, r0:r0 + nr], wuv[:, k, :]) for k in range(2)], [wuvk, "ckn"], [pk])
                vcopy(mV[0:nr, (pos0 + r0) // 128, :], pt[0:nr, 0:512], [pk], ["mV"])

        def prep_mem_prompt(l):
            P.dma("sp", gmemT[:, l, :], I["g_mem"][l:l + 1, :].rearrange("o (c p) -> p (o c)", p=128), writes=["gmemT"], slow=True)
            for mt in range(2):
                P.dma("sp", memx[:, 0, :], I["mem_prompt"][mt * 128:(mt + 1) * 128, :], writes=[("memx", 0), ("memx", 1)])
                i = rr("stg", 2)
                act(stg[i][:, 0:512], memx[:, 0, 0:512], AF.Square, [("memx", 0), ("memx", 1)], [("stg", i), "small4"], accum_out=small[:, 4:5])
                act(stg[i][:, 0:512], memx[:, 0, 512:1024], AF.Square, [("memx", 0), ("memx", 1)], [("stg", i), "small5"], accum_out=small[:, 5:6])
                tt(small[:, 4:5], small[:, 4:5], small[:, 5:6], ALU.add, ["small4", "small5"], ["small4"])
                act(small[:, 6:7], small[:, 4:5], AF.Sqrt, ["small4"], ["small6"], scale=1.0 / D, bias=small[:, 0:1])
                P.op("dve", lambda e: e.reciprocal(small[:, 6:7], small[:, 6:7]), ["small6"], ["small6"])
                ts(memx[:, 0, :], memx[:, 0, :], small[:, 6:7], None, ALU.mult, None, [("memx", 0), ("memx", 1)] + ["small6"], [("memx", 0), ("memx", 1)])
                for c in range(8):
                    pt, pk = ps_next()
                    P.op("pe", lambda e, pt=pt, c=c: e.transpose(pt[:, 0:128], memx[:, 0, c * 128:(c + 1) * 128], ident[:, :]), [("memx", 0), ("memx", 1)] + ["ident"], [pk])
                    ts(hT[:, c, mt * 128:(mt + 1) * 128], pt[:, 0:128], gmemT[:, l, c:c + 1], None, ALU.mult, None, [pk, "gmemT"], [("hT", c)])

            def cons_mk(j, pt, pk):
                acopy(memKT[:, j, :], pt[:, 0:256], [pk], ["memKT"])
            proj_fm(lambda c0, cw: wrows(I["w_ca_k"][l], c0, cw), 4, 8, lambda k: hT[:, k, 0:256], HT, 256, cons_mk)
            for (wn, on, isv) in (("w_ca_k", "mem_k_p", False), ("w_ca_v", "mem_v_p", True)):
                for hf in range(2):
                    wv_, wk_ = wload(wrows(I[wn][l], hf * 256, 256), 8, 256)
                    for mt in range(2):
                        pt, pk = ps_next()
                        mm(pt[:, 0:256], [(hT[:, k, mt * 128:(mt + 1) * 128], wv_[:, k, :]) for k in range(8)], [wk_] + HT, [pk])
                        i = rr("stg", 2)
                        acopy(stg[i][:, 0:256], pt[:, 0:256], [pk], [("stg", i)])
                        if isv:
                            vcopy(memV[:, mt, hf * 256:(hf + 1) * 256], pt[:, 0:256], [pk], ["memV"])
                        out_dma(O[on][l, mt * 128:(mt + 1) * 128, hf * 256:(hf + 1) * 256], stg[i][:, 0:256], [("stg", i)])

        def prep_mem_sample(l, b):
            for mt in range(2):
                mo = mt * 512
                P.dma("sp", memx[:, 0, mo:mo + 512], I["cache_mem_k"][l, b, mt * 128:(mt + 1) * 128, :], writes=[("memx", mt)])
                pt, pk = ps_next(full=True)
                for h in range(4):
                    P.op("pe", lambda e, pt=pt, h=h, mo=mo: e.transpose(pt[:, h * 128:(h + 1) * 128], memx[:, 0, mo + h * 128:mo + (h + 1) * 128], ident[:, :]), [("memx", mt), "ident"], [pk])
                acopy(memKT[:, :, mt * 128:(mt + 1) * 128], pt[:, 0:512].rearrange("p (h m) -> p h m", h=4), [pk], ["memKT"])
            P.dma("pool", memV[:, :, :], I["cache_mem_v"][l, b].rearrange("(t p) c -> p t c", p=128), writes=["memV"])

        def prep_cache(l, b):
            P.dma("pool", fV[:, 0:16, :], I["cache_fox_v"][l, b].rearrange("(t p) c -> p t c", p=128), writes=["fV"])
            for kt in range(16):
                mj = kt % 2
                mo = mj * 512
                P.dma("sp", memx[:, 0, mo:mo + 512], I["cache_fox_k"][l, b, kt * 128:(kt + 1) * 128, :], writes=[("memx", mj)])
                for hg in range(2):
                    pt, pk = ps_next(full=True)
                    for hh in range(4):
                        h = hg * 4 + hh
                        P.op("pe", lambda e, pt=pt, h=h, hh=hh, mo=mo: e.transpose(pt[0:64, hh * 128:(hh + 1) * 128], memx[:, 0, mo + h * 64:mo + (h + 1) * 64], ident[:, :]), [("memx", mj), "ident"], [pk])
                    acopy(fK[0:64, hg * 4:(hg + 1) * 4, kt * 128:(kt + 1) * 128], pt[0:64, 0:512].rearrange("p (h m) -> p h m", h=4), [pk], ["fK"])
            P.op("dve", lambda e: e.memset(carry[:], 0.0), writes=["carry"])
            for pc in range(PAST // TP):
                P.dma("sp", lf[:, 0:TP], I["cache_fox_logf"][l, b, pc * TP:(pc + 1) * TP, :].rearrange("t h -> h t"), writes=["lf"], slow=True)
                P.op("dve", lambda e: e.tensor_tensor_scan(cum[:, 0:TP], ones_f[0:8, 0:TP], lf[:, 0:TP], carry[:, 0:1], ALU.mult, ALU.add),
                     ["lf", "carry", "ones_f"], ["cum"])
                cum_to_rows(TP, pc * TP, False)
            for pc in range(PAST // TP):
                for s in range(TP // 128):
                    kt = pc * (TP // 128) + s
                    mj = kt % 2
                    mo = mj * 512
                    P.dma("sp", memx[:, 0, mo:mo + 256], I["cache_mla_ckv"][l, b, kt * 128:(kt + 1) * 128, :], writes=[("memx", mj)])
                    P.dma("sp", memx[:, 0, mo + 320:mo + 352], I["cache_mla_kr"][l, b, kt * 128:(kt + 1) * 128, :], writes=[("memx", mj)])
                    pt, pk = ps_next()
                    for c in range(2):
                        P.op("pe", lambda e, pt=pt, c=c, mo=mo: e.transpose(pt[:, c * 128:(c + 1) * 128], memx[:, 0, mo + c * 128:mo + (c + 1) * 128], ident[:, :]), [("memx", mj), "ident"], [pk])
                    acopy(ckn[:, :, s * 128:(s + 1) * 128], pt[:, 0:256].rearrange("p (c m) -> p c m", c=2), [pk], ["ckn"])
                    pt, pk = ps_next()
                    P.op("pe", lambda e, pt=pt, mo=mo: e.transpose(pt[0:96, 0:128], memx[:, 0, mo + 256:mo + 352], ident[:, :]), [("memx", mj), "ident"], [pk])
                    acopy(krT[64:96, s * 128:(s + 1) * 128], pt[64:96, 0:128], [pk], ["krT"])
                mla_kv(l, pc * TP, TP)

        def cum_to_rows(T, pos0, is_q):
            ts(c8[:, 0:T], cum[:, 0:T], 8.0, None, ALU.mult, None, ["cum"], ["c8"])
            vcopy(chi[:, 0:T], c8[:, 0:T], ["c8"], ["chi"])
            tt(clo[:, 0:T], c8[:, 0:T], chi[:, 0:T], ALU.subtract, ["c8", "chi"], ["clo"])
            if is_q:
                P.dma("act", fQ[64:65, :, 0:T], chi[:, 0:T], reads=["chi"], writes=["fQa"])
                P.dma("act", fQ[65:66, :, 0:T], clo[:, 0:T], reads=["clo"], writes=["fQa"])
            vcopy(carry[:, 0:1], cum[:, T - 1:T], ["cum"], ["carry"])
            ts(chi[:, 0:T], chi[:, 0:T], -1.0, None, ALU.mult, None, ["chi", "fQa"], ["chi"])
            ts(clo[:, 0:T], clo[:, 0:T], -1.0, None, ALU.mult, None, ["clo", "fQa"], ["clo"])
            P.dma("act", fK[66:67, :, pos0:pos0 + T], chi[:, 0:T], reads=["chi"], writes=["fKa"])
            P.dma("act", fK[67:68, :, pos0:pos0 + T], clo[:, 0:T], reads=["clo"], writes=["fKa"])

        ones_f = sb("ones_f", (8, TP))
        P.op("dve", lambda e: e.memset(ones_f[:], 1.0), writes=["ones_f"])

        NTP = SEQ // TP

        class _Dummy:
            def op(self, *a, **k):
                return None

            def dma(self, *a, **k):
                return None

        def convert_ahead(l):
            nonlocal P
            realP, saved_ctr, nfin = P, dict(ctr), len(fin)
            COLLECT[0] = []
            P = _Dummy()
            try:
                run_tile(l, TP, 0, 0, True, False, "p", 0, 0)
            finally:
                P = realP
                blocks, COLLECT[0] = COLLECT[0], None
                ctr.clear()
                ctr.update(saved_ctr)
                del fin[nfin:]
            for (bkey, kc, cw, p, build_fn) in blocks:
                if bkey in wscratch:
                    continue
                wscratch[bkey] = nc.dram_tensor("wsc%d" % len(wscratch), [p, kc * cw], BF16, kind="Internal").ap()
                build_fn(wscratch[bkey].rearrange("p (k c) -> p k c", k=kc), ("wsc", bkey))

        convert_ahead(0)
        for l in range(2):
            prep_mem_prompt(l)
            for i in range(NTP):
                run_tile(l, TP, i * TP, i * TP, i == 0, i == NTP - 1, "p", 0, i)
                if l == 0 and i == 0:
                    convert_ahead(1)
            for b in range(2):
                prep_mem_sample(l, b)
                prep_cache(l, b)
                run_tile(l, TS, PAST, SEQ + b * TS, True, True, "s", b, 0)
        P.run(fin)
    return nc


_CACHE = {}


def _consts():
    ident = np.eye(128, dtype=np.float32)
    k = np.arange(128)[:, None]
    q = np.arange(128)[None, :]
    mtri = np.where(k <= q, 0.0, NEG).astype(np.float32)
    mchk = np.where((k // 64) <= (q // 64), 0.0, NEG).astype(np.float32)
    pos = np.arange(KTOT, dtype=np.float32)
    inv = (10000.0 ** (-np.arange(16, dtype=np.float32) / 16)).astype(np.float32)
    ang = pos[:, None] * inv[None, :]
    cos, sin = np.cos(ang).astype(np.float32), np.sin(ang).astype(np.float32)
    ropeT = np.stack([np.concatenate([cos, cos], 1).T, np.concatenate([-sin, sin], 1).T], 0)
    ropeK = np.stack([cos, sin], 0)
    return dict(c_ident=ident, c_mtri=mtri, c_mchk=mchk, c_ropeT=np.ascontiguousarray(ropeT, dtype=np.float32),
                c_ropeK=np.ascontiguousarray(ropeK, dtype=np.float32))


def kernel(**inputs):
    inp = {k: np.asarray(v, dtype=np.float32) for k, v in inputs.items()}
    if "nc" not in _CACHE:
        _CACHE["nc"] = build_program()
    nc = _CACHE["nc"]
    consts = _consts()
    L = 2
    shared = {}
    for k in ("w_in", "b_forget", "conv_w", "g_q_lora", "g_kv_lora", "w_conv_out", "w_fox_out", "w_mla_out", "w_mix_out",
              "g_mem", "w_up", "ffn_conv_w", "w_down", "g_norms"):
        shared[k] = inp[k]
    for k in ("w_uq", "w_uk", "w_uv", "w_ca_q", "w_ca_k", "w_ca_v"):
        a = inp[k]
        shared[k] = a.reshape(a.shape[0], a.shape[1], -1)
    shared["w_ca_o"] = inp["w_ca_o"].reshape(L, 512, D)
    shared.update(consts)
    in_maps = []
    for c in range(NCORE):
        m = dict(shared)
        m["x_prompt"] = inp["x_prompt"][c]
        m["x_sample"] = inp["x_sample"][2 * c:2 * c + 2]
        m["mem_prompt"] = inp["mem_prompt"][c]
        for k in ("cache_fox_k", "cache_fox_v", "cache_mem_k", "cache_mem_v"):
            a = inp[k][:, 2 * c:2 * c + 2]
            m[k] = np.ascontiguousarray(a.reshape(a.shape[0], 2, a.shape[2], -1))
        for k in ("cache_fox_logf", "cache_mla_ckv", "cache_mla_kr", "state_conv", "state_ffn_conv"):
            m[k] = np.ascontiguousarray(inp[k][:, 2 * c:2 * c + 2])
        in_maps.append(m)
    res = run_bass_kernel_spmd(nc, in_maps, core_ids=list(range(NCORE)))
    R = res.results
    _CACHE["last"] = R

    def cat(name, axis, shape=None):
        a = np.stack([np.asarray(R[c][name]) for c in range(NCORE)], axis=axis)
        return a

    y_p = cat("y_p", 0)
    y_s = np.concatenate([R[c]["y_s"] for c in range(NCORE)], 0)
    outs = [y_p, y_s]
    outs.append(cat("fox_k_p", 1).reshape(L, 8, SEQ, 8, 64))
    outs.append(cat("fox_v_p", 1).reshape(L, 8, SEQ, 8, 64))
    outs.append(cat("logf_p", 1))
    outs.append(cat("ckv_p", 1))
    outs.append(cat("kr_p", 1))
    outs.append(cat("conv_p", 1))
    outs.append(cat("ffn_p", 1))
    outs.append(cat("mem_k_p", 1).reshape(L, 8, 256, 4, 128))
    outs.append(cat("mem_v_p", 1).reshape(L, 8, 256, 4, 128))

    def cats(name):
        return np.concatenate([R[c][name] for c in range(NCORE)], 1)
    outs.append(cats("fox_k_s").reshape(L, 16, TS, 8, 64))
    outs.append(cats("fox_v_s").reshape(L, 16, TS, 8, 64))
    outs.append(cats("logf_s"))
    outs.append(cats("ckv_s"))
    outs.append(cats("kr_s"))
    outs.append(cats("conv_s"))
    outs.append(cats("ffn_s"))
    return tuple(np.ascontiguousarray(o, dtype=np.float32) for o in outs)
```
